# Optimizing a Trainium2 kernel written in Bass

```python
import jax, jax.numpy as jnp
from jax import lax
import numpy as np

D_MODEL = 2048
BATCH = 4
SEQ = 2048
DEPTH = 1

CHUNK = 64
Q_BLOCK = 128
D_CONV = D_MODEL // 2
CONV_GROUPS = 8
CONV_K = 3
QK_NOPE = 128
QK_ROPE = 64
V_HEAD = 128
MLA_HEADS = (D_MODEL // 2) // V_HEAD
D_ATTN_OUT = MLA_HEADS * V_HEAD
Q_LORA = D_MODEL // 4
KV_LORA = D_MODEL // 4
ROPE_THETA = 10000.0
D_MIX = D_CONV + D_ATTN_OUT
IN_SPLIT_SIZES = (D_CONV, D_CONV, D_CONV, Q_LORA, KV_LORA, QK_ROPE)
IN_COLS = sum(IN_SPLIT_SIZES)
PEER_HEADS = 8
PEER_N_KEYS = 128
PEER_N_EXPERTS = PEER_N_KEYS * PEER_N_KEYS
PEER_QDIM = 256
PEER_TOPK = 16
PEER_TOKEN_BLOCK = 128
N_MOD = 6
EPS = 1e-6
NEG_INF = -1e30

kernel_name = 'hybrid_conv_mla_peer_adaln_block'


def rms_norm(x, g):
    x32 = x.astype(jnp.float32)
    y = x32 * lax.rsqrt(jnp.mean(x32 * x32, axis=-1, keepdims=True) + EPS)
    return (y * g.astype(jnp.float32)).astype(x.dtype)


def group_rms_norm(x, g, n_groups):
    shp = x.shape
    xg = x.reshape(shp[:-1] + (n_groups, shp[-1] // n_groups)).astype(jnp.float32)
    y = xg * lax.rsqrt(jnp.mean(xg * xg, axis=-1, keepdims=True) + EPS)
    return (y.reshape(shp) * g.astype(jnp.float32)).astype(x.dtype)


def rope_tables(seq, dim):
    inv = 1.0 / (ROPE_THETA ** (jnp.arange(0, dim, 2, dtype=jnp.float32) / dim))
    ang = jnp.arange(seq, dtype=jnp.float32)[:, None] * inv[None, :]
    return jnp.cos(ang), jnp.sin(ang)


def apply_rope(x, cos, sin):
    x32 = x.astype(jnp.float32)
    x1, x2 = jnp.split(x32, 2, axis=-1)
    out = jnp.concatenate([x1 * cos - x2 * sin, x2 * cos + x1 * sin], axis=-1)
    return out.astype(x.dtype)


def short_conv_mixer(b_gate, c_gate, h, conv_w):
    z = c_gate * h
    seq = z.shape[1]
    zp = jnp.pad(z, ((0, 0), (CONV_K - 1, 0), (0, 0)))
    y = conv_w[0] * zp[:, 0:seq]
    for k in range(1, CONV_K):
        y = y + conv_w[k] * zp[:, k:k + seq]
    return b_gate * y


def mla_attention(q_lat, kv_lat, k_rope_raw, g_q_lat, w_uq, g_kv_lat, w_ukv):
    bsz, seq, _ = q_lat.shape
    q = (rms_norm(q_lat, g_q_lat) @ w_uq).reshape(bsz, seq, MLA_HEADS, QK_NOPE + QK_ROPE)
    q_nope, q_rope = q[..., :QK_NOPE], q[..., QK_NOPE:]
    kv = (rms_norm(kv_lat, g_kv_lat) @ w_ukv).reshape(bsz, seq, MLA_HEADS, QK_NOPE + V_HEAD)
    k_nope, v = kv[..., :QK_NOPE], kv[..., QK_NOPE:]
    cos, sin = rope_tables(seq, QK_ROPE)
    q_rope = apply_rope(q_rope, cos[None, :, None, :], sin[None, :, None, :])
    k_rope = apply_rope(k_rope_raw, cos[None], sin[None])
    scale = (QK_NOPE + QK_ROPE) ** -0.5
    chunk_id = jnp.arange(seq) // CHUNK
    outs = []
    for blk in range(seq // Q_BLOCK):
        q0, q1 = blk * Q_BLOCK, (blk + 1) * Q_BLOCK
        s = (jnp.einsum('bqhd,bkhd->bhqk', q_nope[:, q0:q1], k_nope[:, :q1])
             + jnp.einsum('bqhr,bkr->bhqk', q_rope[:, q0:q1], k_rope[:, :q1])).astype(jnp.float32) * scale
        mask = chunk_id[None, :q1] <= chunk_id[q0:q1, None]
        s = jnp.where(mask, s, NEG_INF)
        p = jax.nn.softmax(s, axis=-1).astype(v.dtype)
        outs.append(jnp.einsum('bhqk,bkhd->bqhd', p, v[:, :q1]))
    o = jnp.concatenate(outs, axis=1)
    return o.reshape(bsz, seq, D_ATTN_OUT)


def peer_ffn(h, w_q, sub_keys, u_tab, v_tab):
    bsz, seq, d = h.shape
    q = (h @ w_q).reshape(bsz, seq, PEER_HEADS, 2, PEER_QDIM // 2)
    scores = jnp.einsum('bshpd,hpnd->bshpn', q, sub_keys).astype(jnp.float32)
    top_v, top_i = lax.top_k(scores, PEER_TOPK)
    cand = top_v[..., 0, :, None] + top_v[..., 1, None, :]
    cand = cand.reshape(bsz, seq, PEER_HEADS, PEER_TOPK * PEER_TOPK)
    best_v, best_i = lax.top_k(cand, PEER_TOPK)
    i1 = jnp.take_along_axis(top_i[..., 0, :], best_i // PEER_TOPK, axis=-1)
    i2 = jnp.take_along_axis(top_i[..., 1, :], best_i % PEER_TOPK, axis=-1)
    expert = i1 * PEER_N_KEYS + i2
    gate = jax.nn.softmax(best_v, axis=-1).astype(h.dtype)
    n_sel = PEER_HEADS * PEER_TOPK
    n_blk = (bsz * seq) // PEER_TOKEN_BLOCK
    hb = h.reshape(n_blk, PEER_TOKEN_BLOCK, d)
    eb = expert.reshape(n_blk, PEER_TOKEN_BLOCK, n_sel)
    gb = gate.reshape(n_blk, PEER_TOKEN_BLOCK, n_sel)

    def token_block(args):
        hx, ids, g = args
        u = u_tab[ids]
        a = jnp.einsum('td,tnd->tn', hx, u)
        act = jax.nn.gelu(a, approximate=False) * g
        return jnp.einsum('tn,tnd->td', act, v_tab[ids])

    out = lax.map(token_block, (hb, eb, gb))
    return out.reshape(bsz, seq, d)


def setup_inputs(seed: int = 0) -> dict:
    key = jax.random.key(seed)
    ks = jax.random.split(key, 20)
    f32 = jnp.float32
    nrm = lambda k, shape, s: jax.random.normal(k, shape, f32) * s
    gain = lambda k, shape: 1.0 + 0.02 * jax.random.normal(k, shape, f32)
    L = DEPTH
    return {
        'x': jax.random.normal(ks[0], (BATCH, SEQ, D_MODEL), f32),
        'c': jax.random.normal(ks[1], (BATCH, D_MODEL), f32),
        'w_ada': nrm(ks[2], (L, D_MODEL, N_MOD * D_MODEL), 0.5 * D_MODEL ** -0.5),
        'b_ada': nrm(ks[3], (L, N_MOD * D_MODEL), 0.02),
        'g_norm_mix': gain(ks[4], (L, D_MODEL)),
        'w_in': nrm(ks[5], (L, D_MODEL, IN_COLS), D_MODEL ** -0.5),
        'conv_w': nrm(ks[6], (L, CONV_K, D_CONV), CONV_K ** -0.5),
        'g_q_lat': gain(ks[7], (L, Q_LORA)),
        'w_uq': nrm(ks[8], (L, Q_LORA, MLA_HEADS * (QK_NOPE + QK_ROPE)), Q_LORA ** -0.5),
        'g_kv_lat': gain(ks[9], (L, KV_LORA)),
        'w_ukv': nrm(ks[10], (L, KV_LORA, MLA_HEADS * (QK_NOPE + V_HEAD)), KV_LORA ** -0.5),
        'g_out_conv': gain(ks[11], (L, D_CONV)),
        'g_out_attn': gain(ks[12], (L, D_ATTN_OUT)),
        'w_out': nrm(ks[13], (L, D_MIX, D_MODEL), D_MIX ** -0.5),
        'g_norm_ffn': gain(ks[14], (L, D_MODEL)),
        'peer_w_q': nrm(ks[15], (L, D_MODEL, PEER_HEADS * PEER_QDIM), D_MODEL ** -0.5),
        'peer_sub_keys': nrm(ks[16], (L, PEER_HEADS, 2, PEER_N_KEYS, PEER_QDIM // 2), (PEER_QDIM // 2) ** -0.5),
        'peer_u': nrm(ks[17], (L, PEER_N_EXPERTS, D_MODEL), D_MODEL ** -0.5),
        'peer_v': nrm(ks[18], (L, PEER_N_EXPERTS, D_MODEL), PEER_HEADS ** -0.5),
        'g_final': gain(ks[19], (D_MODEL,)),
    }


def reference(x, c, w_ada, b_ada, g_norm_mix, w_in, conv_w, g_q_lat, w_uq, g_kv_lat, w_ukv,
              g_out_conv, g_out_attn, w_out, g_norm_ffn, peer_w_q, peer_sub_keys, peer_u, peer_v, g_final):
    split_at = np.cumsum(IN_SPLIT_SIZES)[:-1].tolist()
    c_act = jax.nn.silu(c)
    for l in range(DEPTH):
        mod = c_act @ w_ada[l] + b_ada[l]
        sh_m, sc_m, gt_m, sh_f, sc_f, gt_f = [m[:, None, :] for m in jnp.split(mod, N_MOD, axis=-1)]
        h = rms_norm(x, g_norm_mix[l]) * (1 + sc_m) + sh_m
        proj = h @ w_in[l]
        b_g, c_g, h_c, q_lat, kv_lat, k_rope_raw = jnp.split(proj, split_at, axis=-1)
        conv_out = short_conv_mixer(b_g, c_g, h_c, conv_w[l])
        attn_out = mla_attention(q_lat, kv_lat, k_rope_raw, g_q_lat[l], w_uq[l],
                                 g_kv_lat[l], w_ukv[l])
        merged = jnp.concatenate([group_rms_norm(conv_out, g_out_conv[l], CONV_GROUPS),
                                  group_rms_norm(attn_out, g_out_attn[l], MLA_HEADS)], axis=-1)
        x = x + gt_m * (merged @ w_out[l])
        h2 = rms_norm(x, g_norm_ffn[l]) * (1 + sc_f) + sh_f
        x = x + gt_f * peer_ffn(h2, peer_w_q[l], peer_sub_keys[l], peer_u[l], peer_v[l])
    return rms_norm(x, g_final)
```

```python
import numpy as np
from contextlib import ExitStack
import concourse.bass as bass
import concourse.mybir as mybir
from concourse.bass_utils import run_bass_kernel_spmd

F32 = mybir.dt.float32
BF16 = mybir.dt.bfloat16
AF = mybir.ActivationFunctionType
ALU = mybir.AluOpType

D = 2048
SEQ = 2048
NT = 1024
EPS = 1e-6
NEXP = 16384
GE = 512
NPASS = 2
TP = 8 // NPASS
SCALE = 192 ** -0.5
POOL_AS = None


class Sched:
    ENG = ("pe", "act", "dve", "pool", "sp")

    def __init__(self, nc):
        self.nc = nc
        self.streams = {e: [] for e in self.ENG}
        self.sems = {}
        self._sem_ctx = []
        self.count = {}
        self.seen = {e: {} for e in self.ENG}
        self.dep = {}
        for e in ("pe", "act", "dve", "pool"):
            self._mksem("E_" + e)

    def _mksem(self, name):
        if name not in self.sems:
            ctx = self.nc.semaphore(name)
            self.sems[name] = ctx.__enter__()
            self._sem_ctx.append(ctx)
            self.count[name] = 0
        return name

    def close(self):
        for ctx in reversed(self._sem_ctx):
            ctx.__exit__(None, None, None)

    def _d(self, k):
        return self.dep.setdefault(k, {"w": {}, "r": {}})

    def _collect(self, reads, writes):
        need = {}
        for k in reads:
            for s, v in self._d(k)["w"].items():
                need[s] = max(need.get(s, 0), v)
        for k in writes:
            d = self._d(k)
            for dd in (d["w"], d["r"]):
                for s, v in dd.items():
                    need[s] = max(need.get(s, 0), v)
        return need

    def _emit_waits(self, eng, need):
        own = "E_" + eng
        for s, v in need.items():
            if s == own and eng == "pe":
                continue
            if self.seen[eng].get(s, 0) >= v:
                continue
            self.seen[eng][s] = v
            self.streams[eng].append(("wait", s, v))

    def _record(self, reads, writes, s, v):
        for k in writes:
            d = self._d(k)
            d["w"] = {s: v}
            d["r"] = {}
        for k in reads:
            if k in writes:
                continue
            d = self._d(k)
            d["r"][s] = max(d["r"].get(s, 0), v)

    def op(self, eng, fn, reads=(), writes=()):
        if eng == "pool" and POOL_AS is not None:
            eng = POOL_AS
        self._emit_waits(eng, self._collect(reads, writes))
        s = "E_" + eng
        self.count[s] += 1
        self.streams[eng].append(("op", fn, s, 1))
        self._record(reads, writes, s, self.count[s])

    def dma(self, out_ap, in_ap, reads=(), writes=(), semkey=None):
        self._emit_waits("sp", self._collect(reads, writes))
        s = self._mksem("D_" + str(semkey if semkey is not None else (list(writes) + list(reads))[0]))
        self.count[s] += 16
        self.streams["sp"].append(("op", lambda e, o=out_ap, i=in_ap: e.dma_start(out=o, in_=i), s, 16))
        self._record(reads, writes, s, self.count[s])

    def barrier(self):
        for eng in self.ENG:
            for s, v in self.count.items():
                if v > 0 and self.seen[eng].get(s, 0) < v and not (eng == "pe" and s == "E_pe"):
                    self.seen[eng][s] = v
                    self.streams[eng].append(("wait", s, v))

    def final_wait(self, eng="sp"):
        for s, v in self.count.items():
            if v > 0 and self.seen[eng].get(s, 0) < v:
                self.seen[eng][s] = v
                self.streams[eng].append(("wait", s, v))

    def emit(self):
        names = {"pe": "tensor", "act": "scalar", "dve": "vector", "pool": "gpsimd", "sp": "sync"}
        with self.nc.Block() as block:
            for e in self.ENG:
                stream = self.streams[e]

                def body(engine, stream=stream):
                    for it in stream:
                        if it[0] == "wait":
                            engine.wait_ge(self.sems[it[1]], it[2])
                        else:
                            it[1](engine).then_inc(self.sems[it[2]], it[3])
                getattr(block, names[e])(body)


class Arena:
    def __init__(self, nc, stack, nbytes=206000):
        self.words = nbytes // 4
        self.t = stack.enter_context(nc.sbuf_tensor("arena", [128, self.words], F32))
        self.free = [(0, self.words)]
        self.live = {}

    def alloc(self, name, shape, dt=F32, hi=False):
        n = 1
        for s in shape[1:]:
            n *= s
        w = (n * (4 if dt == F32 else 2) + 3) // 4
        w = (w + 15) // 16 * 16
        order = range(len(self.free) - 1, -1, -1) if hi else range(len(self.free))
        for i in order:
            o, sz = self.free[i]
            if sz >= w:
                if hi:
                    self.free[i] = (o, sz - w)
                    o = o + sz - w
                else:
                    self.free[i] = (o + w, sz - w)
                break
        else:
            raise RuntimeError("arena full allocating %s (%d words); free=%s live=%s" % (name, w, self.free, sorted(self.live)))
        self.live[name] = (o, w)
        ap = self.t[0:shape[0], o:o + w]
        if dt != F32:
            ap = ap.bitcast(dt)
        ap = ap[:, 0:n]
        if len(shape) == 3:
            ap = ap.rearrange("p (a b) -> p a b", a=shape[1])
        elif len(shape) == 4:
            ap = ap.rearrange("p (a b c) -> p a b c", a=shape[1], b=shape[2])
        elif len(shape) == 5:
            ap = ap.rearrange("p (a b c d) -> p a b c d", a=shape[1], b=shape[2], c=shape[3])
        return ap

    def release(self, *names):
        for name in names:
            o, w = self.live.pop(name)
            self.free.append((o, w))
        self.free.sort()
        merged = []
        for o, w in self.free:
            if w == 0:
                continue
            if merged and merged[-1][0] + merged[-1][1] == o:
                merged[-1] = (merged[-1][0], merged[-1][1] + w)
            else:
                merged.append((o, w))
        self.free = merged

def build(debug=(), stop=None):
    nc = bass.Bass("TRN2", target_bir_lowering=False)
    S = Sched(nc)
    top = ExitStack()
    dbg_outs = []

    def din(name, shape):
        return nc.dram_tensor(name, list(shape), F32, kind="ExternalInput").ap()

    x_own = din("x_own", [NT, D])
    x_prev = din("x_prev", [NT, D])
    c_vec = din("c_vec", [128, 16])
    consts = din("consts", [128, 2])
    rope = din("rope", [4, 64, NT])
    ident_d = din("ident", [128, 128])
    w_ada = din("w_ada", [D, 6 * D])
    b_ada = din("b_ada", [6 * D])
    g_norm_mix = din("g_norm_mix", [D])
    w_in = din("w_in", [D, 4160])
    conv_w = din("conv_w", [3, 1024])
    g_q_lat = din("g_q_lat", [512])
    w_uq = din("w_uq", [512, 1536])
    g_kv_lat = din("g_kv_lat", [512])
    w_ukv = din("w_ukv", [512, 2048])
    g_out_conv = din("g_out_conv", [1024])
    g_out_attn = din("g_out_attn", [1024])
    w_out = din("w_out", [D, D])
    g_norm_ffn = din("g_norm_ffn", [D])
    peer_w_q = din("peer_w_q", [D, D])
    peer_keys = din("peer_keys", [16 * 128, 128])
    peer_u = din("peer_u", [NEXP, D])
    peer_v = din("peer_v", [NEXP, D])
    g_final = din("g_final", [D])
    y_out = nc.dram_tensor("y_out", [NT, D], F32, kind="ExternalOutput").ap()
    x2_d = nc.dram_tensor("x2_scratch", [NT, D], F32, kind="Internal").ap()
    sc_d = nc.dram_tensor("sc_scratch", [NT, D], F32, kind="Internal").ap()

    A = Arena(nc, top)

    def fin():
        S.final_wait("sp")
        with nc.allow_non_contiguous_dma(reason="tiny per-partition parameter vectors"):
            S.emit()
        S.close()
        return nc, dbg_outs

    def dbg(name, ap, reads):
        if name in debug:
            o = nc.dram_tensor("dbg_" + name, list(ap.shape), ap.dtype, kind="ExternalOutput").ap()
            S.dma(o, ap, reads=reads, semkey="dbg_" + name)
            dbg_outs.append("dbg_" + name)

    ps = [top.enter_context(nc.psum_tensor("ps%d" % i, [128, 512], F32)) for i in range(4)]
    psO = top.enter_context(nc.psum_tensor("psO", [128, 2048], F32))
    psOb = [psO[:, i * 512:(i + 1) * 512] for i in range(4)]

    def psT(i):
        return ps[i][:].bitcast(BF16)

    ident_f = A.alloc("ident_f", [128, 128])
    ident_b = A.alloc("ident_b", [128, 128], BF16)
    ones_f = A.alloc("ones_f", [128, 128])
    ones_b = A.alloc("ones_b", [128, 128], BF16)
    cst = A.alloc("cst", [128, 2])
    cT = A.alloc("cT", [128, 16, 128])
    cs_ = A.alloc("cs_", [128, 16])
    prm = A.alloc("prm", [128, 64])

    S.dma(ident_f, ident_d, writes=["ident_f"])
    S.dma(cst, consts, writes=["cst"])
    S.dma(cs_, c_vec, writes=["cs_"])
    S.dma(prm[:, 0:24].rearrange("p (k j) -> p k j", k=3), conv_w.rearrange("k (j p) -> p k j", p=128),
          writes=["prm_a"])
    S.dma(prm[:, 24:32], g_out_conv.rearrange("(j p) -> p j", p=128), writes=["prm_b"])
    S.dma(prm[:, 32:40], g_out_attn.rearrange("(j p) -> p j", p=128), writes=["prm_c"])
    S.dma(prm[:, 40:44], g_q_lat.rearrange("(j p) -> p j", p=128), writes=["prm_d"])
    S.dma(prm[:, 44:48], g_kv_lat.rearrange("(j p) -> p j", p=128), writes=["prm_e"])
    PRM = ["prm_a", "prm_b", "prm_c", "prm_d", "prm_e"]
    S.op("act", lambda e: e.copy(out=ident_b, in_=ident_f), reads=["ident_f"], writes=["ident_b"])
    S.op("pool", lambda e: e.memset(ones_f, 1.0), writes=["ones_f"])
    S.op("pool", lambda e: e.memset(ones_b, 1.0), writes=["ones_b"])
    S.op("act", lambda e: e.activation(out=cs_, in_=cs_, func=AF.Silu), reads=["cs_"], writes=["cs_"])
    for kc in range(16):
        S.op("dve", lambda e, kc=kc: e.tensor_copy(out=cT[:, kc, :], in_=cs_[:, kc:kc + 1].to_broadcast([128, 128])),
             reads=["cs_"], writes=["cT"])

    def rstd_from(out_ap, in_ap, n, reads, writes):
        S.op("act", lambda e: e.activation(out=out_ap, in_=in_ap, func=AF.Sqrt, scale=1.0 / n, bias=EPS),
             reads=reads, writes=writes)
        S.op("dve", lambda e: e.reciprocal(out=out_ap, in_=out_ap), reads=writes, writes=writes)

    def compute_mod(col0, ncols, dst, dst_key):
        wa = [A.alloc("wa%d" % i, [128, 16, 256]) for i in range(2)]
        bb = [A.alloc("bb%d" % i, [128, 256]) for i in range(2)]
        for g in range(ncols // 256):
            sl = g % 2
            c0 = col0 + g * 256
            S.dma(wa[sl], w_ada[:, c0:c0 + 256].rearrange("(k p) c -> p k c", p=128), writes=["wa%d" % sl])
            S.dma(bb[sl], b_ada[c0:c0 + 256].partition_broadcast(128), writes=["bb%d" % sl])
            pb = ps[g % 2]
            for kc in range(16):
                S.op("pe", lambda e, kc=kc, sl=sl, pb=pb: e.matmul(pb[:, 0:256], cT[:, kc, :], wa[sl][:, kc, :],
                                                                     start=(kc == 0), stop=(kc == 15)),
                     reads=["cT", "wa%d" % sl], writes=["ps%d" % (g % 2)])
            S.op("dve", lambda e, g=g, sl=sl, pb=pb: e.tensor_tensor(out=dst[:, g * 256:(g + 1) * 256], in0=pb[:, 0:256],
                                                                      in1=bb[sl], op=ALU.add),
                 reads=["ps%d" % (g % 2), "bb%d" % sl], writes=[dst_key])
        S.barrier()
        A.release("wa0", "wa1", "bb0", "bb1")

    def norm_tile(xt, xkey, hb, hbkey, gsc, sh, ss, gkeys):
        S.op("act", lambda e: e.activation(out=hb, in_=xt, func=AF.Square, accum_out=ss[:, 0:1]),
             reads=[xkey], writes=[hbkey, "ss"])
        rstd_from(ss[:, 0:1], ss[:, 0:1], D, ["ss"], ["ss"])
        S.op("dve", lambda e: e.scalar_tensor_tensor(out=xt, in0=xt, scalar=ss[:, 0:1], in1=gsc, op0=ALU.mult,
                                                      op1=ALU.mult), reads=[xkey, "ss"] + gkeys, writes=[xkey])
        S.op("pool", lambda e: e.tensor_tensor(out=hb, in0=xt, in1=sh, op=ALU.add), reads=[xkey] + gkeys,
             writes=[hbkey])

    def transpose_tile(hb, hbkey, dstT, dkey, col0):
        for half in range(2):
            bank = 2 + half
            pt = psT(bank)
            for k in range(8):
                kc = half * 8 + k
                S.op("pe", lambda e, kc=kc, k=k, pt=pt: e.transpose(out=pt[:, k * 128:(k + 1) * 128],
                                                                     in_=hb[:, kc * 128:(kc + 1) * 128], identity=ident_b),
                     reads=[hbkey, "ident_b"], writes=["ps%d" % bank])
            S.op("act", lambda e, half=half, pt=pt: e.copy(out=dstT[:, half * 8:(half + 1) * 8, col0:col0 + 128],
                                                            in_=pt[:, 0:1024].rearrange("p (k t) -> p k t", k=8)),
                 reads=["ps%d" % bank], writes=[dkey])

    gsc = A.alloc("gsc", [128, D])
    shm = A.alloc("shm", [128, D])
    compute_mod(0, D, shm, "shm")
    compute_mod(D, D, gsc, "gsc")
    gnm = A.alloc("gnm", [128, D])
    S.dma(gnm, g_norm_mix.partition_broadcast(128), writes=["gnm"])
    S.op("dve", lambda e: e.scalar_tensor_tensor(out=gsc, in0=gsc, scalar=1.0, in1=gnm, op0=ALU.add,
                                                  op1=ALU.mult), reads=["gsc", "gnm"], writes=["gsc"])
    S.barrier()
    A.release("gnm")
    dbg("shm", shm[0:1, :], ["shm"])
    dbg("gsc", gsc[0:1, :], ["gsc"])
    dbg("prm", prm, PRM)
    if stop == "mod":
        return fin()

    hT = A.alloc("hT", [128, 16, NT], BF16)
    hTh = A.alloc("hTh", [128, 16, 2], BF16)
    kvn = A.alloc("kvn", [128, 4, 2 * NT], BF16, hi=True)
    qn = A.alloc("qn", [128, 4, NT], BF16, hi=True)
    krope = A.alloc("krope", [64, 2 * NT], BF16, hi=True)
    rkv_bc = A.alloc("rkv_bc", [128, 2 * NT], hi=True)
    rq_bc = A.alloc("rq_bc", [128, NT], hi=True)
    ropet = A.alloc("ropet", [64, 2, NT])
    xts = [A.alloc("xt%d" % i, [128, D]) for i in range(2)]
    hb = A.alloc("hb", [128, D], BF16)
    ss = A.alloc("ss", [128, 2])
    wraw = A.alloc("wraw", [128, 8, 3, 128])
    wbf = [A.alloc("wbf%d" % i, [128, 16, 3, 128], BF16) for i in range(2)]
    sqb = A.alloc("sqb", [128, NT])

    wctr = [0]

    def load_w(cols_list, width, swap=False):
        sl = wctr[0] % 2
        wctr[0] += 1
        n = len(cols_list)
        for kh in range(2):
            for i, c0 in enumerate(cols_list):
                S.dma(wraw[:, :, i, 0:width],
                      w_in[kh * 1024:(kh + 1) * 1024, c0:c0 + width].rearrange("(k p) c -> p k c", p=128),
                      writes=["wraw%d" % i])
            if not swap:
                S.op("pool", lambda e, kh=kh: e.tensor_copy(out=wbf[sl][:, kh * 8:(kh + 1) * 8, 0:n, 0:width],
                                                            in_=wraw[:, :, 0:n, 0:width]),
                     reads=["wraw%d" % i for i in range(n)], writes=["wbf%d" % sl])
            else:
                S.op("pool", lambda e, kh=kh: e.tensor_copy(out=wbf[sl][:, kh * 8:(kh + 1) * 8, 0, 0:64], in_=wraw[:, :, 0, 0:64]),
                     reads=["wraw0"], writes=["wbf%d" % sl])
                S.op("pool", lambda e, kh=kh: e.tensor_copy(out=wbf[sl][:, kh * 8:(kh + 1) * 8, 1, 0:32], in_=wraw[:, :, 0, 32:64]),
                     reads=["wraw0"], writes=["wbf%d" % sl])
                S.op("pool", lambda e, kh=kh: e.tensor_copy(out=wbf[sl][:, kh * 8:(kh + 1) * 8, 1, 32:64], in_=wraw[:, :, 0, 0:32]),
                     reads=["wraw0"], writes=["wbf%d" % sl])
        return wbf[sl], "wbf%d" % sl

    def lin(pbank, wt, wkey, gi, width, rhs_fn, n):
        for kc in range(16):
            S.op("pe", lambda e, kc=kc: e.matmul(ps[pbank][0:width, 0:n], wt[:, kc, gi, 0:width], rhs_fn(kc),
                                                  start=(kc == 0), stop=(kc == 15)),
                 reads=[wkey, "hT", "hTh"], writes=["ps%d" % pbank])

    def lat_chunk(wt, wkey, dst, dkey, dcol0, gcol, stat_first, stat_last):
        for th in range(2):
            lin(0, wt, wkey, 0, 128, lambda kc, th=th: hT[:, kc, th * 512:(th + 1) * 512], 512)
            S.op("act", lambda e: e.copy(out=sqb[:, 512:1024], in_=ps[0][:, :]), reads=["ps0"], writes=["sqraw"])
            S.op("dve", lambda e, th=th: e.tensor_scalar(out=dst[:, dcol0 + th * 512: dcol0 + (th + 1) * 512],
                                                          in0=sqb[:, 512:1024], scalar1=prm[:, gcol:gcol + 1], scalar2=None,
                                                          op0=ALU.mult), reads=["sqraw"] + PRM, writes=[dkey])
            S.op("pool", lambda e: e.tensor_tensor(out=sqb[:, 0:512], in0=sqb[:, 512:1024], in1=sqb[:, 512:1024], op=ALU.mult),
                 reads=["sqraw"], writes=["sqb"])
            if stop == "kv0c":
                continue
            S.op("pe", lambda e, th=th: e.matmul(ps[2 + th][:, :], ones_f, sqb[:, 0:512], start=stat_first,
                                                  stop=stat_last), reads=["ones_f", "sqb"], writes=["ps%d" % (2 + th)])

    def krope_part(wt, wkey, tok0):
        for th in range(2):
            lin(0, wt, wkey, 0, 64, lambda kc, th=th: hT[:, kc, th * 512:(th + 1) * 512], 512)
            lin(1, wt, wkey, 1, 64, lambda kc, th=th: hT[:, kc, th * 512:(th + 1) * 512], 512)
            S.op("dve", lambda e, th=th: e.tensor_tensor(out=sqb[0:64, 0:512], in0=ps[0][0:64, :],
                                                          in1=ropet[:, 0, th * 512:(th + 1) * 512], op=ALU.mult),
                 reads=["ps0", "ropet"], writes=["sqb"])
            S.op("dve", lambda e, th=th: e.tensor_tensor(out=sqb[0:64, 512:1024], in0=ps[1][0:64, :],
                                                          in1=ropet[:, 1, th * 512:(th + 1) * 512], op=ALU.mult),
                 reads=["ps1", "ropet"], writes=["sqraw"])
            S.op("pool", lambda e, th=th: e.tensor_tensor(out=krope[:, tok0 + th * 512: tok0 + (th + 1) * 512],
                                                           in0=sqb[0:64, 0:512], in1=sqb[0:64, 512:1024], op=ALU.add),
                 reads=["sqb", "sqraw"], writes=["krope"])

    for part in range(2):
        src = x_prev if part == 0 else x_own
        S.dma(ropet, rope[2 * part:2 * part + 2].rearrange("f p t -> p f t"), writes=["ropet"])
        for ti in range(8):
            sl = ti % 2
            S.dma(xts[sl], src[ti * 128:(ti + 1) * 128, :], writes=["xt%d" % sl])
            norm_tile(xts[sl], "xt%d" % sl, hb, "hb", gsc, shm, ss, ["gsc", "shm"])
            transpose_tile(hb, "hb", hT, "hT", ti * 128)
            if stop == "A1a" and ti == 1:
                dbg("hT", hT[:, 3, 0:256], ["hT"])
                return fin()
        if stop == "A1":
            dbg("hT", hT[:, 3, :], ["hT"])
            return fin()
        tok0 = part * NT
        for qc in range(4):
            wt, wkey = load_w([3584 + qc * 128], 128)
            if stop == "kv0a":
                dbg("wbf", wt[:, :, 0, :], [wkey])
                return fin()
            lat_chunk(wt, wkey, kvn[:, qc, :], "kvn", tok0, 44 + qc, qc == 0, qc == 3)
            if stop in ("kv0", "kv0b", "kv0c"):
                dbg("kvn", kvn[:, 0, 0:NT], ["kvn"])
                return fin()
        for th in range(2):
            rstd_from(rkv_bc[:, tok0 + th * 512: tok0 + (th + 1) * 512], ps[2 + th][:, :], 512,
                      ["ps%d" % (2 + th)], ["rkv_bc"])
        wt, wkey = load_w([4096], 64, swap=True)
        krope_part(wt, wkey, tok0)
        if part == 0:
            S.op("pool", lambda e: e.tensor_copy(out=hTh, in_=hT[:, :, NT - 2:NT]), reads=["hT"], writes=["hTh"])
            continue
        for qc in range(4):
            wt, wkey = load_w([3072 + qc * 128], 128)
            lat_chunk(wt, wkey, qn[:, qc, :], "qn", 0, 40 + qc, qc == 0, qc == 3)
        for th in range(2):
            rstd_from(rq_bc[:, th * 512:(th + 1) * 512], ps[2 + th][:, :], 512, ["ps%d" % (2 + th)], ["rq_bc"])
    dbg("kvn", kvn[:, 0, :], ["kvn"])
    dbg("krope", krope, ["krope"])
    dbg("rkv", rkv_bc[0:1, :], ["rkv_bc"])
    dbg("qn", qn[:, 0, :], ["qn"])
    dbg("rq", rq_bc[0:1, :], ["rq_bc"])

    if stop == "A2":
        return fin()
    S.barrier()
    A.release("gsc", "shm", "xt0", "xt1", "hb", "ss")
    merged = A.alloc("merged", [128, 16, NT], BF16, hi=True)
    ctmp = A.alloc("ctmp", [128, 512])
    zbuf = A.alloc("zbuf", [128, NT + 2])
    ybuf = A.alloc("ybuf", [128, NT])
    bbuf = A.alloc("bbuf", [128, NT])
    rsb = A.alloc("rsb", [128, NT])
    xs4 = A.alloc("xs4", [128, 4])
    for j in range(8):
        wt, wkey = load_w([j * 128, 1024 + j * 128, 2048 + j * 128], 128)
        for gi, off in ((1, 0), (2, 2)):
            for kc in range(16):
                S.op("pe", lambda e, kc=kc, gi=gi, off=off, wt=wt: e.matmul(ps[3][:, off:off + 2], wt[:, kc, gi, :],
                                                                             hTh[:, kc, :], start=(kc == 0), stop=(kc == 15)),
                     reads=[wkey, "hTh"], writes=["ps3"])
            if gi == 1:
                S.op("act", lambda e: e.copy(out=xs4[:, 0:2], in_=ps[3][:, 0:2]), reads=["ps3"], writes=["xs4"])
            else:
                S.op("dve", lambda e: e.scalar_tensor_tensor(out=zbuf[:, 0:2], in0=xs4[:, 0:2], scalar=cst[:, 0:1],
                                                              in1=ps[3][:, 2:4], op0=ALU.mult, op1=ALU.mult),
                     reads=["xs4", "cst", "ps3"], writes=["zbuf"])
        for th in range(2):
            rf = lambda kc, th=th: hT[:, kc, th * 512:(th + 1) * 512]
            lin(0, wt, wkey, 1, 128, rf, 512)
            lin(1, wt, wkey, 2, 128, rf, 512)
            lin(2, wt, wkey, 0, 128, rf, 512)
            S.op("act", lambda e: e.copy(out=ctmp, in_=ps[0][:, :]), reads=["ps0"], writes=["ctmp"])
            S.op("dve", lambda e, th=th: e.tensor_tensor(out=zbuf[:, 2 + th * 512: 2 + (th + 1) * 512], in0=ctmp,
                                                          in1=ps[1][:, :], op=ALU.mult),
                 reads=["ctmp", "ps1"], writes=["zbuf"])
            S.op("act", lambda e, th=th: e.copy(out=bbuf[:, th * 512:(th + 1) * 512], in_=ps[2][:, :]),
                 reads=["ps2"], writes=["bbuf"])
        S.op("dve", lambda e, j=j: e.tensor_scalar(out=ybuf, in0=zbuf[:, 0:NT], scalar1=prm[:, j:j + 1],
                                                    scalar2=None, op0=ALU.mult), reads=["zbuf"] + PRM, writes=["ybuf"])
        for k in (1, 2):
            S.op("dve", lambda e, j=j, k=k: e.scalar_tensor_tensor(out=ybuf, in0=zbuf[:, k:NT + k],
                                                                    scalar=prm[:, 8 * k + j: 8 * k + j + 1], in1=ybuf,
                                                                    op0=ALU.mult, op1=ALU.add),
                 reads=["zbuf", "ybuf"] + PRM, writes=["ybuf"])
        S.op("pool", lambda e: e.tensor_tensor(out=ybuf, in0=ybuf, in1=bbuf, op=ALU.mult),
             reads=["ybuf", "bbuf"], writes=["ybuf"])
        S.op("act", lambda e: e.activation(out=sqb, in_=ybuf, func=AF.Square), reads=["ybuf"], writes=["sqb"])
        for th in range(2):
            S.op("pe", lambda e, th=th: e.matmul(ps[3][:, :], ones_f, sqb[:, th * 512:(th + 1) * 512], start=True,
                                                  stop=True), reads=["ones_f", "sqb"], writes=["ps3"])
            rstd_from(rsb[:, th * 512:(th + 1) * 512], ps[3][:, :], 128, ["ps3"], ["rsb"])
        S.op("dve", lambda e, j=j: e.scalar_tensor_tensor(out=merged[:, j, :], in0=ybuf, scalar=prm[:, 24 + j:25 + j],
                                                           in1=rsb, op0=ALU.mult, op1=ALU.mult),
             reads=["ybuf", "rsb"] + PRM, writes=["merged"])
    dbg("mconv", merged[:, 0, :], ["merged"])

    if stop == "conv":
        return fin()
    S.barrier()
    A.release("hT", "hTh", "wraw", "wbf0", "wbf1", "sqb", "ctmp", "zbuf", "ybuf", "bbuf", "rsb", "xs4")
    qTn = A.alloc("qTn", [128, 8, NT], BF16, hi=True)
    qTr = A.alloc("qTr", [64, 8, NT], BF16, hi=True)
    wq = A.alloc("wq", [128, 4, 1536], BF16)
    wqr = A.alloc("wqr", [128, 4, 8, 64], BF16)
    wst = A.alloc("wst", [128, 2048])
    cq = A.alloc("cq", [64, 2, NT])
    rtmp = A.alloc("rtmp", [64, 1024])
    for kc in range(4):
        S.dma(wst[:, 0:1536], w_uq[kc * 128:(kc + 1) * 128, :], writes=["wst"])
        S.op("pool", lambda e, kc=kc: e.tensor_copy(out=wq[:, kc, :], in_=wst[:, 0:1536]), reads=["wst"], writes=["wq"])
    wq4 = wq.rearrange("p k (h c) -> p k h c", h=8)
    S.op("pool", lambda e: e.tensor_copy(out=wqr[:, :, :, 0:32], in_=wq4[:, :, :, 160:192]), reads=["wq"], writes=["wqr"])
    S.op("pool", lambda e: e.tensor_copy(out=wqr[:, :, :, 32:64], in_=wq4[:, :, :, 128:160]), reads=["wq"], writes=["wqr"])
    for f in range(2):
        S.op("dve", lambda e, f=f: e.tensor_tensor(out=cq[:, f, :], in0=ropet[:, f, :], in1=rq_bc[0:64, :], op=ALU.mult),
             reads=["ropet", "rq_bc"], writes=["cq"])
    dbg("rq2", rq_bc[0:1, :], ["rq_bc"])
    dbg("wq", wq[:, 0, 0:192], ["wq"])
    dbg("wq3", wq[:, 3, 0:192], ["wq"])
    dbg("wst", wst[:, 0:192], ["wst"])
    if stop == "A3q0":
        return fin()
    for h in range(8):
        for th in range(2):
            tsl = slice(th * 512, (th + 1) * 512)
            for kc in range(4):
                S.op("pe", lambda e, kc=kc, h=h, tsl=tsl: e.matmul(ps[0][:, :], wq[:, kc, h * 192:h * 192 + 128], qn[:, kc, tsl],
                                                                   start=(kc == 0), stop=(kc == 3)),
                     reads=["wq", "qn"], writes=["ps0"])
            S.op("dve", lambda e, h=h, tsl=tsl: e.tensor_tensor(out=qTn[:, h, tsl], in0=ps[0][:, :], in1=rq_bc[:, tsl], op=ALU.mult),
                 reads=["ps0", "rq_bc"], writes=["qTn"])
            for kc in range(4):
                S.op("pe", lambda e, kc=kc, h=h, tsl=tsl: e.matmul(ps[1][0:64, :], wq[:, kc, h * 192 + 128:h * 192 + 192], qn[:, kc, tsl],
                                                                   start=(kc == 0), stop=(kc == 3)),
                     reads=["wq", "qn"], writes=["ps1"])
            for kc in range(4):
                S.op("pe", lambda e, kc=kc, h=h, tsl=tsl: e.matmul(ps[2][0:64, :], wqr[:, kc, h, :], qn[:, kc, tsl],
                                                                   start=(kc == 0), stop=(kc == 3)),
                     reads=["wqr", "qn"], writes=["ps2"])
            S.op("dve", lambda e, tsl=tsl: e.tensor_tensor(out=rtmp[:, 0:512], in0=ps[1][0:64, :], in1=cq[:, 0, tsl], op=ALU.mult),
                 reads=["ps1", "cq"], writes=["rtmp"])
            S.op("dve", lambda e, tsl=tsl: e.tensor_tensor(out=rtmp[:, 512:1024], in0=ps[2][0:64, :], in1=cq[:, 1, tsl], op=ALU.mult),
                 reads=["ps2", "cq"], writes=["rtmp"])
            S.op("pool", lambda e, h=h, tsl=tsl: e.tensor_tensor(out=qTr[:, h, tsl], in0=rtmp[:, 0:512], in1=rtmp[:, 512:1024], op=ALU.add),
                 reads=["rtmp"], writes=["qTr"])
        if stop == "A3q1":
            dbg("qTn", qTn[:, 0, :], ["qTn"])
            dbg("wqb", wq[:, 0, 0:192], ["wq"])
            dbg("qn2", qn[:, 0, :], ["qn"])
            return fin()
    dbg("qTn", qTn[:, 0, :], ["qTn"])
    dbg("qTr", qTr[:, 0, :], ["qTr"])

    S.barrier()
    A.release("qn", "rq_bc", "ropet", "wq", "wqr", "cq", "rtmp", "wst")
    kTn = A.alloc("kTn", [128, 8, 2 * NT], BF16, hi=True)
    vtm = A.alloc("vtm", [128, 16, 1024], BF16)
    rkv_tm = A.alloc("rkv_tm", [128, 16])
    wkv = A.alloc("wkv", [128, 4, 2048], BF16)
    wst_kv = A.alloc("wst_kv", [128, 2048])
    for kc in range(4):
        S.dma(wst_kv, w_ukv[kc * 128:(kc + 1) * 128, :], writes=["wst_kv"])
        S.op("pool", lambda e, kc=kc: e.tensor_copy(out=wkv[:, kc, :], in_=wst_kv), reads=["wst_kv"], writes=["wkv"])
    for blk in range(16):
        S.op("pe", lambda e, blk=blk: e.transpose(out=ps[3][:, 0:128], in_=rkv_bc[:, blk * 128:(blk + 1) * 128],
                                                   identity=ident_f), reads=["rkv_bc", "ident_f"], writes=["ps3"])
        S.op("act", lambda e, blk=blk: e.copy(out=rkv_tm[:, blk:blk + 1], in_=ps[3][:, 0:1]), reads=["ps3"], writes=["rkv_tm"])
    for h in range(8):
        for tc in range(4):
            tsl = slice(tc * 512, (tc + 1) * 512)
            pb = tc % 2
            for kc in range(4):
                S.op("pe", lambda e, kc=kc, h=h, tsl=tsl, pb=pb: e.matmul(ps[pb][:, :], wkv[:, kc, h * 256:h * 256 + 128], kvn[:, kc, tsl],
                                                                          start=(kc == 0), stop=(kc == 3)),
                     reads=["wkv", "kvn"], writes=["ps%d" % pb])
            S.op("dve", lambda e, h=h, tsl=tsl, pb=pb: e.tensor_tensor(out=kTn[:, h, tsl], in0=ps[pb][:, :], in1=rkv_bc[:, tsl], op=ALU.mult),
                 reads=["ps%d" % pb, "rkv_bc"], writes=["kTn"])
    wkv4 = wkv.rearrange("p k (h c) -> p k h c", h=8)
    for blk in range(16):
        for hg in range(2):
            pb = hg
            for kc in range(4):
                S.op("pe", lambda e, kc=kc, blk=blk, hg=hg, pb=pb: e.matmul(
                    ps[pb][:, :].rearrange("p (h c) -> p h c", h=4), kvn[:, kc, blk * 128:(blk + 1) * 128],
                    wkv4[:, kc, hg * 4:(hg + 1) * 4, 128:256], start=(kc == 0), stop=(kc == 3)),
                    reads=["wkv", "kvn"], writes=["ps%d" % pb])
            S.op("act", lambda e, blk=blk, hg=hg, pb=pb: e.activation(out=vtm[:, blk, hg * 512:(hg + 1) * 512], in_=ps[pb][:, :],
                                                                      func=AF.Copy, scale=rkv_tm[:, blk:blk + 1]),
                 reads=["ps%d" % pb, "rkv_tm"], writes=["vtm"])
    dbg("kTn", kTn[:, 0, :], ["kTn"])
    dbg("vtm", vtm[:, 0, :], ["vtm"])

    if stop == "A3":
        return fin()
    S.barrier()
    A.release("kvn", "rkv_bc", "wkv", "wst_kv", "rkv_tm")
    pTb = [A.alloc("pT%d" % i, [128, NT], BF16) for i in range(2)]
    oT = A.alloc("oT", [128, NT])
    rz = A.alloc("rz", [128, NT])
    sq4 = A.alloc("sq4", [128, NT])
    it = 0
    last_j = {0: 11, 1: 15}
    for h in range(8):
        for j in range(16):
            own = j >= 8
            jj = j - 8
            sl = it % 2
            it += 1
            ranges = []
            for th in range(2):
                lo = max(jj * 128, th * 512) if own else th * 512
                hi = (th + 1) * 512
                if lo < hi:
                    ranges.append((th, lo, hi))
            for (th, lo, hi) in ranges:
                n = hi - lo
                S.op("pe", lambda e, h=h, j=j, lo=lo, hi=hi, n=n, th=th: e.matmul(ps[th][:, 0:n], kTn[:, h, j * 128:(j + 1) * 128],
                                                                                 qTn[:, h, lo:hi], start=True, stop=False),
                     reads=["kTn", "qTn"], writes=["ps%d" % th])
                S.op("pe", lambda e, h=h, j=j, lo=lo, hi=hi, n=n, th=th: e.matmul(ps[th][:, 0:n], krope[:, j * 128:(j + 1) * 128],
                                                                                 qTr[:, h, lo:hi], start=False, stop=True),
                     reads=["krope", "qTr"], writes=["ps%d" % th])
                if own:
                    S.op("act", lambda e, lo=lo, hi=hi, n=n, th=th, sl=sl: e.activation(out=pTb[sl][:, lo:hi], in_=ps[th][:, 0:n],
                                                                                       func=AF.Exp, scale=SCALE),
                         reads=["ps%d" % th], writes=["pT%d" % sl])
                else:
                    S.op("act", lambda e, lo=lo, hi=hi, n=n, th=th, sl=sl: e.activation(out=pTb[sl][:, lo:hi], in_=ps[th][:, 0:n],
                                                                                       func=AF.Exp, scale=SCALE, bias=cst[:, 1:2]),
                         reads=["ps%d" % th, "cst"], writes=["pT%d" % sl])
            if own:
                S.op("pool", lambda e, jj=jj, sl=sl: e.memset(pTb[sl][64:128, jj * 128: jj * 128 + 64], 0.0),
                     reads=[], writes=["pT%d" % sl])
            for (th, lo, hi) in ranges:
                o0 = lo - th * 512
                n = hi - lo
                S.op("pe", lambda e, h=h, j=j, lo=lo, hi=hi, th=th, o0=o0, n=n, sl=sl: e.matmul(
                    psOb[th][:, o0:o0 + n], vtm[:, j, h * 128:(h + 1) * 128], pTb[sl][:, lo:hi],
                    start=(j == 0), stop=(j == last_j[th])), reads=["vtm", "pT%d" % sl], writes=["psO%d" % th])
                S.op("pe", lambda e, j=j, lo=lo, hi=hi, th=th, o0=o0, n=n, sl=sl: e.matmul(
                    psOb[2 + th][:, o0:o0 + n], ones_b, pTb[sl][:, lo:hi],
                    start=(j == 0), stop=(j == last_j[th])), reads=["ones_b", "pT%d" % sl], writes=["psO%d" % (2 + th)])
        for th in range(2):
            tsl = slice(th * 512, (th + 1) * 512)
            S.op("dve", lambda e, th=th, tsl=tsl: e.reciprocal(out=rz[:, tsl], in_=psOb[2 + th]), reads=["psO%d" % (2 + th)],
                 writes=["rz"])
            S.op("dve", lambda e, th=th, tsl=tsl: e.tensor_tensor(out=oT[:, tsl], in0=psOb[th], in1=rz[:, tsl], op=ALU.mult),
                 reads=["psO%d" % th, "rz"], writes=["oT"])
        S.op("act", lambda e: e.activation(out=sq4, in_=oT, func=AF.Square), reads=["oT"], writes=["sq4"])
        for th in range(2):
            tsl = slice(th * 512, (th + 1) * 512)
            S.op("pe", lambda e, tsl=tsl: e.matmul(ps[2][:, :], ones_f, sq4[:, tsl], start=True, stop=True),
                 reads=["ones_f", "sq4"], writes=["ps2"])
            rstd_from(rz[:, tsl], ps[2][:, :], 128, ["ps2"], ["rz"])
        S.op("dve", lambda e, h=h: e.scalar_tensor_tensor(out=merged[:, 8 + h, :], in0=oT, scalar=prm[:, 32 + h:33 + h],
                                                           in1=rz, op0=ALU.mult, op1=ALU.mult),
             reads=["oT", "rz"] + PRM, writes=["merged"])
    dbg("mattn", merged[:, 8, :], ["merged"])

    if stop == "A4":
        return fin()
    S.barrier()
    A.release("qTn", "qTr", "kTn", "vtm", "krope", "pT0", "pT1", "oT", "rz", "sq4")
    gtm = A.alloc("gtm", [128, D])
    compute_mod(2 * D, D, gtm, "gtm")
    wout = A.alloc("wout", [128, 16, D], BF16)
    wst_o = [A.alloc("wsto_%d" % i, [128, D]) for i in range(2)]
    xts_o = [A.alloc("xto%d" % i, [128, D]) for i in range(2)]
    x2t = [A.alloc("x2t%d" % i, [128, D]) for i in range(2)]
    for kc in range(16):
        sl = kc % 2
        S.dma(wst_o[sl], w_out[kc * 128:(kc + 1) * 128, :], writes=["wsto_%d" % sl])
        S.op("pool", lambda e, kc=kc, sl=sl: e.tensor_copy(out=wout[:, kc, :], in_=wst_o[sl]), reads=["wsto_%d" % sl], writes=["wout"])
    for ti in range(8):
        sl = ti % 2
        S.dma(xts_o[sl], x_own[ti * 128:(ti + 1) * 128, :], writes=["xto%d" % sl])
        for nq in range(4):
            pb = nq % 2
            for kc in range(16):
                S.op("pe", lambda e, kc=kc, ti=ti, nq=nq, pb=pb: e.matmul(ps[pb][:, :], merged[:, kc, ti * 128:(ti + 1) * 128],
                                                                          wout[:, kc, nq * 512:(nq + 1) * 512], start=(kc == 0), stop=(kc == 15)),
                     reads=["merged", "wout"], writes=["ps%d" % pb])
            S.op("dve", lambda e, nq=nq, pb=pb, sl=sl: e.tensor_tensor(out=x2t[sl][:, nq * 512:(nq + 1) * 512], in0=ps[pb][:, :],
                                                                       in1=gtm[:, nq * 512:(nq + 1) * 512], op=ALU.mult),
                 reads=["ps%d" % pb, "gtm"], writes=["x2t%d" % sl])
        S.op("pool", lambda e, sl=sl: e.tensor_tensor(out=x2t[sl], in0=x2t[sl], in1=xts_o[sl], op=ALU.add),
             reads=["x2t%d" % sl, "xto%d" % sl], writes=["x2t%d" % sl])
        S.dma(x2_d[ti * 128:(ti + 1) * 128, :], x2t[sl], reads=["x2t%d" % sl], writes=["x2_d%d" % ti], semkey="x2st%d" % sl)
    dbg("x2t", x2t[1], ["x2t1"])
    if stop == "A5":
        return fin()
    S.barrier()
    A.release("gtm", "wout", "wsto_0", "wsto_1", "xto0", "xto1", "x2t0", "x2t1", "merged")
    gscf = A.alloc("gscf", [128, D])
    shf = A.alloc("shf", [128, D])
    gtf = A.alloc("gtf", [128, D], hi=True)
    compute_mod(3 * D, D, shf, "shf")
    compute_mod(4 * D, D, gscf, "gscf")
    compute_mod(5 * D, D, gtf, "gtf")
    gnf = A.alloc("gnf", [128, D])
    S.dma(gnf, g_norm_ffn.partition_broadcast(128), writes=["gnf"])
    S.op("dve", lambda e: e.scalar_tensor_tensor(out=gscf, in0=gscf, scalar=1.0, in1=gnf, op0=ALU.add,
                                                  op1=ALU.mult), reads=["gscf", "gnf"], writes=["gscf"])
    S.barrier()
    A.release("gnf")
    h2T = A.alloc("h2T", [128, 16, NT], BF16, hi=True)
    thr = A.alloc("thr", [128, 8, 8], hi=True)
    nb = A.alloc("nb", [128, 8, 8], hi=True)
    wqp = A.alloc("wqp", [128, 16, D], BF16)
    wst_p = [A.alloc("wstp_%d" % i, [128, D]) for i in range(2)]
    keysT = A.alloc("keysT", [128, 16, 128], BF16)
    kst = A.alloc("kst", [128, 16, 128], BF16)
    xts_p = [A.alloc("xtp%d" % i, [128, D]) for i in range(2)]
    hb_p = A.alloc("hb_p", [128, D], BF16)
    ss_p = A.alloc("ss_p", [128, 2])
    qTt = A.alloc("qTt", [128, 16, 128], BF16)
    sct = [A.alloc("sct0", [128, 8, 2, 128])] * 2
    tv = A.alloc("tv", [128, 8, 2, 16])
    wk = A.alloc("wk", [128, 256])
    cand = A.alloc("cand", [128, 8, 16, 16])
    bv = A.alloc("bv", [128, 8, 16])
    ez = A.alloc("ez", [128, 16])
    zz = A.alloc("zz", [128, 8])
    nmx = A.alloc("nmx", [128, 8])
    for kc in range(16):
        sl = kc % 2
        S.dma(wst_p[sl], peer_w_q[kc * 128:(kc + 1) * 128, :], writes=["wstp_%d" % sl])
        S.op("pool", lambda e, kc=kc, sl=sl: e.tensor_copy(out=wqp[:, kc, :], in_=wst_p[sl]), reads=["wstp_%d" % sl], writes=["wqp"])
    S.barrier()
    S.dma(wst_p[0].rearrange("p (a b) -> p a b", a=16), peer_keys.rearrange("(a n) d -> n a d", n=128), writes=["wstp_0"])
    S.op("pool", lambda e: e.tensor_copy(out=kst, in_=wst_p[0].rearrange("p (a b) -> p a b", a=16)), reads=["wstp_0"], writes=["kst"])
    for half in range(2):
        pt = psT(2 + half)
        for k in range(8):
            S.op("pe", lambda e, k=k, half=half, pt=pt: e.transpose(out=pt[:, k * 128:(k + 1) * 128], in_=kst[:, half * 8 + k, :],
                                                                     identity=ident_b), reads=["kst", "ident_b"], writes=["ps%d" % (2 + half)])
        S.op("act", lambda e, half=half, pt=pt: e.copy(out=keysT[:, half * 8:(half + 1) * 8, :],
                                                        in_=pt[:, 0:1024].rearrange("p (k t) -> p k t", k=8)),
             reads=["ps%d" % (2 + half)], writes=["keysT"])
    for ti in range(8):
        sl = ti % 2
        S.dma(xts_p[sl], x2_d[ti * 128:(ti + 1) * 128, :], reads=["x2_d%d" % ti], writes=["xtp%d" % sl])
        norm_tile(xts_p[sl], "xtp%d" % sl, hb_p, "hb_p", gscf, shf, ss_p, ["gscf", "shf"])
        transpose_tile(hb_p, "hb_p", h2T, "h2T", ti * 128)
        for hp in range(16):
            pb = hp % 2
            for kc in range(16):
                S.op("pe", lambda e, kc=kc, hp=hp, ti=ti, pb=pb: e.matmul(ps[pb][:, 0:128], wqp[:, kc, hp * 128:(hp + 1) * 128],
                                                                          h2T[:, kc, ti * 128:(ti + 1) * 128], start=(kc == 0), stop=(kc == 15)),
                     reads=["wqp", "h2T"], writes=["ps%d" % pb])
            S.op("act", lambda e, hp=hp, pb=pb: e.copy(out=qTt[:, hp, :], in_=ps[pb][:, 0:128]), reads=["ps%d" % pb], writes=["qTt"])
        for hp in range(16):
            S.op("pe", lambda e, hp=hp: e.matmul(psO[:, hp * 128:(hp + 1) * 128], qTt[:, hp, :], keysT[:, hp, :], start=True, stop=True),
                 reads=["qTt", "keysT"], writes=["psO"])
        sc = sct[sl]
        S.op("act", lambda e, sc=sc: e.copy(out=sc.rearrange("p h s n -> p (h s n)"), in_=psO[:, :]), reads=["psO"], writes=["sct0"])
        S.dma(sc_d[ti * 128:(ti + 1) * 128, :], sc.rearrange("p h s n -> p (h s n)"), reads=["sct0"], writes=["sc_d%d" % ti],
              semkey="scst%d" % sl)
        for h in range(8):
            for p_ in range(2):
                S.op("dve", lambda e, h=h, p_=p_, sc=sc: e.max(out=tv[:, h, p_, 0:8], in_=sc[:, h, p_, :]), reads=["sct0"], writes=["tv"])
                S.op("dve", lambda e, h=h, p_=p_, sc=sc: e.match_replace(out=wk[:, 0:128], in_to_replace=tv[:, h, p_, 0:8],
                                                                         in_values=sc[:, h, p_, :], imm_value=-1e30),
                     reads=["sct0", "tv"], writes=["wk"])
                S.op("dve", lambda e, h=h, p_=p_: e.max(out=tv[:, h, p_, 8:16], in_=wk[:, 0:128]), reads=["wk"], writes=["tv"])
        S.op("dve", lambda e: e.tensor_tensor(out=cand, in0=tv[:, :, 0, :].unsqueeze(3).to_broadcast([128, 8, 16, 16]),
                                              in1=tv[:, :, 1, :].unsqueeze(2).to_broadcast([128, 8, 16, 16]), op=ALU.add),
             reads=["tv"], writes=["cand"])
        for h in range(8):
            ch = cand[:, h, :, :].rearrange("p a b -> p (a b)")
            S.op("dve", lambda e, h=h, ch=ch: e.max(out=bv[:, h, 0:8], in_=ch), reads=["cand"], writes=["bv"])
            S.op("dve", lambda e, h=h, ch=ch: e.match_replace(out=wk, in_to_replace=bv[:, h, 0:8], in_values=ch, imm_value=-1e30),
                 reads=["cand", "bv"], writes=["wk"])
            S.op("dve", lambda e, h=h: e.max(out=bv[:, h, 8:16], in_=wk), reads=["wk"], writes=["bv"])
        S.op("dve", lambda e, ti=ti: e.tensor_copy(out=thr[:, ti, :], in_=bv[:, :, 15]), reads=["bv"], writes=["thr"])
        S.op("dve", lambda e: e.tensor_scalar(out=nmx, in0=bv[:, :, 0], scalar1=-1.0, scalar2=None, op0=ALU.mult),
             reads=["bv"], writes=["nmx"])
        for h in range(8):
            S.op("act", lambda e, h=h: e.activation(out=ez, in_=bv[:, h, :], func=AF.Exp, bias=nmx[:, h:h + 1],
                                                    accum_out=zz[:, h:h + 1]), reads=["bv", "nmx"], writes=["ez", "zz"])
        S.op("act", lambda e: e.activation(out=zz, in_=zz, func=AF.Ln), reads=["zz"], writes=["zz"])
        S.op("dve", lambda e, ti=ti: e.tensor_tensor(out=nb[:, ti, :], in0=nmx, in1=zz, op=ALU.subtract), reads=["nmx", "zz"], writes=["nb"])
    dbg("thr", thr, ["thr"])
    dbg("nb", nb, ["nb"])
    dbg("sct", sct[1].rearrange("p h s n -> p (h s n)"), ["sct0"])

    if stop == "B0":
        return fin()
    S.barrier()
    A.release("wqp", "wstp_0", "wstp_1", "keysT", "kst", "xtp0", "xtp1", "hb_p", "ss_p", "qTt", "sct0", "tv", "wk", "cand",
              "bv", "ez", "zz", "nmx", "gscf", "shf", "cT", "cs_")
    NE = GE // 128
    acc = A.alloc("acc", [128, TP, D])
    sc4 = A.alloc("sc4", [128, TP, 8, 2, 128])
    uraw = [A.alloc("uraw%d" % i, [128, D]) for i in range(2)]
    ubf = [A.alloc("ubf%d" % i, [128, D], BF16) for i in range(2)]
    UT = A.alloc("UT", [128, 16, GE], BF16)
    vraw = [A.alloc("vraw%d" % i, [128, D]) for i in range(2)]
    Vg = A.alloc("Vg", [128, NE, D], BF16)
    Sg = [A.alloc("Sg%d" % i, [128, NE, 128]) for i in range(2)]
    Eg = [A.alloc("Eg%d" % i, [128, NE, 128], BF16) for i in range(2)]
    Tg = [A.alloc("Tg%d" % i, [128, NE, 128], BF16) for i in range(2)]
    Gg = [A.alloc("Gg%d" % i, [128, GE]) for i in range(2)]
    ga = A.alloc("ga", [128, GE], BF16)
    actv = A.alloc("actv", [128, GE], BF16)
    actT = A.alloc("actT", [128, NE, 128], BF16)
    ss_f = A.alloc("ss_f", [128, 2])
    gfin = vraw[0]
    gi_ = 0
    for p in range(NPASS):
        for tt in range(TP):
            ti = p * TP + tt
            S.dma(acc[:, tt, :], x2_d[ti * 128:(ti + 1) * 128, :], reads=["x2_d%d" % ti], writes=["acc%d" % tt])
            S.dma(sc4[:, tt].rearrange("p h s n -> p (h s n)"), sc_d[ti * 128:(ti + 1) * 128, :], reads=["sc_d%d" % ti], writes=["sc4"],
                  semkey="sc4_%d" % tt)
        for g in range(NEXP // GE):
            c0 = g * NE
            for e_ in range(NE):
                sl = (g * NE + e_) % 2
                r0 = (g * NE + e_) * 128
                S.dma(uraw[sl], peer_u[r0:r0 + 128, :], writes=["uraw%d" % sl])
                S.dma(vraw[sl], peer_v[r0:r0 + 128, :], writes=["vraw%d" % sl])
                S.op("pool", lambda e, sl=sl: e.tensor_copy(out=ubf[sl], in_=uraw[sl]), reads=["uraw%d" % sl], writes=["ubf%d" % sl])
                S.op("pool", lambda e, sl=sl, e_=e_: e.tensor_tensor(out=Vg[:, e_, :], in0=vraw[sl], in1=gtf, op=ALU.mult),
                     reads=["vraw%d" % sl, "gtf"], writes=["Vg"])
                for half in range(2):
                    pt = psT(2 + half)
                    for k in range(8):
                        S.op("pe", lambda e, k=k, half=half, pt=pt, sl=sl: e.transpose(out=pt[:, k * 128:(k + 1) * 128],
                                                                                      in_=ubf[sl][:, (half * 8 + k) * 128:(half * 8 + k + 1) * 128],
                                                                                      identity=ident_b),
                             reads=["ubf%d" % sl, "ident_b"], writes=["ps%d" % (2 + half)])
                    S.op("act", lambda e, half=half, pt=pt, e_=e_: e.copy(out=UT[:, half * 8:(half + 1) * 8, e_ * 128:(e_ + 1) * 128],
                                                                          in_=pt[:, 0:1024].rearrange("p (k t) -> p k t", k=8)),
                         reads=["ps%d" % (2 + half)], writes=["UT"])
            for tt in range(TP):
                ti = p * TP + tt
                b = gi_ % 2
                gi_ += 1
                for kc in range(16):
                    S.op("pe", lambda e, kc=kc, ti=ti, b=b: e.matmul(ps[b][:, 0:GE], h2T[:, kc, ti * 128:(ti + 1) * 128], UT[:, kc, :],
                                                                      start=(kc == 0), stop=(kc == 15)),
                         reads=["h2T", "UT"], writes=["ps%d" % b])
                for h in range(8):
                    q = h % 2
                    S.op("pool", lambda e, h=h, tt=tt, q=q, c0=c0: e.tensor_tensor(
                        out=Sg[q], in0=sc4[:, tt, h, 0, c0:c0 + NE].unsqueeze(2).to_broadcast([128, NE, 128]),
                        in1=sc4[:, tt, h, 1, :].unsqueeze(1).to_broadcast([128, NE, 128]), op=ALU.add),
                        reads=["sc4"], writes=["Sg%d" % q])
                    S.op("act", lambda e, h=h, ti=ti, q=q: e.activation(out=Eg[q], in_=Sg[q], func=AF.Exp, bias=nb[:, ti, h:h + 1]),
                         reads=["Sg%d" % q, "nb"], writes=["Eg%d" % q])
                    if h == 0:
                        S.op("dve", lambda e, h=h, ti=ti, q=q, b=b: e.scalar_tensor_tensor(
                            out=Gg[b].rearrange("p (a n) -> p a n", a=NE), in0=Sg[q], scalar=thr[:, ti, h:h + 1], in1=Eg[q],
                            op0=ALU.is_ge, op1=ALU.mult), reads=["Sg%d" % q, "Eg%d" % q, "thr"], writes=["Gg%d" % b])
                    else:
                        S.op("dve", lambda e, h=h, ti=ti, q=q: e.scalar_tensor_tensor(
                            out=Tg[q], in0=Sg[q], scalar=thr[:, ti, h:h + 1], in1=Eg[q],
                            op0=ALU.is_ge, op1=ALU.mult), reads=["Sg%d" % q, "Eg%d" % q, "thr"], writes=["Tg%d" % q])
                        S.op("pool", lambda e, q=q, b=b: e.tensor_tensor(out=Gg[b], in0=Gg[b], in1=Tg[q].rearrange("p a n -> p (a n)"),
                                                                         op=ALU.add), reads=["Gg%d" % b, "Tg%d" % q], writes=["Gg%d" % b])
                S.op("act", lambda e, b=b: e.activation(out=ga, in_=ps[b][:, 0:GE], func=AF.Gelu), reads=["ps%d" % b], writes=["ga"])
                S.op("dve", lambda e, b=b: e.tensor_tensor(out=actv, in0=ga, in1=Gg[b], op=ALU.mult), reads=["ga", "Gg%d" % b], writes=["actv"])
                pt = psT(2 + b)
                for e_ in range(NE):
                    S.op("pe", lambda e, e_=e_, pt=pt: e.transpose(out=pt[:, e_ * 128:(e_ + 1) * 128], in_=actv[:, e_ * 128:(e_ + 1) * 128],
                                                                    identity=ident_b), reads=["actv", "ident_b"], writes=["ps%d" % (2 + b)])
                S.op("act", lambda e, pt=pt: e.copy(out=actT, in_=pt[:, 0:GE].rearrange("p (k t) -> p k t", k=NE)),
                     reads=["ps%d" % (2 + b)], writes=["actT"])
                for nq in range(4):
                    for e_ in range(NE):
                        S.op("pe", lambda e, e_=e_, nq=nq: e.matmul(psOb[nq], actT[:, e_, :], Vg[:, e_, nq * 512:(nq + 1) * 512],
                                                                    start=(e_ == 0), stop=(e_ == NE - 1)),
                             reads=["actT", "Vg"], writes=["psO"])
                S.op("dve", lambda e, tt=tt: e.tensor_tensor(out=acc[:, tt, :], in0=acc[:, tt, :], in1=psO[:, :], op=ALU.add),
                     reads=["psO", "acc%d" % tt], writes=["acc%d" % tt])
        S.barrier()
        S.dma(gfin, g_final.partition_broadcast(128), writes=["vraw0"])
        for tt in range(TP):
            ti = p * TP + tt
            a_t = acc[:, tt, :]
            S.op("act", lambda e, a_t=a_t: e.activation(out=ubf[0], in_=a_t, func=AF.Square, accum_out=ss_f[:, 0:1]),
                 reads=["acc%d" % tt], writes=["ubf0", "ss"])
            rstd_from(ss_f[:, 0:1], ss_f[:, 0:1], D, ["ss"], ["ss"])
            S.op("dve", lambda e, a_t=a_t: e.scalar_tensor_tensor(out=a_t, in0=a_t, scalar=ss_f[:, 0:1], in1=gfin, op0=ALU.mult,
                                                                   op1=ALU.mult), reads=["acc%d" % tt, "ss", "vraw0"], writes=["acc%d" % tt])
            S.dma(y_out[ti * 128:(ti + 1) * 128, :], a_t, reads=["acc%d" % tt], semkey="yout%d" % tt)
    return fin()


_CACHE = {}


def _host_inputs(x, c, w):
    ident = np.eye(128, dtype=np.float32)
    inv = 1.0 / (10000.0 ** (np.arange(0, 64, 2, dtype=np.float32) / 64.0))
    maps = []
    for core in range(8):
        b, s = core // 2, core % 2
        pos_own = np.arange(s * NT, (s + 1) * NT, dtype=np.float32)
        pos_prev = np.arange(0, NT, dtype=np.float32)
        tabs = []
        for pos in (pos_prev, pos_own):
            ang = inv[:, None] * pos[None, :]
            cs, sn = np.cos(ang).astype(np.float32), np.sin(ang).astype(np.float32)
            tabs.append(np.concatenate([cs, cs], 0))
            tabs.append(np.concatenate([-sn, sn], 0))
        consts = np.zeros((128, 2), np.float32)
        consts[:, 0] = float(s)
        consts[:, 1] = 0.0 if s == 1 else -30000.0
        m = dict(w)
        m["x_own"] = np.ascontiguousarray(x[b, s * NT:(s + 1) * NT])
        m["x_prev"] = np.ascontiguousarray(x[b, 0:NT])
        m["c_vec"] = np.ascontiguousarray(c[b].reshape(16, 128).T)
        m["consts"] = consts
        m["rope"] = np.stack(tabs).astype(np.float32)
        m["ident"] = ident
        maps.append(m)
    return maps


def kernel(x, c, w_ada, b_ada, g_norm_mix, w_in, conv_w, g_q_lat, w_uq, g_kv_lat, w_ukv, g_out_conv, g_out_attn,
           w_out, g_norm_ffn, peer_w_q, peer_sub_keys, peer_u, peer_v, g_final, _debug=(), _stop=None):
    f = lambda a: np.ascontiguousarray(np.asarray(a, dtype=np.float32))
    x, c = f(x), f(c)
    w = {
        "w_ada": f(w_ada)[0], "b_ada": f(b_ada)[0], "g_norm_mix": f(g_norm_mix)[0], "w_in": f(w_in)[0],
        "conv_w": f(conv_w)[0], "g_q_lat": f(g_q_lat)[0], "w_uq": f(w_uq)[0], "g_kv_lat": f(g_kv_lat)[0],
        "w_ukv": f(w_ukv)[0], "g_out_conv": f(g_out_conv)[0], "g_out_attn": f(g_out_attn)[0], "w_out": f(w_out)[0],
        "g_norm_ffn": f(g_norm_ffn)[0], "peer_w_q": f(peer_w_q)[0],
        "peer_keys": f(peer_sub_keys)[0].reshape(16 * 128, 128), "peer_u": f(peer_u)[0], "peer_v": f(peer_v)[0],
        "g_final": f(g_final),
    }
    key = (tuple(_debug), _stop)
    if key not in _CACHE:
        _CACHE[key] = build(debug=_debug, stop=_stop)
    nc, dbg_outs = _CACHE[key]
    maps = _host_inputs(x, c, w)
    res = run_bass_kernel_spmd(nc, maps, core_ids=list(range(8)))
    out = np.empty((4, SEQ, D), np.float32)
    for core in range(8):
        b, s = core // 2, core % 2
        out[b, s * NT:(s + 1) * NT] = res.results[core]["y_out"]
    if _debug:
        return out, [{k: r[k] for k in dbg_outs} for r in res.results]
    return out
```

```python
import numpy as np
from contextlib import ExitStack
import concourse.bass as bass
import concourse.mybir as mybir
from concourse.bass_utils import run_bass_kernel_spmd

F32 = mybir.dt.float32
BF16 = mybir.dt.bfloat16
AF = mybir.ActivationFunctionType
ALU = mybir.AluOpType

D = 2048
SEQ = 2048
NT = 1024
EPS = 1e-6
NEXP = 16384
GE = 512
NPASS = 2
TP = 8 // NPASS
SCALE = 192 ** -0.5
POOL_AS = None


class Sched:
    ENG = ("pe", "act", "dve", "pool", "sp")

    def __init__(self, nc):
        self.nc = nc
        self.streams = {e: [] for e in self.ENG}
        self.sems = {}
        self._sem_ctx = []
        self.count = {}
        self.seen = {e: {} for e in self.ENG}
        self.dep = {}
        for e in ("pe", "act", "dve", "pool"):
            self._mksem("E_" + e)

    def _mksem(self, name):
        if name not in self.sems:
            ctx = self.nc.semaphore(name)
            self.sems[name] = ctx.__enter__()
            self._sem_ctx.append(ctx)
            self.count[name] = 0
        return name

    def close(self):
        for ctx in reversed(self._sem_ctx):
            ctx.__exit__(None, None, None)

    def _d(self, k):
        return self.dep.setdefault(k, {"w": {}, "r": {}})

    def _collect(self, reads, writes):
        need = {}
        for k in reads:
            for s, v in self._d(k)["w"].items():
                need[s] = max(need.get(s, 0), v)
        for k in writes:
            d = self._d(k)
            for dd in (d["w"], d["r"]):
                for s, v in dd.items():
                    need[s] = max(need.get(s, 0), v)
        return need

    def _emit_waits(self, eng, need):
        own = "E_" + eng
        for s, v in need.items():
            if s == own and eng == "pe":
                continue
            if self.seen[eng].get(s, 0) >= v:
                continue
            self.seen[eng][s] = v
            self.streams[eng].append(("wait", s, v))

    def _record(self, reads, writes, s, v):
        for k in writes:
            d = self._d(k)
            d["w"] = {s: v}
            d["r"] = {}
        for k in reads:
            if k in writes:
                continue
            d = self._d(k)
            d["r"][s] = max(d["r"].get(s, 0), v)

    def op(self, eng, fn, reads=(), writes=()):
        if eng == "pool" and POOL_AS is not None:
            eng = POOL_AS
        self._emit_waits(eng, self._collect(reads, writes))
        s = "E_" + eng
        self.count[s] += 1
        self.streams[eng].append(("op", fn, s, 1))
        self._record(reads, writes, s, self.count[s])

    def dma(self, out_ap, in_ap, reads=(), writes=(), semkey=None):
        self._emit_waits("sp", self._collect(reads, writes))
        s = self._mksem("D_" + str(semkey if semkey is not None else (list(writes) + list(reads))[0]))
        self.count[s] += 16
        self.streams["sp"].append(("op", lambda e, o=out_ap, i=in_ap: e.dma_start(out=o, in_=i), s, 16))
        self._record(reads, writes, s, self.count[s])

    def barrier(self):
        for eng in self.ENG:
            for s, v in self.count.items():
                if v > 0 and self.seen[eng].get(s, 0) < v and not (eng == "pe" and s == "E_pe"):
                    self.seen[eng][s] = v
                    self.streams[eng].append(("wait", s, v))

    def final_wait(self, eng="sp"):
        for s, v in self.count.items():
            if v > 0 and self.seen[eng].get(s, 0) < v:
                self.seen[eng][s] = v
                self.streams[eng].append(("wait", s, v))

    def emit(self):
        names = {"pe": "tensor", "act": "scalar", "dve": "vector", "pool": "gpsimd", "sp": "sync"}
        with self.nc.Block() as block:
            for e in self.ENG:
                stream = self.streams[e]

                def body(engine, stream=stream):
                    for it in stream:
                        if it[0] == "wait":
                            engine.wait_ge(self.sems[it[1]], it[2])
                        else:
                            it[1](engine).then_inc(self.sems[it[2]], it[3])
                getattr(block, names[e])(body)


class Arena:
    def __init__(self, nc, stack, nbytes=206000):
        self.words = nbytes // 4
        self.t = stack.enter_context(nc.sbuf_tensor("arena", [128, self.words], F32))
        self.free = [(0, self.words)]
        self.live = {}

    def alloc(self, name, shape, dt=F32, hi=False):
        n = 1
        for s in shape[1:]:
            n *= s
        w = (n * (4 if dt == F32 else 2) + 3) // 4
        w = (w + 15) // 16 * 16
        order = range(len(self.free) - 1, -1, -1) if hi else range(len(self.free))
        for i in order:
            o, sz = self.free[i]
            if sz >= w:
                if hi:
                    self.free[i] = (o, sz - w)
                    o = o + sz - w
                else:
                    self.free[i] = (o + w, sz - w)
                break
        else:
            raise RuntimeError("arena full allocating %s (%d words); free=%s live=%s" % (name, w, self.free, sorted(self.live)))
        self.live[name] = (o, w)
        ap = self.t[0:shape[0], o:o + w]
        if dt != F32:
            ap = ap.bitcast(dt)
        ap = ap[:, 0:n]
        if len(shape) == 3:
            ap = ap.rearrange("p (a b) -> p a b", a=shape[1])
        elif len(shape) == 4:
            ap = ap.rearrange("p (a b c) -> p a b c", a=shape[1], b=shape[2])
        elif len(shape) == 5:
            ap = ap.rearrange("p (a b c d) -> p a b c d", a=shape[1], b=shape[2], c=shape[3])
        return ap

    def release(self, *names):
        for name in names:
            o, w = self.live.pop(name)
            self.free.append((o, w))
        self.free.sort()
        merged = []
        for o, w in self.free:
            if w == 0:
                continue
            if merged and merged[-1][0] + merged[-1][1] == o:
                merged[-1] = (merged[-1][0], merged[-1][1] + w)
            else:
                merged.append((o, w))
        self.free = merged

def build(debug=(), stop=None):
    nc = bass.Bass("TRN2", target_bir_lowering=False)
    S = Sched(nc)
    top = ExitStack()
    dbg_outs = []

    def din(name, shape):
        return nc.dram_tensor(name, list(shape), F32, kind="ExternalInput").ap()

    x_own = din("x_own", [NT, D])
    x_prev = din("x_prev", [NT, D])
    c_vec = din("c_vec", [128, 16])
    consts = din("consts", [128, 2])
    rope = din("rope", [4, 64, NT])
    ident_d = din("ident", [128, 128])
    w_ada = din("w_ada", [D, 6 * D])
    b_ada = din("b_ada", [6 * D])
    g_norm_mix = din("g_norm_mix", [D])
    w_in = din("w_in", [D, 4160])
    conv_w = din("conv_w", [3, 1024])
    g_q_lat = din("g_q_lat", [512])
    w_uq = din("w_uq", [512, 1536])
    g_kv_lat = din("g_kv_lat", [512])
    w_ukv = din("w_ukv", [512, 2048])
    g_out_conv = din("g_out_conv", [1024])
    g_out_attn = din("g_out_attn", [1024])
    w_out = din("w_out", [D, D])
    g_norm_ffn = din("g_norm_ffn", [D])
    peer_w_q = din("peer_w_q", [D, D])
    peer_keys = din("peer_keys", [16 * 128, 128])
    peer_u = din("peer_u", [NEXP, D])
    peer_v = din("peer_v", [NEXP, D])
    g_final = din("g_final", [D])
    y_out = nc.dram_tensor("y_out", [NT, D], F32, kind="ExternalOutput").ap()
    x2_d = nc.dram_tensor("x2_scratch", [NT, D], F32, kind="Internal").ap()
    sc_d = nc.dram_tensor("sc_scratch", [NT, D], F32, kind="Internal").ap()

    A = Arena(nc, top)

    def fin():
        S.final_wait("sp")
        with nc.allow_non_contiguous_dma(reason="tiny per-partition parameter vectors"):
            S.emit()
        S.close()
        return nc, dbg_outs

    def dbg(name, ap, reads):
        if name in debug:
            o = nc.dram_tensor("dbg_" + name, list(ap.shape), ap.dtype, kind="ExternalOutput").ap()
            S.dma(o, ap, reads=reads, semkey="dbg_" + name)
            dbg_outs.append("dbg_" + name)

    ps = [top.enter_context(nc.psum_tensor("ps%d" % i, [128, 512], F32)) for i in range(4)]
    psO = top.enter_context(nc.psum_tensor("psO", [128, 2048], F32))
    psOb = [psO[:, i * 512:(i + 1) * 512] for i in range(4)]

    def psT(i):
        return ps[i][:].bitcast(BF16)

    ident_f = A.alloc("ident_f", [128, 128])
    ident_b = A.alloc("ident_b", [128, 128], BF16)
    ones_f = A.alloc("ones_f", [128, 128])
    ones_b = A.alloc("ones_b", [128, 128], BF16)
    cst = A.alloc("cst", [128, 2])
    cT = A.alloc("cT", [128, 16, 128])
    cs_ = A.alloc("cs_", [128, 16])
    prm = A.alloc("prm", [128, 64])

    S.dma(ident_f, ident_d, writes=["ident_f"])
    S.dma(cst, consts, writes=["cst"])
    S.dma(cs_, c_vec, writes=["cs_"])
    S.dma(prm[:, 0:24].rearrange("p (k j) -> p k j", k=3), conv_w.rearrange("k (j p) -> p k j", p=128),
          writes=["prm_a"])
    S.dma(prm[:, 24:32], g_out_conv.rearrange("(j p) -> p j", p=128), writes=["prm_b"])
    S.dma(prm[:, 32:40], g_out_attn.rearrange("(j p) -> p j", p=128), writes=["prm_c"])
    S.dma(prm[:, 40:44], g_q_lat.rearrange("(j p) -> p j", p=128), writes=["prm_d"])
    S.dma(prm[:, 44:48], g_kv_lat.rearrange("(j p) -> p j", p=128), writes=["prm_e"])
    PRM = ["prm_a", "prm_b", "prm_c", "prm_d", "prm_e"]
    S.op("act", lambda e: e.copy(out=ident_b, in_=ident_f), reads=["ident_f"], writes=["ident_b"])
    S.op("pool", lambda e: e.memset(ones_f, 1.0), writes=["ones_f"])
    S.op("pool", lambda e: e.memset(ones_b, 1.0), writes=["ones_b"])
    S.op("act", lambda e: e.activation(out=cs_, in_=cs_, func=AF.Silu), reads=["cs_"], writes=["cs_"])
    for kc in range(16):
        S.op("dve", lambda e, kc=kc: e.tensor_copy(out=cT[:, kc, :], in_=cs_[:, kc:kc + 1].to_broadcast([128, 128])),
             reads=["cs_"], writes=["cT"])

    def rstd_from(out_ap, in_ap, n, reads, writes):
        S.op("act", lambda e: e.activation(out=out_ap, in_=in_ap, func=AF.Sqrt, scale=1.0 / n, bias=EPS),
             reads=reads, writes=writes)
        S.op("dve", lambda e: e.reciprocal(out=out_ap, in_=out_ap), reads=writes, writes=writes)

    def compute_mod(col0, ncols, dst, dst_key):
        wa = [A.alloc("wa%d" % i, [128, 16, 256]) for i in range(2)]
        bb = [A.alloc("bb%d" % i, [128, 256]) for i in range(2)]
        for g in range(ncols // 256):
            sl = g % 2
            c0 = col0 + g * 256
            S.dma(wa[sl], w_ada[:, c0:c0 + 256].rearrange("(k p) c -> p k c", p=128), writes=["wa%d" % sl])
            S.dma(bb[sl], b_ada[c0:c0 + 256].partition_broadcast(128), writes=["bb%d" % sl])
            pb = ps[g % 2]
            for kc in range(16):
                S.op("pe", lambda e, kc=kc, sl=sl, pb=pb: e.matmul(pb[:, 0:256], cT[:, kc, :], wa[sl][:, kc, :],
                                                                     start=(kc == 0), stop=(kc == 15)),
                     reads=["cT", "wa%d" % sl], writes=["ps%d" % (g % 2)])
            S.op("dve", lambda e, g=g, sl=sl, pb=pb: e.tensor_tensor(out=dst[:, g * 256:(g + 1) * 256], in0=pb[:, 0:256],
                                                                      in1=bb[sl], op=ALU.add),
                 reads=["ps%d" % (g % 2), "bb%d" % sl], writes=[dst_key])
        S.barrier()
        A.release("wa0", "wa1", "bb0", "bb1")

    def norm_tile(xt, xkey, hb, hbkey, gsc, sh, ss, gkeys):
        S.op("act", lambda e: e.activation(out=hb, in_=xt, func=AF.Square, accum_out=ss[:, 0:1]),
             reads=[xkey], writes=[hbkey, "ss"])
        rstd_from(ss[:, 0:1], ss[:, 0:1], D, ["ss"], ["ss"])
        S.op("dve", lambda e: e.scalar_tensor_tensor(out=xt, in0=xt, scalar=ss[:, 0:1], in1=gsc, op0=ALU.mult,
                                                      op1=ALU.mult), reads=[xkey, "ss"] + gkeys, writes=[xkey])
        S.op("pool", lambda e: e.tensor_tensor(out=hb, in0=xt, in1=sh, op=ALU.add), reads=[xkey] + gkeys,
             writes=[hbkey])

    def transpose_tile(hb, hbkey, dstT, dkey, col0):
        for half in range(2):
            bank = 2 + half
            pt = psT(bank)
            for k in range(8):
                kc = half * 8 + k
                S.op("pe", lambda e, kc=kc, k=k, pt=pt: e.transpose(out=pt[:, k * 128:(k + 1) * 128],
                                                                     in_=hb[:, kc * 128:(kc + 1) * 128], identity=ident_b),
                     reads=[hbkey, "ident_b"], writes=["ps%d" % bank])
            S.op("act", lambda e, half=half, pt=pt: e.copy(out=dstT[:, half * 8:(half + 1) * 8, col0:col0 + 128],
                                                            in_=pt[:, 0:1024].rearrange("p (k t) -> p k t", k=8)),
                 reads=["ps%d" % bank], writes=[dkey])

    gsc = A.alloc("gsc", [128, D])
    shm = A.alloc("shm", [128, D])
    compute_mod(0, D, shm, "shm")
    compute_mod(D, D, gsc, "gsc")
    gnm = A.alloc("gnm", [128, D])
    S.dma(gnm, g_norm_mix.partition_broadcast(128), writes=["gnm"])
    S.op("dve", lambda e: e.scalar_tensor_tensor(out=gsc, in0=gsc, scalar=1.0, in1=gnm, op0=ALU.add,
                                                  op1=ALU.mult), reads=["gsc", "gnm"], writes=["gsc"])
    S.barrier()
    A.release("gnm")
    dbg("shm", shm[0:1, :], ["shm"])
    dbg("gsc", gsc[0:1, :], ["gsc"])
    dbg("prm", prm, PRM)
    if stop == "mod":
        return fin()

    hT = A.alloc("hT", [128, 16, NT], BF16)
    hTh = A.alloc("hTh", [128, 16, 2], BF16)
    kvn = A.alloc("kvn", [128, 4, 2 * NT], BF16, hi=True)
    qn = A.alloc("qn", [128, 4, NT], BF16, hi=True)
    krope = A.alloc("krope", [64, 2 * NT], BF16, hi=True)
    rkv_bc = A.alloc("rkv_bc", [128, 2 * NT], hi=True)
    rq_bc = A.alloc("rq_bc", [128, NT], hi=True)
    ropet = A.alloc("ropet", [64, 2, NT])
    xts = [A.alloc("xt%d" % i, [128, D]) for i in range(2)]
    hb = A.alloc("hb", [128, D], BF16)
    ss = A.alloc("ss", [128, 2])
    wraw = A.alloc("wraw", [128, 8, 3, 128])
    wbf = [A.alloc("wbf%d" % i, [128, 16, 3, 128], BF16) for i in range(2)]
    sqb = A.alloc("sqb", [128, NT])

    wctr = [0]

    def load_w(cols_list, width, swap=False):
        sl = wctr[0] % 2
        wctr[0] += 1
        n = len(cols_list)
        for kh in range(2):
            for i, c0 in enumerate(cols_list):
                S.dma(wraw[:, :, i, 0:width],
                      w_in[kh * 1024:(kh + 1) * 1024, c0:c0 + width].rearrange("(k p) c -> p k c", p=128),
                      writes=["wraw%d" % i])
            if not swap:
                S.op("pool", lambda e, kh=kh: e.tensor_copy(out=wbf[sl][:, kh * 8:(kh + 1) * 8, 0:n, 0:width],
                                                            in_=wraw[:, :, 0:n, 0:width]),
                     reads=["wraw%d" % i for i in range(n)], writes=["wbf%d" % sl])
            else:
                S.op("pool", lambda e, kh=kh: e.tensor_copy(out=wbf[sl][:, kh * 8:(kh + 1) * 8, 0, 0:64], in_=wraw[:, :, 0, 0:64]),
                     reads=["wraw0"], writes=["wbf%d" % sl])
                S.op("pool", lambda e, kh=kh: e.tensor_copy(out=wbf[sl][:, kh * 8:(kh + 1) * 8, 1, 0:32], in_=wraw[:, :, 0, 32:64]),
                     reads=["wraw0"], writes=["wbf%d" % sl])
                S.op("pool", lambda e, kh=kh: e.tensor_copy(out=wbf[sl][:, kh * 8:(kh + 1) * 8, 1, 32:64], in_=wraw[:, :, 0, 0:32]),
                     reads=["wraw0"], writes=["wbf%d" % sl])
        return wbf[sl], "wbf%d" % sl

    def lin(pbank, wt, wkey, gi, width, rhs_fn, n):
        for kc in range(16):
            S.op("pe", lambda e, kc=kc: e.matmul(ps[pbank][0:width, 0:n], wt[:, kc, gi, 0:width], rhs_fn(kc),
                                                  start=(kc == 0), stop=(kc == 15)),
                 reads=[wkey, "hT", "hTh"], writes=["ps%d" % pbank])

    def lat_chunk(wt, wkey, dst, dkey, dcol0, gcol, stat_first, stat_last):
        for th in range(2):
            lin(0, wt, wkey, 0, 128, lambda kc, th=th: hT[:, kc, th * 512:(th + 1) * 512], 512)
            S.op("act", lambda e: e.copy(out=sqb[:, 512:1024], in_=ps[0][:, :]), reads=["ps0"], writes=["sqraw"])
            S.op("dve", lambda e, th=th: e.tensor_scalar(out=dst[:, dcol0 + th * 512: dcol0 + (th + 1) * 512],
                                                          in0=sqb[:, 512:1024], scalar1=prm[:, gcol:gcol + 1], scalar2=None,
                                                          op0=ALU.mult), reads=["sqraw"] + PRM, writes=[dkey])
            S.op("pool", lambda e: e.tensor_tensor(out=sqb[:, 0:512], in0=sqb[:, 512:1024], in1=sqb[:, 512:1024], op=ALU.mult),
                 reads=["sqraw"], writes=["sqb"])
            if stop == "kv0c":
                continue
            S.op("pe", lambda e, th=th: e.matmul(ps[2 + th][:, :], ones_f, sqb[:, 0:512], start=stat_first,
                                                  stop=stat_last), reads=["ones_f", "sqb"], writes=["ps%d" % (2 + th)])

    def krope_part(wt, wkey, tok0):
        for th in range(2):
            lin(0, wt, wkey, 0, 64, lambda kc, th=th: hT[:, kc, th * 512:(th + 1) * 512], 512)
            lin(1, wt, wkey, 1, 64, lambda kc, th=th: hT[:, kc, th * 512:(th + 1) * 512], 512)
            S.op("dve", lambda e, th=th: e.tensor_tensor(out=sqb[0:64, 0:512], in0=ps[0][0:64, :],
                                                          in1=ropet[:, 0, th * 512:(th + 1) * 512], op=ALU.mult),
                 reads=["ps0", "ropet"], writes=["sqb"])
            S.op("dve", lambda e, th=th: e.tensor_tensor(out=sqb[0:64, 512:1024], in0=ps[1][0:64, :],
                                                          in1=ropet[:, 1, th * 512:(th + 1) * 512], op=ALU.mult),
                 reads=["ps1", "ropet"], writes=["sqraw"])
            S.op("pool", lambda e, th=th: e.tensor_tensor(out=krope[:, tok0 + th * 512: tok0 + (th + 1) * 512],
                                                           in0=sqb[0:64, 0:512], in1=sqb[0:64, 512:1024], op=ALU.add),
                 reads=["sqb", "sqraw"], writes=["krope"])

    for part in range(2):
        src = x_prev if part == 0 else x_own
        S.dma(ropet, rope[2 * part:2 * part + 2].rearrange("f p t -> p f t"), writes=["ropet"])
        for ti in range(8):
            sl = ti % 2
            S.dma(xts[sl], src[ti * 128:(ti + 1) * 128, :], writes=["xt%d" % sl])
            norm_tile(xts[sl], "xt%d" % sl, hb, "hb", gsc, shm, ss, ["gsc", "shm"])
            transpose_tile(hb, "hb", hT, "hT", ti * 128)
            if stop == "A1a" and ti == 1:
                dbg("hT", hT[:, 3, 0:256], ["hT"])
                return fin()
        if stop == "A1":
            dbg("hT", hT[:, 3, :], ["hT"])
            return fin()
        tok0 = part * NT
        for qc in range(4):
            wt, wkey = load_w([3584 + qc * 128], 128)
            if stop == "kv0a":
                dbg("wbf", wt[:, :, 0, :], [wkey])
                return fin()
            lat_chunk(wt, wkey, kvn[:, qc, :], "kvn", tok0, 44 + qc, qc == 0, qc == 3)
            if stop in ("kv0", "kv0b", "kv0c"):
                dbg("kvn", kvn[:, 0, 0:NT], ["kvn"])
                return fin()
        for th in range(2):
            rstd_from(rkv_bc[:, tok0 + th * 512: tok0 + (th + 1) * 512], ps[2 + th][:, :], 512,
                      ["ps%d" % (2 + th)], ["rkv_bc"])
        wt, wkey = load_w([4096], 64, swap=True)
        krope_part(wt, wkey, tok0)
        if part == 0:
            S.op("pool", lambda e: e.tensor_copy(out=hTh, in_=hT[:, :, NT - 2:NT]), reads=["hT"], writes=["hTh"])
            continue
        for qc in range(4):
            wt, wkey = load_w([3072 + qc * 128], 128)
            lat_chunk(wt, wkey, qn[:, qc, :], "qn", 0, 40 + qc, qc == 0, qc == 3)
        for th in range(2):
            rstd_from(rq_bc[:, th * 512:(th + 1) * 512], ps[2 + th][:, :], 512, ["ps%d" % (2 + th)], ["rq_bc"])
    dbg("kvn", kvn[:, 0, :], ["kvn"])
    dbg("krope", krope, ["krope"])
    dbg("rkv", rkv_bc[0:1, :], ["rkv_bc"])
    dbg("qn", qn[:, 0, :], ["qn"])
    dbg("rq", rq_bc[0:1, :], ["rq_bc"])

    if stop == "A2":
        return fin()
    S.barrier()
    A.release("gsc", "shm", "xt0", "xt1", "hb", "ss")
    merged = A.alloc("merged", [128, 16, NT], BF16, hi=True)
    ctmp = A.alloc("ctmp", [128, 512])
    zbuf = A.alloc("zbuf", [128, NT + 2])
    ybuf = A.alloc("ybuf", [128, NT])
    bbuf = A.alloc("bbuf", [128, NT])
    rsb = A.alloc("rsb", [128, NT])
    xs4 = A.alloc("xs4", [128, 4])
    for j in range(8):
        wt, wkey = load_w([j * 128, 1024 + j * 128, 2048 + j * 128], 128)
        for gi, off in ((1, 0), (2, 2)):
            for kc in range(16):
                S.op("pe", lambda e, kc=kc, gi=gi, off=off, wt=wt: e.matmul(ps[3][:, off:off + 2], wt[:, kc, gi, :],
                                                                             hTh[:, kc, :], start=(kc == 0), stop=(kc == 15)),
                     reads=[wkey, "hTh"], writes=["ps3"])
            if gi == 1:
                S.op("act", lambda e: e.copy(out=xs4[:, 0:2], in_=ps[3][:, 0:2]), reads=["ps3"], writes=["xs4"])
            else:
                S.op("dve", lambda e: e.scalar_tensor_tensor(out=zbuf[:, 0:2], in0=xs4[:, 0:2], scalar=cst[:, 0:1],
                                                              in1=ps[3][:, 2:4], op0=ALU.mult, op1=ALU.mult),
                     reads=["xs4", "cst", "ps3"], writes=["zbuf"])
        for th in range(2):
            rf = lambda kc, th=th: hT[:, kc, th * 512:(th + 1) * 512]
            lin(0, wt, wkey, 1, 128, rf, 512)
            lin(1, wt, wkey, 2, 128, rf, 512)
            lin(2, wt, wkey, 0, 128, rf, 512)
            S.op("act", lambda e: e.copy(out=ctmp, in_=ps[0][:, :]), reads=["ps0"], writes=["ctmp"])
            S.op("dve", lambda e, th=th: e.tensor_tensor(out=zbuf[:, 2 + th * 512: 2 + (th + 1) * 512], in0=ctmp,
                                                          in1=ps[1][:, :], op=ALU.mult),
                 reads=["ctmp", "ps1"], writes=["zbuf"])
            S.op("act", lambda e, th=th: e.copy(out=bbuf[:, th * 512:(th + 1) * 512], in_=ps[2][:, :]),
                 reads=["ps2"], writes=["bbuf"])
        S.op("dve", lambda e, j=j: e.tensor_scalar(out=ybuf, in0=zbuf[:, 0:NT], scalar1=prm[:, j:j + 1],
                                                    scalar2=None, op0=ALU.mult), reads=["zbuf"] + PRM, writes=["ybuf"])
        for k in (1, 2):
            S.op("dve", lambda e, j=j, k=k: e.scalar_tensor_tensor(out=ybuf, in0=zbuf[:, k:NT + k],
                                                                    scalar=prm[:, 8 * k + j: 8 * k + j + 1], in1=ybuf,
                                                                    op0=ALU.mult, op1=ALU.add),
                 reads=["zbuf", "ybuf"] + PRM, writes=["ybuf"])
        S.op("pool", lambda e: e.tensor_tensor(out=ybuf, in0=ybuf, in1=bbuf, op=ALU.mult),
             reads=["ybuf", "bbuf"], writes=["ybuf"])
        S.op("act", lambda e: e.activation(out=sqb, in_=ybuf, func=AF.Square), reads=["ybuf"], writes=["sqb"])
        for th in range(2):
            S.op("pe", lambda e, th=th: e.matmul(ps[3][:, :], ones_f, sqb[:, th * 512:(th + 1) * 512], start=True,
                                                  stop=True), reads=["ones_f", "sqb"], writes=["ps3"])
            rstd_from(rsb[:, th * 512:(th + 1) * 512], ps[3][:, :], 128, ["ps3"], ["rsb"])
        S.op("dve", lambda e, j=j: e.scalar_tensor_tensor(out=merged[:, j, :], in0=ybuf, scalar=prm[:, 24 + j:25 + j],
                                                           in1=rsb, op0=ALU.mult, op1=ALU.mult),
             reads=["ybuf", "rsb"] + PRM, writes=["merged"])
    dbg("mconv", merged[:, 0, :], ["merged"])

    if stop == "conv":
        return fin()
    S.barrier()
    A.release("hT", "hTh", "wraw", "wbf0", "wbf1", "sqb", "ctmp", "zbuf", "ybuf", "bbuf", "rsb", "xs4")
    qTn = A.alloc("qTn", [128, 8, NT], BF16, hi=True)
    qTr = A.alloc("qTr", [64, 8, NT], BF16, hi=True)
    wq = A.alloc("wq", [128, 4, 1536], BF16)
    wqr = A.alloc("wqr", [128, 4, 8, 64], BF16)
    wst = A.alloc("wst", [128, 2048])
    cq = A.alloc("cq", [64, 2, NT])
    rtmp = A.alloc("rtmp", [64, 1024])
    for kc in range(4):
        S.dma(wst[:, 0:1536], w_uq[kc * 128:(kc + 1) * 128, :], writes=["wst"])
        S.op("pool", lambda e, kc=kc: e.tensor_copy(out=wq[:, kc, :], in_=wst[:, 0:1536]), reads=["wst"], writes=["wq"])
    wq4 = wq.rearrange("p k (h c) -> p k h c", h=8)
    S.op("pool", lambda e: e.tensor_copy(out=wqr[:, :, :, 0:32], in_=wq4[:, :, :, 160:192]), reads=["wq"], writes=["wqr"])
    S.op("pool", lambda e: e.tensor_copy(out=wqr[:, :, :, 32:64], in_=wq4[:, :, :, 128:160]), reads=["wq"], writes=["wqr"])
    for f in range(2):
        S.op("dve", lambda e, f=f: e.tensor_tensor(out=cq[:, f, :], in0=ropet[:, f, :], in1=rq_bc[0:64, :], op=ALU.mult),
             reads=["ropet", "rq_bc"], writes=["cq"])
    dbg("rq2", rq_bc[0:1, :], ["rq_bc"])
    dbg("wq", wq[:, 0, 0:192], ["wq"])
    dbg("wq3", wq[:, 3, 0:192], ["wq"])
    dbg("wst", wst[:, 0:192], ["wst"])
    if stop == "A3q0":
        return fin()
    for h in range(8):
        for th in range(2):
            tsl = slice(th * 512, (th + 1) * 512)
            for kc in range(4):
                S.op("pe", lambda e, kc=kc, h=h, tsl=tsl: e.matmul(ps[0][:, :], wq[:, kc, h * 192:h * 192 + 128], qn[:, kc, tsl],
                                                                   start=(kc == 0), stop=(kc == 3)),
                     reads=["wq", "qn"], writes=["ps0"])
            S.op("dve", lambda e, h=h, tsl=tsl: e.tensor_tensor(out=qTn[:, h, tsl], in0=ps[0][:, :], in1=rq_bc[:, tsl], op=ALU.mult),
                 reads=["ps0", "rq_bc"], writes=["qTn"])
            for kc in range(4):
                S.op("pe", lambda e, kc=kc, h=h, tsl=tsl: e.matmul(ps[1][0:64, :], wq[:, kc, h * 192 + 128:h * 192 + 192], qn[:, kc, tsl],
                                                                   start=(kc == 0), stop=(kc == 3)),
                     reads=["wq", "qn"], writes=["ps1"])
            for kc in range(4):
                S.op("pe", lambda e, kc=kc, h=h, tsl=tsl: e.matmul(ps[2][0:64, :], wqr[:, kc, h, :], qn[:, kc, tsl],
                                                                   start=(kc == 0), stop=(kc == 3)),
                     reads=["wqr", "qn"], writes=["ps2"])
            S.op("dve", lambda e, tsl=tsl: e.tensor_tensor(out=rtmp[:, 0:512], in0=ps[1][0:64, :], in1=cq[:, 0, tsl], op=ALU.mult),
                 reads=["ps1", "cq"], writes=["rtmp"])
            S.op("dve", lambda e, tsl=tsl: e.tensor_tensor(out=rtmp[:, 512:1024], in0=ps[2][0:64, :], in1=cq[:, 1, tsl], op=ALU.mult),
                 reads=["ps2", "cq"], writes=["rtmp"])
            S.op("pool", lambda e, h=h, tsl=tsl: e.tensor_tensor(out=qTr[:, h, tsl], in0=rtmp[:, 0:512], in1=rtmp[:, 512:1024], op=ALU.add),
                 reads=["rtmp"], writes=["qTr"])
        if stop == "A3q1":
            dbg("qTn", qTn[:, 0, :], ["qTn"])
            dbg("wqb", wq[:, 0, 0:192], ["wq"])
            dbg("qn2", qn[:, 0, :], ["qn"])
            return fin()
    dbg("qTn", qTn[:, 0, :], ["qTn"])
    dbg("qTr", qTr[:, 0, :], ["qTr"])

    S.barrier()
    A.release("qn", "rq_bc", "ropet", "wq", "wqr", "cq", "rtmp", "wst")
    kTn = A.alloc("kTn", [128, 8, 2 * NT], BF16, hi=True)
    vtm = A.alloc("vtm", [128, 16, 1024], BF16)
    rkv_tm = A.alloc("rkv_tm", [128, 16])
    wkv = A.alloc("wkv", [128, 4, 2048], BF16)
    wst_kv = A.alloc("wst_kv", [128, 2048])
    for kc in range(4):
        S.dma(wst_kv, w_ukv[kc * 128:(kc + 1) * 128, :], writes=["wst_kv"])
        S.op("pool", lambda e, kc=kc: e.tensor_copy(out=wkv[:, kc, :], in_=wst_kv), reads=["wst_kv"], writes=["wkv"])
    for blk in range(16):
        S.op("pe", lambda e, blk=blk: e.transpose(out=ps[3][:, 0:128], in_=rkv_bc[:, blk * 128:(blk + 1) * 128],
                                                   identity=ident_f), reads=["rkv_bc", "ident_f"], writes=["ps3"])
        S.op("act", lambda e, blk=blk: e.copy(out=rkv_tm[:, blk:blk + 1], in_=ps[3][:, 0:1]), reads=["ps3"], writes=["rkv_tm"])
    for h in range(8):
        for tc in range(4):
            tsl = slice(tc * 512, (tc + 1) * 512)
            pb = tc % 2
            for kc in range(4):
                S.op("pe", lambda e, kc=kc, h=h, tsl=tsl, pb=pb: e.matmul(ps[pb][:, :], wkv[:, kc, h * 256:h * 256 + 128], kvn[:, kc, tsl],
                                                                          start=(kc == 0), stop=(kc == 3)),
                     reads=["wkv", "kvn"], writes=["ps%d" % pb])
            S.op("dve", lambda e, h=h, tsl=tsl, pb=pb: e.tensor_tensor(out=kTn[:, h, tsl], in0=ps[pb][:, :], in1=rkv_bc[:, tsl], op=ALU.mult),
                 reads=["ps%d" % pb, "rkv_bc"], writes=["kTn"])
    wkv4 = wkv.rearrange("p k (h c) -> p k h c", h=8)
    for blk in range(16):
        for hg in range(2):
            pb = hg
            for kc in range(4):
                S.op("pe", lambda e, kc=kc, blk=blk, hg=hg, pb=pb: e.matmul(
                    ps[pb][:, :].rearrange("p (h c) -> p h c", h=4), kvn[:, kc, blk * 128:(blk + 1) * 128],
                    wkv4[:, kc, hg * 4:(hg + 1) * 4, 128:256], start=(kc == 0), stop=(kc == 3)),
                    reads=["wkv", "kvn"], writes=["ps%d" % pb])
            S.op("act", lambda e, blk=blk, hg=hg, pb=pb: e.activation(out=vtm[:, blk, hg * 512:(hg + 1) * 512], in_=ps[pb][:, :],
                                                                      func=AF.Copy, scale=rkv_tm[:, blk:blk + 1]),
                 reads=["ps%d" % pb, "rkv_tm"], writes=["vtm"])
    dbg("kTn", kTn[:, 0, :], ["kTn"])
    dbg("vtm", vtm[:, 0, :], ["vtm"])

    if stop == "A3":
        return fin()
    S.barrier()
    A.release("kvn", "rkv_bc", "wkv", "wst_kv", "rkv_tm")
    pTb = [A.alloc("pT%d" % i, [128, NT], BF16) for i in range(2)]
    oT = A.alloc("oT", [128, NT])
    rz = A.alloc("rz", [128, NT])
    sq4 = A.alloc("sq4", [128, NT])
    it = 0
    last_j = {0: 11, 1: 15}
    for h in range(8):
        for j in range(16):
            own = j >= 8
            jj = j - 8
            sl = it % 2
            it += 1
            ranges = []
            for th in range(2):
                lo = max(jj * 128, th * 512) if own else th * 512
                hi = (th + 1) * 512
                if lo < hi:
                    ranges.append((th, lo, hi))
            for (th, lo, hi) in ranges:
                n = hi - lo
                S.op("pe", lambda e, h=h, j=j, lo=lo, hi=hi, n=n, th=th: e.matmul(ps[th][:, 0:n], kTn[:, h, j * 128:(j + 1) * 128],
                                                                                 qTn[:, h, lo:hi], start=True, stop=False),
                     reads=["kTn", "qTn"], writes=["ps%d" % th])
                S.op("pe", lambda e, h=h, j=j, lo=lo, hi=hi, n=n, th=th: e.matmul(ps[th][:, 0:n], krope[:, j * 128:(j + 1) * 128],
                                                                                 qTr[:, h, lo:hi], start=False, stop=True),
                     reads=["krope", "qTr"], writes=["ps%d" % th])
                if own:
                    S.op("act", lambda e, lo=lo, hi=hi, n=n, th=th, sl=sl: e.activation(out=pTb[sl][:, lo:hi], in_=ps[th][:, 0:n],
                                                                                       func=AF.Exp, scale=SCALE),
                         reads=["ps%d" % th], writes=["pT%d" % sl])
                else:
                    S.op("act", lambda e, lo=lo, hi=hi, n=n, th=th, sl=sl: e.activation(out=pTb[sl][:, lo:hi], in_=ps[th][:, 0:n],
                                                                                       func=AF.Exp, scale=SCALE, bias=cst[:, 1:2]),
                         reads=["ps%d" % th, "cst"], writes=["pT%d" % sl])
            if own:
                S.op("pool", lambda e, jj=jj, sl=sl: e.memset(pTb[sl][64:128, jj * 128: jj * 128 + 64], 0.0),
                     reads=[], writes=["pT%d" % sl])
            for (th, lo, hi) in ranges:
                o0 = lo - th * 512
                n = hi - lo
                S.op("pe", lambda e, h=h, j=j, lo=lo, hi=hi, th=th, o0=o0, n=n, sl=sl: e.matmul(
                    psOb[th][:, o0:o0 + n], vtm[:, j, h * 128:(h + 1) * 128], pTb[sl][:, lo:hi],
                    start=(j == 0), stop=(j == last_j[th])), reads=["vtm", "pT%d" % sl], writes=["psO%d" % th])
                S.op("pe", lambda e, j=j, lo=lo, hi=hi, th=th, o0=o0, n=n, sl=sl: e.matmul(
                    psOb[2 + th][:, o0:o0 + n], ones_b, pTb[sl][:, lo:hi],
                    start=(j == 0), stop=(j == last_j[th])), reads=["ones_b", "pT%d" % sl], writes=["psO%d" % (2 + th)])
        for th in range(2):
            tsl = slice(th * 512, (th + 1) * 512)
            S.op("dve", lambda e, th=th, tsl=tsl: e.reciprocal(out=rz[:, tsl], in_=psOb[2 + th]), reads=["psO%d" % (2 + th)],
                 writes=["rz"])
            S.op("dve", lambda e, th=th, tsl=tsl: e.tensor_tensor(out=oT[:, tsl], in0=psOb[th], in1=rz[:, tsl], op=ALU.mult),
                 reads=["psO%d" % th, "rz"], writes=["oT"])
        S.op("act", lambda e: e.activation(out=sq4, in_=oT, func=AF.Square), reads=["oT"], writes=["sq4"])
        for th in range(2):
            tsl = slice(th * 512, (th + 1) * 512)
            S.op("pe", lambda e, tsl=tsl: e.matmul(ps[2][:, :], ones_f, sq4[:, tsl], start=True, stop=True),
                 reads=["ones_f", "sq4"], writes=["ps2"])
            rstd_from(rz[:, tsl], ps[2][:, :], 128, ["ps2"], ["rz"])
        S.op("dve", lambda e, h=h: e.scalar_tensor_tensor(out=merged[:, 8 + h, :], in0=oT, scalar=prm[:, 32 + h:33 + h],
                                                           in1=rz, op0=ALU.mult, op1=ALU.mult),
             reads=["oT", "rz"] + PRM, writes=["merged"])
    dbg("mattn", merged[:, 8, :], ["merged"])

    if stop == "A4":
        return fin()
    S.barrier()
    A.release("qTn", "qTr", "kTn", "vtm", "krope", "pT0", "pT1", "oT", "rz", "sq4")
    gtm = A.alloc("gtm", [128, D])
    compute_mod(2 * D, D, gtm, "gtm")
    wout = A.alloc("wout", [128, 16, D], BF16)
    wst_o = [A.alloc("wsto_%d" % i, [128, D]) for i in range(2)]
    xts_o = [A.alloc("xto%d" % i, [128, D]) for i in range(2)]
    x2t = [A.alloc("x2t%d" % i, [128, D]) for i in range(2)]
    for kc in range(16):
        sl = kc % 2
        S.dma(wst_o[sl], w_out[kc * 128:(kc + 1) * 128, :], writes=["wsto_%d" % sl])
        S.op("pool", lambda e, kc=kc, sl=sl: e.tensor_copy(out=wout[:, kc, :], in_=wst_o[sl]), reads=["wsto_%d" % sl], writes=["wout"])
    for ti in range(8):
        sl = ti % 2
        S.dma(xts_o[sl], x_own[ti * 128:(ti + 1) * 128, :], writes=["xto%d" % sl])
        for nq in range(4):
            pb = nq % 2
            for kc in range(16):
                S.op("pe", lambda e, kc=kc, ti=ti, nq=nq, pb=pb: e.matmul(ps[pb][:, :], merged[:, kc, ti * 128:(ti + 1) * 128],
                                                                          wout[:, kc, nq * 512:(nq + 1) * 512], start=(kc == 0), stop=(kc == 15)),
                     reads=["merged", "wout"], writes=["ps%d" % pb])
            S.op("dve", lambda e, nq=nq, pb=pb, sl=sl: e.tensor_tensor(out=x2t[sl][:, nq * 512:(nq + 1) * 512], in0=ps[pb][:, :],
                                                                       in1=gtm[:, nq * 512:(nq + 1) * 512], op=ALU.mult),
                 reads=["ps%d" % pb, "gtm"], writes=["x2t%d" % sl])
        S.op("pool", lambda e, sl=sl: e.tensor_tensor(out=x2t[sl], in0=x2t[sl], in1=xts_o[sl], op=ALU.add),
             reads=["x2t%d" % sl, "xto%d" % sl], writes=["x2t%d" % sl])
        S.dma(x2_d[ti * 128:(ti + 1) * 128, :], x2t[sl], reads=["x2t%d" % sl], writes=["x2_d%d" % ti], semkey="x2st%d" % sl)
    dbg("x2t", x2t[1], ["x2t1"])
    if stop == "A5":
        return fin()
    S.barrier()
    A.release("gtm", "wout", "wsto_0", "wsto_1", "xto0", "xto1", "x2t0", "x2t1", "merged")
    gscf = A.alloc("gscf", [128, D])
    shf = A.alloc("shf", [128, D])
    gtf = A.alloc("gtf", [128, D], hi=True)
    compute_mod(3 * D, D, shf, "shf")
    compute_mod(4 * D, D, gscf, "gscf")
    compute_mod(5 * D, D, gtf, "gtf")
    gnf = A.alloc("gnf", [128, D])
    S.dma(gnf, g_norm_ffn.partition_broadcast(128), writes=["gnf"])
    S.op("dve", lambda e: e.scalar_tensor_tensor(out=gscf, in0=gscf, scalar=1.0, in1=gnf, op0=ALU.add,
                                                  op1=ALU.mult), reads=["gscf", "gnf"], writes=["gscf"])
    S.barrier()
    A.release("gnf")
    h2T = A.alloc("h2T", [128, 16, NT], BF16, hi=True)
    thr = A.alloc("thr", [128, 8, 8], hi=True)
    nb = A.alloc("nb", [128, 8, 8], hi=True)
    wqp = A.alloc("wqp", [128, 16, D], BF16)
    wst_p = [A.alloc("wstp_%d" % i, [128, D]) for i in range(2)]
    keysT = A.alloc("keysT", [128, 16, 128], BF16)
    kst = A.alloc("kst", [128, 16, 128], BF16)
    xts_p = [A.alloc("xtp%d" % i, [128, D]) for i in range(2)]
    hb_p = A.alloc("hb_p", [128, D], BF16)
    ss_p = A.alloc("ss_p", [128, 2])
    qTt = A.alloc("qTt", [128, 16, 128], BF16)
    sct = [A.alloc("sct0", [128, 8, 2, 128])] * 2
    tv = A.alloc("tv", [128, 8, 2, 16])
    wk = A.alloc("wk", [128, 256])
    cand = A.alloc("cand", [128, 8, 16, 16])
    bv = A.alloc("bv", [128, 8, 16])
    ez = A.alloc("ez", [128, 16])
    zz = A.alloc("zz", [128, 8])
    nmx = A.alloc("nmx", [128, 8])
    for kc in range(16):
        sl = kc % 2
        S.dma(wst_p[sl], peer_w_q[kc * 128:(kc + 1) * 128, :], writes=["wstp_%d" % sl])
        S.op("pool", lambda e, kc=kc, sl=sl: e.tensor_copy(out=wqp[:, kc, :], in_=wst_p[sl]), reads=["wstp_%d" % sl], writes=["wqp"])
    S.barrier()
    S.dma(wst_p[0].rearrange("p (a b) -> p a b", a=16), peer_keys.rearrange("(a n) d -> n a d", n=128), writes=["wstp_0"])
    S.op("pool", lambda e: e.tensor_copy(out=kst, in_=wst_p[0].rearrange("p (a b) -> p a b", a=16)), reads=["wstp_0"], writes=["kst"])
    for half in range(2):
        pt = psT(2 + half)
        for k in range(8):
            S.op("pe", lambda e, k=k, half=half, pt=pt: e.transpose(out=pt[:, k * 128:(k + 1) * 128], in_=kst[:, half * 8 + k, :],
                                                                     identity=ident_b), reads=["kst", "ident_b"], writes=["ps%d" % (2 + half)])
        S.op("act", lambda e, half=half, pt=pt: e.copy(out=keysT[:, half * 8:(half + 1) * 8, :],
                                                        in_=pt[:, 0:1024].rearrange("p (k t) -> p k t", k=8)),
             reads=["ps%d" % (2 + half)], writes=["keysT"])
    for ti in range(8):
        sl = ti % 2
        S.dma(xts_p[sl], x2_d[ti * 128:(ti + 1) * 128, :], reads=["x2_d%d" % ti], writes=["xtp%d" % sl])
        norm_tile(xts_p[sl], "xtp%d" % sl, hb_p, "hb_p", gscf, shf, ss_p, ["gscf", "shf"])
        transpose_tile(hb_p, "hb_p", h2T, "h2T", ti * 128)
        for hp in range(16):
            pb = hp % 2
            for kc in range(16):
                S.op("pe", lambda e, kc=kc, hp=hp, ti=ti, pb=pb: e.matmul(ps[pb][:, 0:128], wqp[:, kc, hp * 128:(hp + 1) * 128],
                                                                          h2T[:, kc, ti * 128:(ti + 1) * 128], start=(kc == 0), stop=(kc == 15)),
                     reads=["wqp", "h2T"], writes=["ps%d" % pb])
            S.op("act", lambda e, hp=hp, pb=pb: e.copy(out=qTt[:, hp, :], in_=ps[pb][:, 0:128]), reads=["ps%d" % pb], writes=["qTt"])
        for hp in range(16):
            S.op("pe", lambda e, hp=hp: e.matmul(psO[:, hp * 128:(hp + 1) * 128], qTt[:, hp, :], keysT[:, hp, :], start=True, stop=True),
                 reads=["qTt", "keysT"], writes=["psO"])
        sc = sct[sl]
        S.op("act", lambda e, sc=sc: e.copy(out=sc.rearrange("p h s n -> p (h s n)"), in_=psO[:, :]), reads=["psO"], writes=["sct0"])
        S.dma(sc_d[ti * 128:(ti + 1) * 128, :], sc.rearrange("p h s n -> p (h s n)"), reads=["sct0"], writes=["sc_d%d" % ti],
              semkey="scst%d" % sl)
        for h in range(8):
            for p_ in range(2):
                S.op("dve", lambda e, h=h, p_=p_, sc=sc: e.max(out=tv[:, h, p_, 0:8], in_=sc[:, h, p_, :]), reads=["sct0"], writes=["tv"])
                S.op("dve", lambda e, h=h, p_=p_, sc=sc: e.match_replace(out=wk[:, 0:128], in_to_replace=tv[:, h, p_, 0:8],
                                                                         in_values=sc[:, h, p_, :], imm_value=-1e30),
                     reads=["sct0", "tv"], writes=["wk"])
                S.op("dve", lambda e, h=h, p_=p_: e.max(out=tv[:, h, p_, 8:16], in_=wk[:, 0:128]), reads=["wk"], writes=["tv"])
        S.op("dve", lambda e: e.tensor_tensor(out=cand, in0=tv[:, :, 0, :].unsqueeze(3).to_broadcast([128, 8, 16, 16]),
                                              in1=tv[:, :, 1, :].unsqueeze(2).to_broadcast([128, 8, 16, 16]), op=ALU.add),
             reads=["tv"], writes=["cand"])
        for h in range(8):
            ch = cand[:, h, :, :].rearrange("p a b -> p (a b)")
            S.op("dve", lambda e, h=h, ch=ch: e.max(out=bv[:, h, 0:8], in_=ch), reads=["cand"], writes=["bv"])
            S.op("dve", lambda e, h=h, ch=ch: e.match_replace(out=wk, in_to_replace=bv[:, h, 0:8], in_values=ch, imm_value=-1e30),
                 reads=["cand", "bv"], writes=["wk"])
            S.op("dve", lambda e, h=h: e.max(out=bv[:, h, 8:16], in_=wk), reads=["wk"], writes=["bv"])
        S.op("dve", lambda e, ti=ti: e.tensor_copy(out=thr[:, ti, :], in_=bv[:, :, 15]), reads=["bv"], writes=["thr"])
        S.op("dve", lambda e: e.tensor_scalar(out=nmx, in0=bv[:, :, 0], scalar1=-1.0, scalar2=None, op0=ALU.mult),
             reads=["bv"], writes=["nmx"])
        for h in range(8):
            S.op("act", lambda e, h=h: e.activation(out=ez, in_=bv[:, h, :], func=AF.Exp, bias=nmx[:, h:h + 1],
                                                    accum_out=zz[:, h:h + 1]), reads=["bv", "nmx"], writes=["ez", "zz"])
        S.op("act", lambda e: e.activation(out=zz, in_=zz, func=AF.Ln), reads=["zz"], writes=["zz"])
        S.op("dve", lambda e, ti=ti: e.tensor_tensor(out=nb[:, ti, :], in0=nmx, in1=zz, op=ALU.subtract), reads=["nmx", "zz"], writes=["nb"])
    dbg("thr", thr, ["thr"])
    dbg("nb", nb, ["nb"])
    dbg("sct", sct[1].rearrange("p h s n -> p (h s n)"), ["sct0"])

    if stop == "B0":
        return fin()
    S.barrier()
    A.release("wqp", "wstp_0", "wstp_1", "keysT", "kst", "xtp0", "xtp1", "hb_p", "ss_p", "qTt", "sct0", "tv", "wk", "cand",
              "bv", "ez", "zz", "nmx", "gscf", "shf", "cT", "cs_")
    NE = GE // 128
    acc = A.alloc("acc", [128, TP, D])
    sc4 = A.alloc("sc4", [128, TP, 8, 2, 128])
    uraw = [A.alloc("uraw%d" % i, [128, D]) for i in range(2)]
    ubf = [A.alloc("ubf%d" % i, [128, D], BF16) for i in range(2)]
    UT = A.alloc("UT", [128, 16, GE], BF16)
    vraw = [A.alloc("vraw%d" % i, [128, D]) for i in range(2)]
    Vg = A.alloc("Vg", [128, NE, D], BF16)
    Sg = [A.alloc("Sg%d" % i, [128, NE, 128]) for i in range(2)]
    Eg = [A.alloc("Eg%d" % i, [128, NE, 128], BF16) for i in range(2)]
    Tg = [A.alloc("Tg%d" % i, [128, NE, 128], BF16) for i in range(2)]
    Gg = [A.alloc("Gg%d" % i, [128, GE]) for i in range(2)]
    ga = A.alloc("ga", [128, GE], BF16)
    actv = A.alloc("actv", [128, GE], BF16)
    actT = A.alloc("actT", [128, NE, 128], BF16)
    ss_f = A.alloc("ss_f", [128, 2])
    gfin = vraw[0]
    def prep_group(g):
        for e_ in range(NE):
            sl = (g * NE + e_) % 2
            r0 = (g * NE + e_) * 128
            S.dma(uraw[sl], peer_u[r0:r0 + 128, :], writes=["uraw%d" % sl])
            S.dma(vraw[sl], peer_v[r0:r0 + 128, :], writes=["vraw%d" % sl])
            S.op("act", lambda e, sl=sl: e.copy(out=ubf[sl], in_=uraw[sl]), reads=["uraw%d" % sl], writes=["ubf%d" % sl])
            veng = "act" if e_ % 2 == 0 else "pool"
            if veng == "act":
                S.op("act", lambda e, sl=sl, e_=e_: e.copy(out=Vg[:, e_, :], in_=vraw[sl]), reads=["vraw%d" % sl], writes=["Vg"])
            else:
                S.op("pool", lambda e, sl=sl, e_=e_: e.tensor_copy(out=Vg[:, e_, :], in_=vraw[sl]), reads=["vraw%d" % sl], writes=["Vg"])
            for half in range(2):
                pt = psT(2 + half)
                for k in range(8):
                    S.op("pe", lambda e, k=k, half=half, pt=pt, sl=sl: e.transpose(out=pt[:, k * 128:(k + 1) * 128],
                                                                                  in_=ubf[sl][:, (half * 8 + k) * 128:(half * 8 + k + 1) * 128],
                                                                                  identity=ident_b),
                         reads=["ubf%d" % sl, "ident_b"], writes=["ps%d" % (2 + half)])
                S.op("act", lambda e, half=half, pt=pt, e_=e_: e.copy(out=UT[:, half * 8:(half + 1) * 8, e_ * 128:(e_ + 1) * 128],
                                                                      in_=pt[:, 0:1024].rearrange("p (k t) -> p k t", k=8)),
                     reads=["ps%d" % (2 + half)], writes=["UT"])

    def stage1(k, p, g, tt):
        ti = p * TP + tt
        b = k % 2
        c0 = g * NE
        for kc in range(16):
            S.op("pe", lambda e, kc=kc: e.matmul(ps[b][:, 0:GE], h2T[:, kc, ti * 128:(ti + 1) * 128], UT[:, kc, :],
                                                  start=(kc == 0), stop=(kc == 15)),
                 reads=["h2T", "UT"], writes=["ps%d" % b])
        for h in range(8):
            q = h % 2
            S.op("dve", lambda e, h=h, q=q: e.tensor_tensor(
                out=Sg[q], in0=sc4[:, tt, h, 0, c0:c0 + NE].unsqueeze(2).to_broadcast([128, NE, 128]),
                in1=sc4[:, tt, h, 1, :].unsqueeze(1).to_broadcast([128, NE, 128]), op=ALU.add),
                reads=["sc4"], writes=["Sg%d" % q])
            S.op("act", lambda e, h=h, q=q: e.activation(out=Eg[q], in_=Sg[q], func=AF.Exp, bias=nb[:, ti, h:h + 1]),
                 reads=["Sg%d" % q, "nb"], writes=["Eg%d" % q])
            if h == 0:
                S.op("dve", lambda e, h=h, q=q: e.scalar_tensor_tensor(
                    out=Gg[b].rearrange("p (a n) -> p a n", a=NE), in0=Sg[q], scalar=thr[:, ti, h:h + 1], in1=Eg[q],
                    op0=ALU.is_ge, op1=ALU.mult), reads=["Sg%d" % q, "Eg%d" % q, "thr"], writes=["Gg%d" % b])
            else:
                S.op("dve", lambda e, h=h, q=q: e.scalar_tensor_tensor(
                    out=Tg[q], in0=Sg[q], scalar=thr[:, ti, h:h + 1], in1=Eg[q],
                    op0=ALU.is_ge, op1=ALU.mult), reads=["Sg%d" % q, "Eg%d" % q, "thr"], writes=["Tg%d" % q])
                S.op("pool", lambda e, q=q: e.tensor_tensor(out=Gg[b], in0=Gg[b], in1=Tg[q].rearrange("p a n -> p (a n)"),
                                                             op=ALU.add), reads=["Gg%d" % b, "Tg%d" % q], writes=["Gg%d" % b])
        S.op("act", lambda e: e.activation(out=ga, in_=ps[b][:, 0:GE], func=AF.Gelu), reads=["ps%d" % b], writes=["ga"])
        S.op("dve", lambda e: e.tensor_tensor(out=actv, in0=ga, in1=Gg[b], op=ALU.mult), reads=["ga", "Gg%d" % b], writes=["actv"])

    def stage2a(k):
        b = k % 2
        pt = psT(2 + b)
        for e_ in range(NE):
            S.op("pe", lambda e, e_=e_: e.transpose(out=pt[:, e_ * 128:(e_ + 1) * 128], in_=actv[:, e_ * 128:(e_ + 1) * 128],
                                                     identity=ident_b), reads=["actv", "ident_b"], writes=["ps%d" % (2 + b)])
        S.op("act", lambda e: e.copy(out=actT, in_=pt[:, 0:GE].rearrange("p (k t) -> p k t", k=NE)),
             reads=["ps%d" % (2 + b)], writes=["actT"])

    def stage2b(k, g, tt):
        for nq in range(4):
            for e_ in range(NE):
                S.op("pe", lambda e, e_=e_, nq=nq: e.matmul(psOb[nq], actT[:, e_, :], Vg[:, e_, nq * 512:(nq + 1) * 512],
                                                            start=(e_ == 0), stop=(e_ == NE - 1)),
                     reads=["actT", "Vg"], writes=["psO"])
        if g == 0:
            S.op("dve", lambda e: e.tensor_copy(out=acc[:, tt, :], in_=psO[:, :]), reads=["psO"], writes=["acc%d" % tt])
        else:
            S.op("dve", lambda e: e.tensor_tensor(out=acc[:, tt, :], in0=acc[:, tt, :], in1=psO[:, :], op=ALU.add),
                 reads=["psO", "acc%d" % tt], writes=["acc%d" % tt])

    NG = NEXP // GE
    kk = 0
    for p in range(NPASS):
        for tt in range(TP):
            ti = p * TP + tt
            S.dma(sc4[:, tt].rearrange("p h s n -> p (h s n)"), sc_d[ti * 128:(ti + 1) * 128, :], reads=["sc_d%d" % ti], writes=["sc4"],
                  semkey="sc4_%d" % tt)
        iters = [(g, tt) for g in range(NG) for tt in range(TP)]
        prep_group(0)
        stage1(kk, p, 0, 0)
        for n_, (g, tt) in enumerate(iters):
            nxt = iters[n_ + 1] if n_ + 1 < len(iters) else None
            stage2a(kk)
            if nxt is not None and nxt[0] == g:
                stage1(kk + 1, p, nxt[0], nxt[1])
                stage2b(kk, g, tt)
            else:
                stage2b(kk, g, tt)
                if nxt is not None:
                    prep_group(nxt[0])
                    stage1(kk + 1, p, nxt[0], nxt[1])
            kk += 1
        S.barrier()
        S.dma(gfin, g_final.partition_broadcast(128), writes=["vraw0"])
        for tt in range(TP):
            ti = p * TP + tt
            a_t = acc[:, tt, :]
            xb_ = uraw[tt % 2]
            S.dma(xb_, x2_d[ti * 128:(ti + 1) * 128, :], reads=["x2_d%d" % ti], writes=["uraw%d" % (tt % 2)])
            S.op("dve", lambda e, a_t=a_t: e.tensor_tensor(out=a_t, in0=a_t, in1=gtf, op=ALU.mult),
                 reads=["acc%d" % tt, "gtf"], writes=["acc%d" % tt])
            S.op("pool", lambda e, a_t=a_t, xb_=xb_: e.tensor_tensor(out=a_t, in0=a_t, in1=xb_, op=ALU.add),
                 reads=["acc%d" % tt, "uraw%d" % (tt % 2)], writes=["acc%d" % tt])
            S.op("act", lambda e, a_t=a_t: e.activation(out=ubf[0], in_=a_t, func=AF.Square, accum_out=ss_f[:, 0:1]),
                 reads=["acc%d" % tt], writes=["ubf0", "ss"])
            rstd_from(ss_f[:, 0:1], ss_f[:, 0:1], D, ["ss"], ["ss"])
            S.op("dve", lambda e, a_t=a_t: e.scalar_tensor_tensor(out=a_t, in0=a_t, scalar=ss_f[:, 0:1], in1=gfin, op0=ALU.mult,
                                                                   op1=ALU.mult), reads=["acc%d" % tt, "ss", "vraw0"], writes=["acc%d" % tt])
            S.dma(y_out[ti * 128:(ti + 1) * 128, :], a_t, reads=["acc%d" % tt], semkey="yout%d" % tt)
        S.barrier()
    return fin()


_CACHE = {}


def _host_inputs(x, c, w):
    ident = np.eye(128, dtype=np.float32)
    inv = 1.0 / (10000.0 ** (np.arange(0, 64, 2, dtype=np.float32) / 64.0))
    maps = []
    for core in range(8):
        b, s = core // 2, core % 2
        pos_own = np.arange(s * NT, (s + 1) * NT, dtype=np.float32)
        pos_prev = np.arange(0, NT, dtype=np.float32)
        tabs = []
        for pos in (pos_prev, pos_own):
            ang = inv[:, None] * pos[None, :]
            cs, sn = np.cos(ang).astype(np.float32), np.sin(ang).astype(np.float32)
            tabs.append(np.concatenate([cs, cs], 0))
            tabs.append(np.concatenate([-sn, sn], 0))
        consts = np.zeros((128, 2), np.float32)
        consts[:, 0] = float(s)
        consts[:, 1] = 0.0 if s == 1 else -30000.0
        m = dict(w)
        m["x_own"] = np.ascontiguousarray(x[b, s * NT:(s + 1) * NT])
        m["x_prev"] = np.ascontiguousarray(x[b, 0:NT])
        m["c_vec"] = np.ascontiguousarray(c[b].reshape(16, 128).T)
        m["consts"] = consts
        m["rope"] = np.stack(tabs).astype(np.float32)
        m["ident"] = ident
        maps.append(m)
    return maps


def kernel(x, c, w_ada, b_ada, g_norm_mix, w_in, conv_w, g_q_lat, w_uq, g_kv_lat, w_ukv, g_out_conv, g_out_attn,
           w_out, g_norm_ffn, peer_w_q, peer_sub_keys, peer_u, peer_v, g_final, _debug=(), _stop=None):
    f = lambda a: np.ascontiguousarray(np.asarray(a, dtype=np.float32))
    x, c = f(x), f(c)
    w = {
        "w_ada": f(w_ada)[0], "b_ada": f(b_ada)[0], "g_norm_mix": f(g_norm_mix)[0], "w_in": f(w_in)[0],
        "conv_w": f(conv_w)[0], "g_q_lat": f(g_q_lat)[0], "w_uq": f(w_uq)[0], "g_kv_lat": f(g_kv_lat)[0],
        "w_ukv": f(w_ukv)[0], "g_out_conv": f(g_out_conv)[0], "g_out_attn": f(g_out_attn)[0], "w_out": f(w_out)[0],
        "g_norm_ffn": f(g_norm_ffn)[0], "peer_w_q": f(peer_w_q)[0],
        "peer_keys": f(peer_sub_keys)[0].reshape(16 * 128, 128), "peer_u": f(peer_u)[0], "peer_v": f(peer_v)[0],
        "g_final": f(g_final),
    }
    key = (tuple(_debug), _stop)
    if key not in _CACHE:
        _CACHE[key] = build(debug=_debug, stop=_stop)
    nc, dbg_outs = _CACHE[key]
    maps = _host_inputs(x, c, w)
    res = run_bass_kernel_spmd(nc, maps, core_ids=list(range(8)))
    out = np.empty((4, SEQ, D), np.float32)
    for core in range(8):
        b, s = core // 2, core % 2
        out[b, s * NT:(s + 1) * NT] = res.results[core]["y_out"]
    if _debug:
        return out, [{k: r[k] for k in dbg_outs} for r in res.results]
    return out
```

```python
import numpy as np
from contextlib import ExitStack
import concourse.bass as bass
import concourse.mybir as mybir
from concourse.bass_utils import run_bass_kernel_spmd

F32 = mybir.dt.float32
BF16 = mybir.dt.bfloat16
AF = mybir.ActivationFunctionType
ALU = mybir.AluOpType

D = 2048
SEQ = 2048
NT = 1024
EPS = 1e-6
NEXP = 16384
GE = 512
NPASS = 2
TP = 8 // NPASS
SCALE = 192 ** -0.5
POOL_AS = None


class Sched:
    ENG = ("pe", "act", "dve", "pool", "sp")

    def __init__(self, nc):
        self.nc = nc
        self.streams = {e: [] for e in self.ENG}
        self.sems = {}
        self._sem_ctx = []
        self.count = {}
        self.seen = {e: {} for e in self.ENG}
        self.dep = {}
        for e in ("pe", "act", "dve", "pool"):
            self._mksem("E_" + e)

    def _mksem(self, name):
        if name not in self.sems:
            ctx = self.nc.semaphore(name)
            self.sems[name] = ctx.__enter__()
            self._sem_ctx.append(ctx)
            self.count[name] = 0
        return name

    def close(self):
        for ctx in reversed(self._sem_ctx):
            ctx.__exit__(None, None, None)

    def _d(self, k):
        return self.dep.setdefault(k, {"w": {}, "r": {}})

    def _collect(self, reads, writes):
        need = {}
        for k in reads:
            for s, v in self._d(k)["w"].items():
                need[s] = max(need.get(s, 0), v)
        for k in writes:
            d = self._d(k)
            for dd in (d["w"], d["r"]):
                for s, v in dd.items():
                    need[s] = max(need.get(s, 0), v)
        return need

    def _emit_waits(self, eng, need):
        own = "E_" + eng
        for s, v in need.items():
            if s == own and eng == "pe":
                continue
            if self.seen[eng].get(s, 0) >= v:
                continue
            self.seen[eng][s] = v
            self.streams[eng].append(("wait", s, v))

    def _record(self, reads, writes, s, v):
        for k in writes:
            d = self._d(k)
            d["w"] = {s: v}
            d["r"] = {}
        for k in reads:
            if k in writes:
                continue
            d = self._d(k)
            d["r"][s] = max(d["r"].get(s, 0), v)

    def op(self, eng, fn, reads=(), writes=()):
        if eng == "pool" and POOL_AS is not None:
            eng = POOL_AS
        self._emit_waits(eng, self._collect(reads, writes))
        s = "E_" + eng
        self.count[s] += 1
        self.streams[eng].append(("op", fn, s, 1))
        self._record(reads, writes, s, self.count[s])

    def dma(self, out_ap, in_ap, reads=(), writes=(), semkey=None):
        self._emit_waits("sp", self._collect(reads, writes))
        s = self._mksem("D_" + str(semkey if semkey is not None else (list(writes) + list(reads))[0]))
        self.count[s] += 16
        self.streams["sp"].append(("op", lambda e, o=out_ap, i=in_ap: e.dma_start(out=o, in_=i), s, 16))
        self._record(reads, writes, s, self.count[s])

    def barrier(self):
        for eng in self.ENG:
            for s, v in self.count.items():
                if v > 0 and self.seen[eng].get(s, 0) < v and not (eng == "pe" and s == "E_pe"):
                    self.seen[eng][s] = v
                    self.streams[eng].append(("wait", s, v))

    def final_wait(self, eng="sp"):
        for s, v in self.count.items():
            if v > 0 and self.seen[eng].get(s, 0) < v:
                self.seen[eng][s] = v
                self.streams[eng].append(("wait", s, v))

    def emit(self):
        names = {"pe": "tensor", "act": "scalar", "dve": "vector", "pool": "gpsimd", "sp": "sync"}
        with self.nc.Block() as block:
            for e in self.ENG:
                stream = self.streams[e]

                def body(engine, stream=stream):
                    for it in stream:
                        if it[0] == "wait":
                            engine.wait_ge(self.sems[it[1]], it[2])
                        else:
                            it[1](engine).then_inc(self.sems[it[2]], it[3])
                getattr(block, names[e])(body)


class Arena:
    def __init__(self, nc, stack, nbytes=206000):
        self.words = nbytes // 4
        self.t = stack.enter_context(nc.sbuf_tensor("arena", [128, self.words], F32))
        self.free = [(0, self.words)]
        self.live = {}

    def alloc(self, name, shape, dt=F32, hi=False):
        n = 1
        for s in shape[1:]:
            n *= s
        w = (n * (4 if dt == F32 else 2) + 3) // 4
        w = (w + 15) // 16 * 16
        order = range(len(self.free) - 1, -1, -1) if hi else range(len(self.free))
        for i in order:
            o, sz = self.free[i]
            if sz >= w:
                if hi:
                    self.free[i] = (o, sz - w)
                    o = o + sz - w
                else:
                    self.free[i] = (o + w, sz - w)
                break
        else:
            raise RuntimeError("arena full allocating %s (%d words); free=%s live=%s" % (name, w, self.free, sorted(self.live)))
        self.live[name] = (o, w)
        ap = self.t[0:shape[0], o:o + w]
        if dt != F32:
            ap = ap.bitcast(dt)
        ap = ap[:, 0:n]
        if len(shape) == 3:
            ap = ap.rearrange("p (a b) -> p a b", a=shape[1])
        elif len(shape) == 4:
            ap = ap.rearrange("p (a b c) -> p a b c", a=shape[1], b=shape[2])
        elif len(shape) == 5:
            ap = ap.rearrange("p (a b c d) -> p a b c d", a=shape[1], b=shape[2], c=shape[3])
        return ap

    def release(self, *names):
        for name in names:
            o, w = self.live.pop(name)
            self.free.append((o, w))
        self.free.sort()
        merged = []
        for o, w in self.free:
            if w == 0:
                continue
            if merged and merged[-1][0] + merged[-1][1] == o:
                merged[-1] = (merged[-1][0], merged[-1][1] + w)
            else:
                merged.append((o, w))
        self.free = merged

def build(debug=(), stop=None):
    nc = bass.Bass("TRN2", target_bir_lowering=False)
    S = Sched(nc)
    top = ExitStack()
    dbg_outs = []

    def din(name, shape):
        return nc.dram_tensor(name, list(shape), F32, kind="ExternalInput").ap()

    x_own = din("x_own", [NT, D])
    x_prev = din("x_prev", [NT, D])
    c_vec = din("c_vec", [128, 16])
    consts = din("consts", [128, 2])
    rope = din("rope", [4, 64, NT])
    ident_d = din("ident", [128, 128])
    w_ada = din("w_ada", [D, 6 * D])
    b_ada = din("b_ada", [6 * D])
    g_norm_mix = din("g_norm_mix", [D])
    w_in = din("w_in", [D, 4160])
    conv_w = din("conv_w", [3, 1024])
    g_q_lat = din("g_q_lat", [512])
    w_uq = din("w_uq", [512, 1536])
    g_kv_lat = din("g_kv_lat", [512])
    w_ukv = din("w_ukv", [512, 2048])
    g_out_conv = din("g_out_conv", [1024])
    g_out_attn = din("g_out_attn", [1024])
    w_out = din("w_out", [D, D])
    g_norm_ffn = din("g_norm_ffn", [D])
    peer_w_q = din("peer_w_q", [D, D])
    peer_keys = din("peer_keys", [16 * 128, 128])
    peer_u = din("peer_u", [NEXP, D])
    peer_v = din("peer_v", [NEXP, D])
    g_final = din("g_final", [D])
    y_out = nc.dram_tensor("y_out", [NT, D], F32, kind="ExternalOutput").ap()
    x2_d = nc.dram_tensor("x2_scratch", [NT, D], F32, kind="Internal").ap()
    sc_d = nc.dram_tensor("sc_scratch", [NT, D], F32, kind="Internal").ap()

    A = Arena(nc, top)

    def fin():
        S.final_wait("sp")
        with nc.allow_non_contiguous_dma(reason="tiny per-partition parameter vectors"):
            S.emit()
        S.close()
        return nc, dbg_outs

    def dbg(name, ap, reads):
        if name in debug:
            o = nc.dram_tensor("dbg_" + name, list(ap.shape), ap.dtype, kind="ExternalOutput").ap()
            S.dma(o, ap, reads=reads, semkey="dbg_" + name)
            dbg_outs.append("dbg_" + name)

    ps = [top.enter_context(nc.psum_tensor("ps%d" % i, [128, 512], F32)) for i in range(4)]
    psO = top.enter_context(nc.psum_tensor("psO", [128, 2048], F32))
    psOb = [psO[:, i * 512:(i + 1) * 512] for i in range(4)]

    def psT(i):
        return ps[i][:].bitcast(BF16)

    ident_f = A.alloc("ident_f", [128, 128])
    ident_b = A.alloc("ident_b", [128, 128], BF16)
    ones_f = A.alloc("ones_f", [128, 128])
    ones_b = A.alloc("ones_b", [128, 128], BF16)
    cst = A.alloc("cst", [128, 2])
    cT = A.alloc("cT", [128, 16, 128])
    cs_ = A.alloc("cs_", [128, 16])
    prm = A.alloc("prm", [128, 64])

    S.dma(ident_f, ident_d, writes=["ident_f"])
    S.dma(cst, consts, writes=["cst"])
    S.dma(cs_, c_vec, writes=["cs_"])
    S.dma(prm[:, 0:24].rearrange("p (k j) -> p k j", k=3), conv_w.rearrange("k (j p) -> p k j", p=128),
          writes=["prm_a"])
    S.dma(prm[:, 24:32], g_out_conv.rearrange("(j p) -> p j", p=128), writes=["prm_b"])
    S.dma(prm[:, 32:40], g_out_attn.rearrange("(j p) -> p j", p=128), writes=["prm_c"])
    S.dma(prm[:, 40:44], g_q_lat.rearrange("(j p) -> p j", p=128), writes=["prm_d"])
    S.dma(prm[:, 44:48], g_kv_lat.rearrange("(j p) -> p j", p=128), writes=["prm_e"])
    PRM = ["prm_a", "prm_b", "prm_c", "prm_d", "prm_e"]
    S.op("act", lambda e: e.copy(out=ident_b, in_=ident_f), reads=["ident_f"], writes=["ident_b"])
    S.op("pool", lambda e: e.memset(ones_f, 1.0), writes=["ones_f"])
    S.op("pool", lambda e: e.memset(ones_b, 1.0), writes=["ones_b"])
    S.op("act", lambda e: e.activation(out=cs_, in_=cs_, func=AF.Silu), reads=["cs_"], writes=["cs_"])
    for kc in range(16):
        S.op("dve", lambda e, kc=kc: e.tensor_copy(out=cT[:, kc, :], in_=cs_[:, kc:kc + 1].to_broadcast([128, 128])),
             reads=["cs_"], writes=["cT"])

    def rstd_from(out_ap, in_ap, n, reads, writes):
        S.op("act", lambda e: e.activation(out=out_ap, in_=in_ap, func=AF.Sqrt, scale=1.0 / n, bias=EPS),
             reads=reads, writes=writes)
        S.op("dve", lambda e: e.reciprocal(out=out_ap, in_=out_ap), reads=writes, writes=writes)

    def compute_mod(col0, ncols, dst, dst_key):
        wa = [A.alloc("wa%d" % i, [128, 16, 256]) for i in range(2)]
        bb = [A.alloc("bb%d" % i, [128, 256]) for i in range(2)]
        for g in range(ncols // 256):
            sl = g % 2
            c0 = col0 + g * 256
            S.dma(wa[sl], w_ada[:, c0:c0 + 256].rearrange("(k p) c -> p k c", p=128), writes=["wa%d" % sl])
            S.dma(bb[sl], b_ada[c0:c0 + 256].partition_broadcast(128), writes=["bb%d" % sl])
            pb = ps[g % 2]
            for kc in range(16):
                S.op("pe", lambda e, kc=kc, sl=sl, pb=pb: e.matmul(pb[:, 0:256], cT[:, kc, :], wa[sl][:, kc, :],
                                                                     start=(kc == 0), stop=(kc == 15)),
                     reads=["cT", "wa%d" % sl], writes=["ps%d" % (g % 2)])
            S.op("dve", lambda e, g=g, sl=sl, pb=pb: e.tensor_tensor(out=dst[:, g * 256:(g + 1) * 256], in0=pb[:, 0:256],
                                                                      in1=bb[sl], op=ALU.add),
                 reads=["ps%d" % (g % 2), "bb%d" % sl], writes=[dst_key])
        S.barrier()
        A.release("wa0", "wa1", "bb0", "bb1")

    def norm_tile(xt, xkey, hb, hbkey, gsc, sh, ss, gkeys):
        S.op("act", lambda e: e.activation(out=hb, in_=xt, func=AF.Square, accum_out=ss[:, 0:1]),
             reads=[xkey], writes=[hbkey, "ss"])
        rstd_from(ss[:, 0:1], ss[:, 0:1], D, ["ss"], ["ss"])
        S.op("dve", lambda e: e.scalar_tensor_tensor(out=xt, in0=xt, scalar=ss[:, 0:1], in1=gsc, op0=ALU.mult,
                                                      op1=ALU.mult), reads=[xkey, "ss"] + gkeys, writes=[xkey])
        S.op("pool", lambda e: e.tensor_tensor(out=hb, in0=xt, in1=sh, op=ALU.add), reads=[xkey] + gkeys,
             writes=[hbkey])

    def transpose_tile(hb, hbkey, dstT, dkey, col0):
        for half in range(2):
            bank = 2 + half
            pt = psT(bank)
            for k in range(8):
                kc = half * 8 + k
                S.op("pe", lambda e, kc=kc, k=k, pt=pt: e.transpose(out=pt[:, k * 128:(k + 1) * 128],
                                                                     in_=hb[:, kc * 128:(kc + 1) * 128], identity=ident_b),
                     reads=[hbkey, "ident_b"], writes=["ps%d" % bank])
            S.op("act", lambda e, half=half, pt=pt: e.copy(out=dstT[:, half * 8:(half + 1) * 8, col0:col0 + 128],
                                                            in_=pt[:, 0:1024].rearrange("p (k t) -> p k t", k=8)),
                 reads=["ps%d" % bank], writes=[dkey])

    gsc = A.alloc("gsc", [128, D])
    shm = A.alloc("shm", [128, D])
    compute_mod(0, D, shm, "shm")
    compute_mod(D, D, gsc, "gsc")
    gnm = A.alloc("gnm", [128, D])
    S.dma(gnm, g_norm_mix.partition_broadcast(128), writes=["gnm"])
    S.op("dve", lambda e: e.scalar_tensor_tensor(out=gsc, in0=gsc, scalar=1.0, in1=gnm, op0=ALU.add,
                                                  op1=ALU.mult), reads=["gsc", "gnm"], writes=["gsc"])
    S.barrier()
    A.release("gnm")
    dbg("shm", shm[0:1, :], ["shm"])
    dbg("gsc", gsc[0:1, :], ["gsc"])
    dbg("prm", prm, PRM)
    if stop == "mod":
        return fin()

    hT = A.alloc("hT", [128, 16, NT], BF16)
    hTh = A.alloc("hTh", [128, 16, 2], BF16)
    kvn = A.alloc("kvn", [128, 4, 2 * NT], BF16, hi=True)
    qn = A.alloc("qn", [128, 4, NT], BF16, hi=True)
    krope = A.alloc("krope", [64, 2 * NT], BF16, hi=True)
    rkv_bc = A.alloc("rkv_bc", [128, 2 * NT], hi=True)
    rq_bc = A.alloc("rq_bc", [128, NT], hi=True)
    ropet = A.alloc("ropet", [64, 2, NT])
    xts = [A.alloc("xt%d" % i, [128, D]) for i in range(2)]
    hb = A.alloc("hb", [128, D], BF16)
    ss = A.alloc("ss", [128, 2])
    wraw = A.alloc("wraw", [128, 8, 3, 128])
    wbf = [A.alloc("wbf%d" % i, [128, 16, 3, 128], BF16) for i in range(2)]
    sqb = A.alloc("sqb", [128, NT])

    wctr = [0]

    def load_w(cols_list, width, swap=False):
        sl = wctr[0] % 2
        wctr[0] += 1
        n = len(cols_list)
        for kh in range(2):
            for i, c0 in enumerate(cols_list):
                S.dma(wraw[:, :, i, 0:width],
                      w_in[kh * 1024:(kh + 1) * 1024, c0:c0 + width].rearrange("(k p) c -> p k c", p=128),
                      writes=["wraw%d" % i])
            if not swap:
                S.op("pool", lambda e, kh=kh: e.tensor_copy(out=wbf[sl][:, kh * 8:(kh + 1) * 8, 0:n, 0:width],
                                                            in_=wraw[:, :, 0:n, 0:width]),
                     reads=["wraw%d" % i for i in range(n)], writes=["wbf%d" % sl])
            else:
                S.op("pool", lambda e, kh=kh: e.tensor_copy(out=wbf[sl][:, kh * 8:(kh + 1) * 8, 0, 0:64], in_=wraw[:, :, 0, 0:64]),
                     reads=["wraw0"], writes=["wbf%d" % sl])
                S.op("pool", lambda e, kh=kh: e.tensor_copy(out=wbf[sl][:, kh * 8:(kh + 1) * 8, 1, 0:32], in_=wraw[:, :, 0, 32:64]),
                     reads=["wraw0"], writes=["wbf%d" % sl])
                S.op("pool", lambda e, kh=kh: e.tensor_copy(out=wbf[sl][:, kh * 8:(kh + 1) * 8, 1, 32:64], in_=wraw[:, :, 0, 0:32]),
                     reads=["wraw0"], writes=["wbf%d" % sl])
        return wbf[sl], "wbf%d" % sl

    def lin(pbank, wt, wkey, gi, width, rhs_fn, n):
        for kc in range(16):
            S.op("pe", lambda e, kc=kc: e.matmul(ps[pbank][0:width, 0:n], wt[:, kc, gi, 0:width], rhs_fn(kc),
                                                  start=(kc == 0), stop=(kc == 15)),
                 reads=[wkey, "hT", "hTh"], writes=["ps%d" % pbank])

    def lat_chunk(wt, wkey, dst, dkey, dcol0, gcol, stat_first, stat_last):
        for th in range(2):
            lin(0, wt, wkey, 0, 128, lambda kc, th=th: hT[:, kc, th * 512:(th + 1) * 512], 512)
            S.op("act", lambda e: e.copy(out=sqb[:, 512:1024], in_=ps[0][:, :]), reads=["ps0"], writes=["sqraw"])
            S.op("dve", lambda e, th=th: e.tensor_scalar(out=dst[:, dcol0 + th * 512: dcol0 + (th + 1) * 512],
                                                          in0=sqb[:, 512:1024], scalar1=prm[:, gcol:gcol + 1], scalar2=None,
                                                          op0=ALU.mult), reads=["sqraw"] + PRM, writes=[dkey])
            S.op("pool", lambda e: e.tensor_tensor(out=sqb[:, 0:512], in0=sqb[:, 512:1024], in1=sqb[:, 512:1024], op=ALU.mult),
                 reads=["sqraw"], writes=["sqb"])
            if stop == "kv0c":
                continue
            S.op("pe", lambda e, th=th: e.matmul(ps[2 + th][:, :], ones_f, sqb[:, 0:512], start=stat_first,
                                                  stop=stat_last), reads=["ones_f", "sqb"], writes=["ps%d" % (2 + th)])

    def krope_part(wt, wkey, tok0):
        for th in range(2):
            lin(0, wt, wkey, 0, 64, lambda kc, th=th: hT[:, kc, th * 512:(th + 1) * 512], 512)
            lin(1, wt, wkey, 1, 64, lambda kc, th=th: hT[:, kc, th * 512:(th + 1) * 512], 512)
            S.op("dve", lambda e, th=th: e.tensor_tensor(out=sqb[0:64, 0:512], in0=ps[0][0:64, :],
                                                          in1=ropet[:, 0, th * 512:(th + 1) * 512], op=ALU.mult),
                 reads=["ps0", "ropet"], writes=["sqb"])
            S.op("dve", lambda e, th=th: e.tensor_tensor(out=sqb[0:64, 512:1024], in0=ps[1][0:64, :],
                                                          in1=ropet[:, 1, th * 512:(th + 1) * 512], op=ALU.mult),
                 reads=["ps1", "ropet"], writes=["sqraw"])
            S.op("pool", lambda e, th=th: e.tensor_tensor(out=krope[:, tok0 + th * 512: tok0 + (th + 1) * 512],
                                                           in0=sqb[0:64, 0:512], in1=sqb[0:64, 512:1024], op=ALU.add),
                 reads=["sqb", "sqraw"], writes=["krope"])

    for part in range(2):
        src = x_prev if part == 0 else x_own
        S.dma(ropet, rope[2 * part:2 * part + 2].rearrange("f p t -> p f t"), writes=["ropet"])
        for ti in range(8):
            sl = ti % 2
            S.dma(xts[sl], src[ti * 128:(ti + 1) * 128, :], writes=["xt%d" % sl])
            norm_tile(xts[sl], "xt%d" % sl, hb, "hb", gsc, shm, ss, ["gsc", "shm"])
            transpose_tile(hb, "hb", hT, "hT", ti * 128)
            if stop == "A1a" and ti == 1:
                dbg("hT", hT[:, 3, 0:256], ["hT"])
                return fin()
        if stop == "A1":
            dbg("hT", hT[:, 3, :], ["hT"])
            return fin()
        tok0 = part * NT
        for qc in range(4):
            wt, wkey = load_w([3584 + qc * 128], 128)
            if stop == "kv0a":
                dbg("wbf", wt[:, :, 0, :], [wkey])
                return fin()
            lat_chunk(wt, wkey, kvn[:, qc, :], "kvn", tok0, 44 + qc, qc == 0, qc == 3)
            if stop in ("kv0", "kv0b", "kv0c"):
                dbg("kvn", kvn[:, 0, 0:NT], ["kvn"])
                return fin()
        for th in range(2):
            rstd_from(rkv_bc[:, tok0 + th * 512: tok0 + (th + 1) * 512], ps[2 + th][:, :], 512,
                      ["ps%d" % (2 + th)], ["rkv_bc"])
        wt, wkey = load_w([4096], 64, swap=True)
        krope_part(wt, wkey, tok0)
        if part == 0:
            S.op("pool", lambda e: e.tensor_copy(out=hTh, in_=hT[:, :, NT - 2:NT]), reads=["hT"], writes=["hTh"])
            continue
        for qc in range(4):
            wt, wkey = load_w([3072 + qc * 128], 128)
            lat_chunk(wt, wkey, qn[:, qc, :], "qn", 0, 40 + qc, qc == 0, qc == 3)
        for th in range(2):
            rstd_from(rq_bc[:, th * 512:(th + 1) * 512], ps[2 + th][:, :], 512, ["ps%d" % (2 + th)], ["rq_bc"])
    dbg("kvn", kvn[:, 0, :], ["kvn"])
    dbg("krope", krope, ["krope"])
    dbg("rkv", rkv_bc[0:1, :], ["rkv_bc"])
    dbg("qn", qn[:, 0, :], ["qn"])
    dbg("rq", rq_bc[0:1, :], ["rq_bc"])

    if stop == "A2":
        return fin()
    S.barrier()
    A.release("gsc", "shm", "xt0", "xt1", "hb", "ss")
    merged = A.alloc("merged", [128, 16, NT], BF16, hi=True)
    ctmp = A.alloc("ctmp", [128, 512])
    zbuf = A.alloc("zbuf", [128, NT + 2])
    ybuf = A.alloc("ybuf", [128, NT])
    bbuf = A.alloc("bbuf", [128, NT])
    rsb = A.alloc("rsb", [128, NT])
    xs4 = A.alloc("xs4", [128, 4])
    for j in range(8):
        wt, wkey = load_w([j * 128, 1024 + j * 128, 2048 + j * 128], 128)
        for gi, off in ((1, 0), (2, 2)):
            for kc in range(16):
                S.op("pe", lambda e, kc=kc, gi=gi, off=off, wt=wt: e.matmul(ps[3][:, off:off + 2], wt[:, kc, gi, :],
                                                                             hTh[:, kc, :], start=(kc == 0), stop=(kc == 15)),
                     reads=[wkey, "hTh"], writes=["ps3"])
            if gi == 1:
                S.op("act", lambda e: e.copy(out=xs4[:, 0:2], in_=ps[3][:, 0:2]), reads=["ps3"], writes=["xs4"])
            else:
                S.op("dve", lambda e: e.scalar_tensor_tensor(out=zbuf[:, 0:2], in0=xs4[:, 0:2], scalar=cst[:, 0:1],
                                                              in1=ps[3][:, 2:4], op0=ALU.mult, op1=ALU.mult),
                     reads=["xs4", "cst", "ps3"], writes=["zbuf"])
        for th in range(2):
            rf = lambda kc, th=th: hT[:, kc, th * 512:(th + 1) * 512]
            lin(0, wt, wkey, 1, 128, rf, 512)
            lin(1, wt, wkey, 2, 128, rf, 512)
            lin(2, wt, wkey, 0, 128, rf, 512)
            S.op("act", lambda e: e.copy(out=ctmp, in_=ps[0][:, :]), reads=["ps0"], writes=["ctmp"])
            S.op("dve", lambda e, th=th: e.tensor_tensor(out=zbuf[:, 2 + th * 512: 2 + (th + 1) * 512], in0=ctmp,
                                                          in1=ps[1][:, :], op=ALU.mult),
                 reads=["ctmp", "ps1"], writes=["zbuf"])
            S.op("act", lambda e, th=th: e.copy(out=bbuf[:, th * 512:(th + 1) * 512], in_=ps[2][:, :]),
                 reads=["ps2"], writes=["bbuf"])
        S.op("dve", lambda e, j=j: e.tensor_scalar(out=ybuf, in0=zbuf[:, 0:NT], scalar1=prm[:, j:j + 1],
                                                    scalar2=None, op0=ALU.mult), reads=["zbuf"] + PRM, writes=["ybuf"])
        for k in (1, 2):
            S.op("dve", lambda e, j=j, k=k: e.scalar_tensor_tensor(out=ybuf, in0=zbuf[:, k:NT + k],
                                                                    scalar=prm[:, 8 * k + j: 8 * k + j + 1], in1=ybuf,
                                                                    op0=ALU.mult, op1=ALU.add),
                 reads=["zbuf", "ybuf"] + PRM, writes=["ybuf"])
        S.op("pool", lambda e: e.tensor_tensor(out=ybuf, in0=ybuf, in1=bbuf, op=ALU.mult),
             reads=["ybuf", "bbuf"], writes=["ybuf"])
        S.op("act", lambda e: e.activation(out=sqb, in_=ybuf, func=AF.Square), reads=["ybuf"], writes=["sqb"])
        for th in range(2):
            S.op("pe", lambda e, th=th: e.matmul(ps[3][:, :], ones_f, sqb[:, th * 512:(th + 1) * 512], start=True,
                                                  stop=True), reads=["ones_f", "sqb"], writes=["ps3"])
            rstd_from(rsb[:, th * 512:(th + 1) * 512], ps[3][:, :], 128, ["ps3"], ["rsb"])
        S.op("dve", lambda e, j=j: e.scalar_tensor_tensor(out=merged[:, j, :], in0=ybuf, scalar=prm[:, 24 + j:25 + j],
                                                           in1=rsb, op0=ALU.mult, op1=ALU.mult),
             reads=["ybuf", "rsb"] + PRM, writes=["merged"])
    dbg("mconv", merged[:, 0, :], ["merged"])

    if stop == "conv":
        return fin()
    S.barrier()
    A.release("hT", "hTh", "wraw", "wbf0", "wbf1", "sqb", "ctmp", "zbuf", "ybuf", "bbuf", "rsb", "xs4")
    qTn = A.alloc("qTn", [128, 8, NT], BF16, hi=True)
    qTr = A.alloc("qTr", [64, 8, NT], BF16, hi=True)
    wq = A.alloc("wq", [128, 4, 1536], BF16)
    wqr = A.alloc("wqr", [128, 4, 8, 64], BF16)
    wst = A.alloc("wst", [128, 2048])
    cq = A.alloc("cq", [64, 2, NT])
    rtmp = A.alloc("rtmp", [64, 1024])
    for kc in range(4):
        S.dma(wst[:, 0:1536], w_uq[kc * 128:(kc + 1) * 128, :], writes=["wst"])
        S.op("pool", lambda e, kc=kc: e.tensor_copy(out=wq[:, kc, :], in_=wst[:, 0:1536]), reads=["wst"], writes=["wq"])
    wq4 = wq.rearrange("p k (h c) -> p k h c", h=8)
    S.op("pool", lambda e: e.tensor_copy(out=wqr[:, :, :, 0:32], in_=wq4[:, :, :, 160:192]), reads=["wq"], writes=["wqr"])
    S.op("pool", lambda e: e.tensor_copy(out=wqr[:, :, :, 32:64], in_=wq4[:, :, :, 128:160]), reads=["wq"], writes=["wqr"])
    for f in range(2):
        S.op("dve", lambda e, f=f: e.tensor_tensor(out=cq[:, f, :], in0=ropet[:, f, :], in1=rq_bc[0:64, :], op=ALU.mult),
             reads=["ropet", "rq_bc"], writes=["cq"])
    dbg("rq2", rq_bc[0:1, :], ["rq_bc"])
    dbg("wq", wq[:, 0, 0:192], ["wq"])
    dbg("wq3", wq[:, 3, 0:192], ["wq"])
    dbg("wst", wst[:, 0:192], ["wst"])
    if stop == "A3q0":
        return fin()
    for h in range(8):
        for th in range(2):
            tsl = slice(th * 512, (th + 1) * 512)
            for kc in range(4):
                S.op("pe", lambda e, kc=kc, h=h, tsl=tsl: e.matmul(ps[0][:, :], wq[:, kc, h * 192:h * 192 + 128], qn[:, kc, tsl],
                                                                   start=(kc == 0), stop=(kc == 3)),
                     reads=["wq", "qn"], writes=["ps0"])
            S.op("dve", lambda e, h=h, tsl=tsl: e.tensor_tensor(out=qTn[:, h, tsl], in0=ps[0][:, :], in1=rq_bc[:, tsl], op=ALU.mult),
                 reads=["ps0", "rq_bc"], writes=["qTn"])
            for kc in range(4):
                S.op("pe", lambda e, kc=kc, h=h, tsl=tsl: e.matmul(ps[1][0:64, :], wq[:, kc, h * 192 + 128:h * 192 + 192], qn[:, kc, tsl],
                                                                   start=(kc == 0), stop=(kc == 3)),
                     reads=["wq", "qn"], writes=["ps1"])
            for kc in range(4):
                S.op("pe", lambda e, kc=kc, h=h, tsl=tsl: e.matmul(ps[2][0:64, :], wqr[:, kc, h, :], qn[:, kc, tsl],
                                                                   start=(kc == 0), stop=(kc == 3)),
                     reads=["wqr", "qn"], writes=["ps2"])
            S.op("dve", lambda e, tsl=tsl: e.tensor_tensor(out=rtmp[:, 0:512], in0=ps[1][0:64, :], in1=cq[:, 0, tsl], op=ALU.mult),
                 reads=["ps1", "cq"], writes=["rtmp"])
            S.op("dve", lambda e, tsl=tsl: e.tensor_tensor(out=rtmp[:, 512:1024], in0=ps[2][0:64, :], in1=cq[:, 1, tsl], op=ALU.mult),
                 reads=["ps2", "cq"], writes=["rtmp"])
            S.op("pool", lambda e, h=h, tsl=tsl: e.tensor_tensor(out=qTr[:, h, tsl], in0=rtmp[:, 0:512], in1=rtmp[:, 512:1024], op=ALU.add),
                 reads=["rtmp"], writes=["qTr"])
        if stop == "A3q1":
            dbg("qTn", qTn[:, 0, :], ["qTn"])
            dbg("wqb", wq[:, 0, 0:192], ["wq"])
            dbg("qn2", qn[:, 0, :], ["qn"])
            return fin()
    dbg("qTn", qTn[:, 0, :], ["qTn"])
    dbg("qTr", qTr[:, 0, :], ["qTr"])

    S.barrier()
    A.release("qn", "rq_bc", "ropet", "wq", "wqr", "cq", "rtmp", "wst")
    kTn = A.alloc("kTn", [128, 8, 2 * NT], BF16, hi=True)
    vtm = A.alloc("vtm", [128, 16, 1024], BF16)
    rkv_tm = A.alloc("rkv_tm", [128, 16])
    wkv = A.alloc("wkv", [128, 4, 2048], BF16)
    wst_kv = A.alloc("wst_kv", [128, 2048])
    for kc in range(4):
        S.dma(wst_kv, w_ukv[kc * 128:(kc + 1) * 128, :], writes=["wst_kv"])
        S.op("pool", lambda e, kc=kc: e.tensor_copy(out=wkv[:, kc, :], in_=wst_kv), reads=["wst_kv"], writes=["wkv"])
    for blk in range(16):
        S.op("pe", lambda e, blk=blk: e.transpose(out=ps[3][:, 0:128], in_=rkv_bc[:, blk * 128:(blk + 1) * 128],
                                                   identity=ident_f), reads=["rkv_bc", "ident_f"], writes=["ps3"])
        S.op("act", lambda e, blk=blk: e.copy(out=rkv_tm[:, blk:blk + 1], in_=ps[3][:, 0:1]), reads=["ps3"], writes=["rkv_tm"])
    for h in range(8):
        for tc in range(4):
            tsl = slice(tc * 512, (tc + 1) * 512)
            pb = tc % 2
            for kc in range(4):
                S.op("pe", lambda e, kc=kc, h=h, tsl=tsl, pb=pb: e.matmul(ps[pb][:, :], wkv[:, kc, h * 256:h * 256 + 128], kvn[:, kc, tsl],
                                                                          start=(kc == 0), stop=(kc == 3)),
                     reads=["wkv", "kvn"], writes=["ps%d" % pb])
            S.op("dve", lambda e, h=h, tsl=tsl, pb=pb: e.tensor_tensor(out=kTn[:, h, tsl], in0=ps[pb][:, :], in1=rkv_bc[:, tsl], op=ALU.mult),
                 reads=["ps%d" % pb, "rkv_bc"], writes=["kTn"])
    wkv4 = wkv.rearrange("p k (h c) -> p k h c", h=8)
    for blk in range(16):
        for hg in range(2):
            pb = hg
            for kc in range(4):
                S.op("pe", lambda e, kc=kc, blk=blk, hg=hg, pb=pb: e.matmul(
                    ps[pb][:, :].rearrange("p (h c) -> p h c", h=4), kvn[:, kc, blk * 128:(blk + 1) * 128],
                    wkv4[:, kc, hg * 4:(hg + 1) * 4, 128:256], start=(kc == 0), stop=(kc == 3)),
                    reads=["wkv", "kvn"], writes=["ps%d" % pb])
            S.op("act", lambda e, blk=blk, hg=hg, pb=pb: e.activation(out=vtm[:, blk, hg * 512:(hg + 1) * 512], in_=ps[pb][:, :],
                                                                      func=AF.Copy, scale=rkv_tm[:, blk:blk + 1]),
                 reads=["ps%d" % pb, "rkv_tm"], writes=["vtm"])
    dbg("kTn", kTn[:, 0, :], ["kTn"])
    dbg("vtm", vtm[:, 0, :], ["vtm"])

    if stop == "A3":
        return fin()
    S.barrier()
    A.release("kvn", "rkv_bc", "wkv", "wst_kv", "rkv_tm")
    pTb = [A.alloc("pT%d" % i, [128, NT], BF16) for i in range(2)]
    oT = A.alloc("oT", [128, NT])
    rz = A.alloc("rz", [128, NT])
    sq4 = A.alloc("sq4", [128, NT])
    it = 0
    last_j = {0: 11, 1: 15}
    for h in range(8):
        for j in range(16):
            own = j >= 8
            jj = j - 8
            sl = it % 2
            it += 1
            ranges = []
            for th in range(2):
                lo = max(jj * 128, th * 512) if own else th * 512
                hi = (th + 1) * 512
                if lo < hi:
                    ranges.append((th, lo, hi))
            for (th, lo, hi) in ranges:
                n = hi - lo
                S.op("pe", lambda e, h=h, j=j, lo=lo, hi=hi, n=n, th=th: e.matmul(ps[th][:, 0:n], kTn[:, h, j * 128:(j + 1) * 128],
                                                                                 qTn[:, h, lo:hi], start=True, stop=False),
                     reads=["kTn", "qTn"], writes=["ps%d" % th])
                S.op("pe", lambda e, h=h, j=j, lo=lo, hi=hi, n=n, th=th: e.matmul(ps[th][:, 0:n], krope[:, j * 128:(j + 1) * 128],
                                                                                 qTr[:, h, lo:hi], start=False, stop=True),
                     reads=["krope", "qTr"], writes=["ps%d" % th])
                if own:
                    S.op("act", lambda e, lo=lo, hi=hi, n=n, th=th, sl=sl: e.activation(out=pTb[sl][:, lo:hi], in_=ps[th][:, 0:n],
                                                                                       func=AF.Exp, scale=SCALE),
                         reads=["ps%d" % th], writes=["pT%d" % sl])
                else:
                    S.op("act", lambda e, lo=lo, hi=hi, n=n, th=th, sl=sl: e.activation(out=pTb[sl][:, lo:hi], in_=ps[th][:, 0:n],
                                                                                       func=AF.Exp, scale=SCALE, bias=cst[:, 1:2]),
                         reads=["ps%d" % th, "cst"], writes=["pT%d" % sl])
            if own:
                S.op("pool", lambda e, jj=jj, sl=sl: e.memset(pTb[sl][64:128, jj * 128: jj * 128 + 64], 0.0),
                     reads=[], writes=["pT%d" % sl])
            for (th, lo, hi) in ranges:
                o0 = lo - th * 512
                n = hi - lo
                S.op("pe", lambda e, h=h, j=j, lo=lo, hi=hi, th=th, o0=o0, n=n, sl=sl: e.matmul(
                    psOb[th][:, o0:o0 + n], vtm[:, j, h * 128:(h + 1) * 128], pTb[sl][:, lo:hi],
                    start=(j == 0), stop=(j == last_j[th])), reads=["vtm", "pT%d" % sl], writes=["psO%d" % th])
                S.op("pe", lambda e, j=j, lo=lo, hi=hi, th=th, o0=o0, n=n, sl=sl: e.matmul(
                    psOb[2 + th][:, o0:o0 + n], ones_b, pTb[sl][:, lo:hi],
                    start=(j == 0), stop=(j == last_j[th])), reads=["ones_b", "pT%d" % sl], writes=["psO%d" % (2 + th)])
        for th in range(2):
            tsl = slice(th * 512, (th + 1) * 512)
            S.op("dve", lambda e, th=th, tsl=tsl: e.reciprocal(out=rz[:, tsl], in_=psOb[2 + th]), reads=["psO%d" % (2 + th)],
                 writes=["rz"])
            S.op("dve", lambda e, th=th, tsl=tsl: e.tensor_tensor(out=oT[:, tsl], in0=psOb[th], in1=rz[:, tsl], op=ALU.mult),
                 reads=["psO%d" % th, "rz"], writes=["oT"])
        S.op("act", lambda e: e.activation(out=sq4, in_=oT, func=AF.Square), reads=["oT"], writes=["sq4"])
        for th in range(2):
            tsl = slice(th * 512, (th + 1) * 512)
            S.op("pe", lambda e, tsl=tsl: e.matmul(ps[2][:, :], ones_f, sq4[:, tsl], start=True, stop=True),
                 reads=["ones_f", "sq4"], writes=["ps2"])
            rstd_from(rz[:, tsl], ps[2][:, :], 128, ["ps2"], ["rz"])
        S.op("dve", lambda e, h=h: e.scalar_tensor_tensor(out=merged[:, 8 + h, :], in0=oT, scalar=prm[:, 32 + h:33 + h],
                                                           in1=rz, op0=ALU.mult, op1=ALU.mult),
             reads=["oT", "rz"] + PRM, writes=["merged"])
    dbg("mattn", merged[:, 8, :], ["merged"])

    if stop == "A4":
        return fin()
    S.barrier()
    A.release("qTn", "qTr", "kTn", "vtm", "krope", "pT0", "pT1", "oT", "rz", "sq4")
    gtm = A.alloc("gtm", [128, D])
    compute_mod(2 * D, D, gtm, "gtm")
    wout = A.alloc("wout", [128, 16, D], BF16)
    wst_o = [A.alloc("wsto_%d" % i, [128, D]) for i in range(2)]
    xts_o = [A.alloc("xto%d" % i, [128, D]) for i in range(2)]
    x2t = [A.alloc("x2t%d" % i, [128, D]) for i in range(2)]
    for kc in range(16):
        sl = kc % 2
        S.dma(wst_o[sl], w_out[kc * 128:(kc + 1) * 128, :], writes=["wsto_%d" % sl])
        S.op("pool", lambda e, kc=kc, sl=sl: e.tensor_copy(out=wout[:, kc, :], in_=wst_o[sl]), reads=["wsto_%d" % sl], writes=["wout"])
    for ti in range(8):
        sl = ti % 2
        S.dma(xts_o[sl], x_own[ti * 128:(ti + 1) * 128, :], writes=["xto%d" % sl])
        for nq in range(4):
            pb = nq % 2
            for kc in range(16):
                S.op("pe", lambda e, kc=kc, ti=ti, nq=nq, pb=pb: e.matmul(ps[pb][:, :], merged[:, kc, ti * 128:(ti + 1) * 128],
                                                                          wout[:, kc, nq * 512:(nq + 1) * 512], start=(kc == 0), stop=(kc == 15)),
                     reads=["merged", "wout"], writes=["ps%d" % pb])
            S.op("dve", lambda e, nq=nq, pb=pb, sl=sl: e.tensor_tensor(out=x2t[sl][:, nq * 512:(nq + 1) * 512], in0=ps[pb][:, :],
                                                                       in1=gtm[:, nq * 512:(nq + 1) * 512], op=ALU.mult),
                 reads=["ps%d" % pb, "gtm"], writes=["x2t%d" % sl])
        S.op("pool", lambda e, sl=sl: e.tensor_tensor(out=x2t[sl], in0=x2t[sl], in1=xts_o[sl], op=ALU.add),
             reads=["x2t%d" % sl, "xto%d" % sl], writes=["x2t%d" % sl])
        S.dma(x2_d[ti * 128:(ti + 1) * 128, :], x2t[sl], reads=["x2t%d" % sl], writes=["x2_d%d" % ti], semkey="x2st%d" % sl)
    dbg("x2t", x2t[1], ["x2t1"])
    if stop == "A5":
        return fin()
    S.barrier()
    A.release("gtm", "wout", "wsto_0", "wsto_1", "xto0", "xto1", "x2t0", "x2t1", "merged")
    gscf = A.alloc("gscf", [128, D])
    shf = A.alloc("shf", [128, D])
    gtf = A.alloc("gtf", [128, D], hi=True)
    compute_mod(3 * D, D, shf, "shf")
    compute_mod(4 * D, D, gscf, "gscf")
    compute_mod(5 * D, D, gtf, "gtf")
    gnf = A.alloc("gnf", [128, D])
    S.dma(gnf, g_norm_ffn.partition_broadcast(128), writes=["gnf"])
    S.op("dve", lambda e: e.scalar_tensor_tensor(out=gscf, in0=gscf, scalar=1.0, in1=gnf, op0=ALU.add,
                                                  op1=ALU.mult), reads=["gscf", "gnf"], writes=["gscf"])
    S.barrier()
    A.release("gnf")
    h2T = A.alloc("h2T", [128, 16, NT], BF16, hi=True)
    thr = A.alloc("thr", [128, 8, 8], hi=True)
    nb = A.alloc("nb", [128, 8, 8], hi=True)
    wqp = A.alloc("wqp", [128, 16, D], BF16)
    wst_p = [A.alloc("wstp_%d" % i, [128, D]) for i in range(2)]
    keysT = A.alloc("keysT", [128, 16, 128], BF16)
    kst = A.alloc("kst", [128, 16, 128], BF16)
    xts_p = [A.alloc("xtp%d" % i, [128, D]) for i in range(2)]
    hb_p = A.alloc("hb_p", [128, D], BF16)
    ss_p = A.alloc("ss_p", [128, 2])
    qTt = A.alloc("qTt", [128, 16, 128], BF16)
    sct = [A.alloc("sct0", [128, 8, 2, 128])] * 2
    tv = A.alloc("tv", [128, 8, 2, 16])
    wk = A.alloc("wk", [128, 256])
    cand = A.alloc("cand", [128, 8, 16, 16])
    bv = A.alloc("bv", [128, 8, 16])
    ez = A.alloc("ez", [128, 16])
    zz = A.alloc("zz", [128, 8])
    nmx = A.alloc("nmx", [128, 8])
    for kc in range(16):
        sl = kc % 2
        S.dma(wst_p[sl], peer_w_q[kc * 128:(kc + 1) * 128, :], writes=["wstp_%d" % sl])
        S.op("pool", lambda e, kc=kc, sl=sl: e.tensor_copy(out=wqp[:, kc, :], in_=wst_p[sl]), reads=["wstp_%d" % sl], writes=["wqp"])
    S.barrier()
    S.dma(wst_p[0].rearrange("p (a b) -> p a b", a=16), peer_keys.rearrange("(a n) d -> n a d", n=128), writes=["wstp_0"])
    S.op("pool", lambda e: e.tensor_copy(out=kst, in_=wst_p[0].rearrange("p (a b) -> p a b", a=16)), reads=["wstp_0"], writes=["kst"])
    for half in range(2):
        pt = psT(2 + half)
        for k in range(8):
            S.op("pe", lambda e, k=k, half=half, pt=pt: e.transpose(out=pt[:, k * 128:(k + 1) * 128], in_=kst[:, half * 8 + k, :],
                                                                     identity=ident_b), reads=["kst", "ident_b"], writes=["ps%d" % (2 + half)])
        S.op("act", lambda e, half=half, pt=pt: e.copy(out=keysT[:, half * 8:(half + 1) * 8, :],
                                                        in_=pt[:, 0:1024].rearrange("p (k t) -> p k t", k=8)),
             reads=["ps%d" % (2 + half)], writes=["keysT"])
    for ti in range(8):
        sl = ti % 2
        S.dma(xts_p[sl], x2_d[ti * 128:(ti + 1) * 128, :], reads=["x2_d%d" % ti], writes=["xtp%d" % sl])
        norm_tile(xts_p[sl], "xtp%d" % sl, hb_p, "hb_p", gscf, shf, ss_p, ["gscf", "shf"])
        transpose_tile(hb_p, "hb_p", h2T, "h2T", ti * 128)
        for hp in range(16):
            pb = hp % 2
            for kc in range(16):
                S.op("pe", lambda e, kc=kc, hp=hp, ti=ti, pb=pb: e.matmul(ps[pb][:, 0:128], wqp[:, kc, hp * 128:(hp + 1) * 128],
                                                                          h2T[:, kc, ti * 128:(ti + 1) * 128], start=(kc == 0), stop=(kc == 15)),
                     reads=["wqp", "h2T"], writes=["ps%d" % pb])
            S.op("act", lambda e, hp=hp, pb=pb: e.copy(out=qTt[:, hp, :], in_=ps[pb][:, 0:128]), reads=["ps%d" % pb], writes=["qTt"])
        for hp in range(16):
            S.op("pe", lambda e, hp=hp: e.matmul(psO[:, hp * 128:(hp + 1) * 128], qTt[:, hp, :], keysT[:, hp, :], start=True, stop=True),
                 reads=["qTt", "keysT"], writes=["psO"])
        sc = sct[sl]
        S.op("act", lambda e, sc=sc: e.copy(out=sc.rearrange("p h s n -> p (h s n)"), in_=psO[:, :]), reads=["psO"], writes=["sct0"])
        S.dma(sc_d[ti * 128:(ti + 1) * 128, :], sc.rearrange("p h s n -> p (h s n)"), reads=["sct0"], writes=["sc_d%d" % ti],
              semkey="scst%d" % sl)
        for h in range(8):
            for p_ in range(2):
                S.op("dve", lambda e, h=h, p_=p_, sc=sc: e.max(out=tv[:, h, p_, 0:8], in_=sc[:, h, p_, :]), reads=["sct0"], writes=["tv"])
                S.op("dve", lambda e, h=h, p_=p_, sc=sc: e.match_replace(out=wk[:, 0:128], in_to_replace=tv[:, h, p_, 0:8],
                                                                         in_values=sc[:, h, p_, :], imm_value=-1e30),
                     reads=["sct0", "tv"], writes=["wk"])
                S.op("dve", lambda e, h=h, p_=p_: e.max(out=tv[:, h, p_, 8:16], in_=wk[:, 0:128]), reads=["wk"], writes=["tv"])
        S.op("dve", lambda e: e.tensor_tensor(out=cand, in0=tv[:, :, 0, :].unsqueeze(3).to_broadcast([128, 8, 16, 16]),
                                              in1=tv[:, :, 1, :].unsqueeze(2).to_broadcast([128, 8, 16, 16]), op=ALU.add),
             reads=["tv"], writes=["cand"])
        for h in range(8):
            ch = cand[:, h, :, :].rearrange("p a b -> p (a b)")
            S.op("dve", lambda e, h=h, ch=ch: e.max(out=bv[:, h, 0:8], in_=ch), reads=["cand"], writes=["bv"])
            S.op("dve", lambda e, h=h, ch=ch: e.match_replace(out=wk, in_to_replace=bv[:, h, 0:8], in_values=ch, imm_value=-1e30),
                 reads=["cand", "bv"], writes=["wk"])
            S.op("dve", lambda e, h=h: e.max(out=bv[:, h, 8:16], in_=wk), reads=["wk"], writes=["bv"])
        S.op("dve", lambda e, ti=ti: e.tensor_copy(out=thr[:, ti, :], in_=bv[:, :, 15]), reads=["bv"], writes=["thr"])
        S.op("dve", lambda e: e.tensor_scalar(out=nmx, in0=bv[:, :, 0], scalar1=-1.0, scalar2=None, op0=ALU.mult),
             reads=["bv"], writes=["nmx"])
        for h in range(8):
            S.op("act", lambda e, h=h: e.activation(out=ez, in_=bv[:, h, :], func=AF.Exp, bias=nmx[:, h:h + 1],
                                                    accum_out=zz[:, h:h + 1]), reads=["bv", "nmx"], writes=["ez", "zz"])
        S.op("act", lambda e: e.activation(out=zz, in_=zz, func=AF.Ln), reads=["zz"], writes=["zz"])
        S.op("dve", lambda e, ti=ti: e.tensor_tensor(out=nb[:, ti, :], in0=nmx, in1=zz, op=ALU.subtract), reads=["nmx", "zz"], writes=["nb"])
    dbg("thr", thr, ["thr"])
    dbg("nb", nb, ["nb"])
    dbg("sct", sct[1].rearrange("p h s n -> p (h s n)"), ["sct0"])

    if stop == "B0":
        return fin()
    S.barrier()
    A.release("wqp", "wstp_0", "wstp_1", "keysT", "kst", "xtp0", "xtp1", "hb_p", "ss_p", "qTt", "sct0", "tv", "wk", "cand",
              "bv", "ez", "zz", "nmx", "gscf", "shf", "cT", "cs_")
    NE = GE // 128
    acc = A.alloc("acc", [128, TP, D])
    sc4 = A.alloc("sc4", [128, TP, 8, 2, 128])
    uraw = [A.alloc("uraw%d" % i, [128, D]) for i in range(2)]
    ubf = [A.alloc("ubf%d" % i, [128, D], BF16) for i in range(2)]
    UT = A.alloc("UT", [128, 16, GE], BF16)
    vraw = [A.alloc("vraw%d" % i, [128, D]) for i in range(2)]
    Vg = A.alloc("Vg", [128, NE, D], BF16)
    NB = 3
    Sg = [A.alloc("Sg%d" % i, [128, NE, 128]) for i in range(NB)]
    Eg = [A.alloc("Eg%d" % i, [128, NE, 128], BF16) for i in range(NB)]
    Tg = [A.alloc("Tg%d" % i, [128, NE, 128], BF16) for i in range(NB)]
    Gg = [A.alloc("Gg%d" % i, [128, GE]) for i in range(2)]
    ga = A.alloc("ga", [128, GE], BF16)
    actv = A.alloc("actv", [128, GE], BF16)
    actT = A.alloc("actT", [128, NE, 128], BF16)
    ss_f = A.alloc("ss_f", [128, 2])
    gfin = vraw[0]
    def prep_group(g):
        for e_ in range(NE):
            sl = (g * NE + e_) % 2
            r0 = (g * NE + e_) * 128
            S.dma(uraw[sl], peer_u[r0:r0 + 128, :], writes=["uraw%d" % sl])
            S.dma(vraw[sl], peer_v[r0:r0 + 128, :], writes=["vraw%d" % sl])
            S.op("act", lambda e, sl=sl: e.copy(out=ubf[sl], in_=uraw[sl]), reads=["uraw%d" % sl], writes=["ubf%d" % sl])
            veng = "act" if e_ % 2 == 0 else "pool"
            if veng == "act":
                S.op("act", lambda e, sl=sl, e_=e_: e.copy(out=Vg[:, e_, :], in_=vraw[sl]), reads=["vraw%d" % sl], writes=["Vg"])
            else:
                S.op("pool", lambda e, sl=sl, e_=e_: e.tensor_copy(out=Vg[:, e_, :], in_=vraw[sl]), reads=["vraw%d" % sl], writes=["Vg"])
            for half in range(2):
                pt = psT(2 + half)
                for k in range(8):
                    S.op("pe", lambda e, k=k, half=half, pt=pt, sl=sl: e.transpose(out=pt[:, k * 128:(k + 1) * 128],
                                                                                  in_=ubf[sl][:, (half * 8 + k) * 128:(half * 8 + k + 1) * 128],
                                                                                  identity=ident_b),
                         reads=["ubf%d" % sl, "ident_b"], writes=["ps%d" % (2 + half)])
                S.op("act", lambda e, half=half, pt=pt, e_=e_: e.copy(out=UT[:, half * 8:(half + 1) * 8, e_ * 128:(e_ + 1) * 128],
                                                                      in_=pt[:, 0:1024].rearrange("p (k t) -> p k t", k=8)),
                     reads=["ps%d" % (2 + half)], writes=["UT"])

    def stage1(k, p, g, tt):
        ti = p * TP + tt
        b = k % 2
        c0 = g * NE
        for kc in range(16):
            S.op("pe", lambda e, kc=kc: e.matmul(ps[b][:, 0:GE], h2T[:, kc, ti * 128:(ti + 1) * 128], UT[:, kc, :],
                                                  start=(kc == 0), stop=(kc == 15)),
                 reads=["h2T", "UT"], writes=["ps%d" % b])
        def sg_op(h):
            q = h % NB
            S.op("dve", lambda e, h=h, q=q: e.tensor_tensor(
                out=Sg[q], in0=sc4[:, tt, h, 0, c0:c0 + NE].unsqueeze(2).to_broadcast([128, NE, 128]),
                in1=sc4[:, tt, h, 1, :].unsqueeze(1).to_broadcast([128, NE, 128]), op=ALU.add),
                reads=["sc4"], writes=["Sg%d" % q])
        sg_op(0)
        for h in range(8):
            q = h % NB
            if h + 1 < 8:
                sg_op(h + 1)
            S.op("act", lambda e, h=h, q=q: e.activation(out=Eg[q], in_=Sg[q], func=AF.Exp, bias=nb[:, ti, h:h + 1]),
                 reads=["Sg%d" % q, "nb"], writes=["Eg%d" % q])
            if h == 0:
                S.op("dve", lambda e, h=h, q=q: e.scalar_tensor_tensor(
                    out=Gg[b].rearrange("p (a n) -> p a n", a=NE), in0=Sg[q], scalar=thr[:, ti, h:h + 1], in1=Eg[q],
                    op0=ALU.is_ge, op1=ALU.mult), reads=["Sg%d" % q, "Eg%d" % q, "thr"], writes=["Gg%d" % b])
            else:
                S.op("dve", lambda e, h=h, q=q: e.scalar_tensor_tensor(
                    out=Tg[q], in0=Sg[q], scalar=thr[:, ti, h:h + 1], in1=Eg[q],
                    op0=ALU.is_ge, op1=ALU.mult), reads=["Sg%d" % q, "Eg%d" % q, "thr"], writes=["Tg%d" % q])
                S.op("pool", lambda e, q=q: e.tensor_tensor(out=Gg[b], in0=Gg[b], in1=Tg[q].rearrange("p a n -> p (a n)"),
                                                             op=ALU.add), reads=["Gg%d" % b, "Tg%d" % q], writes=["Gg%d" % b])
        S.op("act", lambda e: e.activation(out=ga, in_=ps[b][:, 0:GE], func=AF.Gelu), reads=["ps%d" % b], writes=["ga"])
        S.op("dve", lambda e: e.tensor_tensor(out=actv, in0=ga, in1=Gg[b], op=ALU.mult), reads=["ga", "Gg%d" % b], writes=["actv"])

    def stage2a(k):
        b = k % 2
        pt = psT(2 + b)
        for e_ in range(NE):
            S.op("pe", lambda e, e_=e_: e.transpose(out=pt[:, e_ * 128:(e_ + 1) * 128], in_=actv[:, e_ * 128:(e_ + 1) * 128],
                                                     identity=ident_b), reads=["actv", "ident_b"], writes=["ps%d" % (2 + b)])
        S.op("act", lambda e: e.copy(out=actT, in_=pt[:, 0:GE].rearrange("p (k t) -> p k t", k=NE)),
             reads=["ps%d" % (2 + b)], writes=["actT"])

    def stage2b(k, g, tt):
        for nq in range(4):
            for e_ in range(NE):
                S.op("pe", lambda e, e_=e_, nq=nq: e.matmul(psOb[nq], actT[:, e_, :], Vg[:, e_, nq * 512:(nq + 1) * 512],
                                                            start=(e_ == 0), stop=(e_ == NE - 1)),
                     reads=["actT", "Vg"], writes=["psO"])
        if g == 0:
            S.op("dve", lambda e: e.tensor_copy(out=acc[:, tt, :], in_=psO[:, :]), reads=["psO"], writes=["acc%d" % tt])
        else:
            S.op("dve", lambda e: e.tensor_tensor(out=acc[:, tt, :], in0=acc[:, tt, :], in1=psO[:, :], op=ALU.add),
                 reads=["psO", "acc%d" % tt], writes=["acc%d" % tt])

    NG = NEXP // GE
    kk = 0
    for p in range(NPASS):
        for tt in range(TP):
            ti = p * TP + tt
            S.dma(sc4[:, tt].rearrange("p h s n -> p (h s n)"), sc_d[ti * 128:(ti + 1) * 128, :], reads=["sc_d%d" % ti], writes=["sc4"],
                  semkey="sc4_%d" % tt)
        iters = [(g, tt) for g in range(NG) for tt in range(TP)]
        prep_group(0)
        stage1(kk, p, 0, 0)
        for n_, (g, tt) in enumerate(iters):
            nxt = iters[n_ + 1] if n_ + 1 < len(iters) else None
            stage2a(kk)
            if nxt is not None and nxt[0] == g:
                stage1(kk + 1, p, nxt[0], nxt[1])
                stage2b(kk, g, tt)
            else:
                stage2b(kk, g, tt)
                if nxt is not None:
                    prep_group(nxt[0])
                    stage1(kk + 1, p, nxt[0], nxt[1])
            kk += 1
        S.barrier()
        S.dma(gfin, g_final.partition_broadcast(128), writes=["vraw0"])
        for tt in range(TP):
            ti = p * TP + tt
            a_t = acc[:, tt, :]
            xb_ = uraw[tt % 2]
            S.dma(xb_, x2_d[ti * 128:(ti + 1) * 128, :], reads=["x2_d%d" % ti], writes=["uraw%d" % (tt % 2)])
            S.op("dve", lambda e, a_t=a_t: e.tensor_tensor(out=a_t, in0=a_t, in1=gtf, op=ALU.mult),
                 reads=["acc%d" % tt, "gtf"], writes=["acc%d" % tt])
            S.op("pool", lambda e, a_t=a_t, xb_=xb_: e.tensor_tensor(out=a_t, in0=a_t, in1=xb_, op=ALU.add),
                 reads=["acc%d" % tt, "uraw%d" % (tt % 2)], writes=["acc%d" % tt])
            S.op("act", lambda e, a_t=a_t: e.activation(out=ubf[0], in_=a_t, func=AF.Square, accum_out=ss_f[:, 0:1]),
                 reads=["acc%d" % tt], writes=["ubf0", "ss"])
            rstd_from(ss_f[:, 0:1], ss_f[:, 0:1], D, ["ss"], ["ss"])
            S.op("dve", lambda e, a_t=a_t: e.scalar_tensor_tensor(out=a_t, in0=a_t, scalar=ss_f[:, 0:1], in1=gfin, op0=ALU.mult,
                                                                   op1=ALU.mult), reads=["acc%d" % tt, "ss", "vraw0"], writes=["acc%d" % tt])
            S.dma(y_out[ti * 128:(ti + 1) * 128, :], a_t, reads=["acc%d" % tt], semkey="yout%d" % tt)
        S.barrier()
    return fin()


_CACHE = {}


def _host_inputs(x, c, w):
    ident = np.eye(128, dtype=np.float32)
    inv = 1.0 / (10000.0 ** (np.arange(0, 64, 2, dtype=np.float32) / 64.0))
    maps = []
    for core in range(8):
        b, s = core // 2, core % 2
        pos_own = np.arange(s * NT, (s + 1) * NT, dtype=np.float32)
        pos_prev = np.arange(0, NT, dtype=np.float32)
        tabs = []
        for pos in (pos_prev, pos_own):
            ang = inv[:, None] * pos[None, :]
            cs, sn = np.cos(ang).astype(np.float32), np.sin(ang).astype(np.float32)
            tabs.append(np.concatenate([cs, cs], 0))
            tabs.append(np.concatenate([-sn, sn], 0))
        consts = np.zeros((128, 2), np.float32)
        consts[:, 0] = float(s)
        consts[:, 1] = 0.0 if s == 1 else -30000.0
        m = dict(w)
        m["x_own"] = np.ascontiguousarray(x[b, s * NT:(s + 1) * NT])
        m["x_prev"] = np.ascontiguousarray(x[b, 0:NT])
        m["c_vec"] = np.ascontiguousarray(c[b].reshape(16, 128).T)
        m["consts"] = consts
        m["rope"] = np.stack(tabs).astype(np.float32)
        m["ident"] = ident
        maps.append(m)
    return maps


def kernel(x, c, w_ada, b_ada, g_norm_mix, w_in, conv_w, g_q_lat, w_uq, g_kv_lat, w_ukv, g_out_conv, g_out_attn,
           w_out, g_norm_ffn, peer_w_q, peer_sub_keys, peer_u, peer_v, g_final, _debug=(), _stop=None):
    f = lambda a: np.ascontiguousarray(np.asarray(a, dtype=np.float32))
    x, c = f(x), f(c)
    w = {
        "w_ada": f(w_ada)[0], "b_ada": f(b_ada)[0], "g_norm_mix": f(g_norm_mix)[0], "w_in": f(w_in)[0],
        "conv_w": f(conv_w)[0], "g_q_lat": f(g_q_lat)[0], "w_uq": f(w_uq)[0], "g_kv_lat": f(g_kv_lat)[0],
        "w_ukv": f(w_ukv)[0], "g_out_conv": f(g_out_conv)[0], "g_out_attn": f(g_out_attn)[0], "w_out": f(w_out)[0],
        "g_norm_ffn": f(g_norm_ffn)[0], "peer_w_q": f(peer_w_q)[0],
        "peer_keys": f(peer_sub_keys)[0].reshape(16 * 128, 128), "peer_u": f(peer_u)[0], "peer_v": f(peer_v)[0],
        "g_final": f(g_final),
    }
    key = (tuple(_debug), _stop)
    if key not in _CACHE:
        _CACHE[key] = build(debug=_debug, stop=_stop)
    nc, dbg_outs = _CACHE[key]
    maps = _host_inputs(x, c, w)
    res = run_bass_kernel_spmd(nc, maps, core_ids=list(range(8)))
    out = np.empty((4, SEQ, D), np.float32)
    for core in range(8):
        b, s = core // 2, core % 2
        out[b, s * NT:(s + 1) * NT] = res.results[core]["y_out"]
    if _debug:
        return out, [{k: r[k] for k in dbg_outs} for r in res.results]
    return out
```

```python
import numpy as np
from contextlib import ExitStack
import concourse.bass as bass
import concourse.mybir as mybir
from concourse.bass_utils import run_bass_kernel_spmd

F32 = mybir.dt.float32
BF16 = mybir.dt.bfloat16
AF = mybir.ActivationFunctionType
ALU = mybir.AluOpType

D = 2048
SEQ = 2048
NT = 1024
EPS = 1e-6
NEXP = 16384
GE = 512
NPASS = 2
TP = 8 // NPASS
SCALE = 192 ** -0.5
POOL_AS = None


class Sched:
    ENG = ("pe", "act", "dve", "pool", "sp")

    def __init__(self, nc):
        self.nc = nc
        self.streams = {e: [] for e in self.ENG}
        self.sems = {}
        self._sem_ctx = []
        self.count = {}
        self.seen = {e: {} for e in self.ENG}
        self.dep = {}
        for e in ("pe", "act", "dve", "pool"):
            self._mksem("E_" + e)

    def _mksem(self, name):
        if name not in self.sems:
            ctx = self.nc.semaphore(name)
            self.sems[name] = ctx.__enter__()
            self._sem_ctx.append(ctx)
            self.count[name] = 0
        return name

    def close(self):
        for ctx in reversed(self._sem_ctx):
            ctx.__exit__(None, None, None)

    def _d(self, k):
        return self.dep.setdefault(k, {"w": {}, "r": {}})

    def _collect(self, reads, writes):
        need = {}
        for k in reads:
            for s, v in self._d(k)["w"].items():
                need[s] = max(need.get(s, 0), v)
        for k in writes:
            d = self._d(k)
            for dd in (d["w"], d["r"]):
                for s, v in dd.items():
                    need[s] = max(need.get(s, 0), v)
        return need

    def _emit_waits(self, eng, need):
        own = "E_" + eng
        for s, v in need.items():
            if s == own and eng == "pe":
                continue
            if self.seen[eng].get(s, 0) >= v:
                continue
            self.seen[eng][s] = v
            self.streams[eng].append(("wait", s, v))

    def _record(self, reads, writes, s, v):
        for k in writes:
            d = self._d(k)
            d["w"] = {s: v}
            d["r"] = {}
        for k in reads:
            if k in writes:
                continue
            d = self._d(k)
            d["r"][s] = max(d["r"].get(s, 0), v)

    def op(self, eng, fn, reads=(), writes=()):
        if eng == "pool" and POOL_AS is not None:
            eng = POOL_AS
        self._emit_waits(eng, self._collect(reads, writes))
        s = "E_" + eng
        self.count[s] += 1
        self.streams[eng].append(("op", fn, s, 1))
        self._record(reads, writes, s, self.count[s])

    def dma(self, out_ap, in_ap, reads=(), writes=(), semkey=None):
        self._emit_waits("sp", self._collect(reads, writes))
        s = self._mksem("D_" + str(semkey if semkey is not None else (list(writes) + list(reads))[0]))
        self.count[s] += 16
        self.streams["sp"].append(("op", lambda e, o=out_ap, i=in_ap: e.dma_start(out=o, in_=i), s, 16))
        self._record(reads, writes, s, self.count[s])

    def barrier(self):
        for eng in self.ENG:
            for s, v in self.count.items():
                if v > 0 and self.seen[eng].get(s, 0) < v and not (eng == "pe" and s == "E_pe"):
                    self.seen[eng][s] = v
                    self.streams[eng].append(("wait", s, v))

    def final_wait(self, eng="sp"):
        for s, v in self.count.items():
            if v > 0 and self.seen[eng].get(s, 0) < v:
                self.seen[eng][s] = v
                self.streams[eng].append(("wait", s, v))

    def emit(self):
        names = {"pe": "tensor", "act": "scalar", "dve": "vector", "pool": "gpsimd", "sp": "sync"}
        with self.nc.Block() as block:
            for e in self.ENG:
                stream = self.streams[e]

                def body(engine, stream=stream):
                    for it in stream:
                        if it[0] == "wait":
                            engine.wait_ge(self.sems[it[1]], it[2])
                        else:
                            it[1](engine).then_inc(self.sems[it[2]], it[3])
                getattr(block, names[e])(body)


class Arena:
    def __init__(self, nc, stack, nbytes=206000):
        self.words = nbytes // 4
        self.t = stack.enter_context(nc.sbuf_tensor("arena", [128, self.words], F32))
        self.free = [(0, self.words)]
        self.live = {}

    def alloc(self, name, shape, dt=F32, hi=False):
        n = 1
        for s in shape[1:]:
            n *= s
        w = (n * (4 if dt == F32 else 2) + 3) // 4
        w = (w + 15) // 16 * 16
        order = range(len(self.free) - 1, -1, -1) if hi else range(len(self.free))
        for i in order:
            o, sz = self.free[i]
            if sz >= w:
                if hi:
                    self.free[i] = (o, sz - w)
                    o = o + sz - w
                else:
                    self.free[i] = (o + w, sz - w)
                break
        else:
            raise RuntimeError("arena full allocating %s (%d words); free=%s live=%s" % (name, w, self.free, sorted(self.live)))
        self.live[name] = (o, w)
        ap = self.t[0:shape[0], o:o + w]
        if dt != F32:
            ap = ap.bitcast(dt)
        ap = ap[:, 0:n]
        if len(shape) == 3:
            ap = ap.rearrange("p (a b) -> p a b", a=shape[1])
        elif len(shape) == 4:
            ap = ap.rearrange("p (a b c) -> p a b c", a=shape[1], b=shape[2])
        elif len(shape) == 5:
            ap = ap.rearrange("p (a b c d) -> p a b c d", a=shape[1], b=shape[2], c=shape[3])
        return ap

    def release(self, *names):
        for name in names:
            o, w = self.live.pop(name)
            self.free.append((o, w))
        self.free.sort()
        merged = []
        for o, w in self.free:
            if w == 0:
                continue
            if merged and merged[-1][0] + merged[-1][1] == o:
                merged[-1] = (merged[-1][0], merged[-1][1] + w)
            else:
                merged.append((o, w))
        self.free = merged

def build(debug=(), stop=None):
    nc = bass.Bass("TRN2", target_bir_lowering=False)
    S = Sched(nc)
    top = ExitStack()
    dbg_outs = []

    def din(name, shape):
        return nc.dram_tensor(name, list(shape), F32, kind="ExternalInput").ap()

    x_own = din("x_own", [NT, D])
    x_prev = din("x_prev", [NT, D])
    c_vec = din("c_vec", [128, 16])
    consts = din("consts", [128, 2])
    rope = din("rope", [4, 64, NT])
    ident_d = din("ident", [128, 128])
    w_ada = din("w_ada", [D, 6 * D])
    b_ada = din("b_ada", [6 * D])
    g_norm_mix = din("g_norm_mix", [D])
    w_in = din("w_in", [D, 4160])
    conv_w = din("conv_w", [3, 1024])
    g_q_lat = din("g_q_lat", [512])
    w_uq = din("w_uq", [512, 1536])
    g_kv_lat = din("g_kv_lat", [512])
    w_ukv = din("w_ukv", [512, 2048])
    g_out_conv = din("g_out_conv", [1024])
    g_out_attn = din("g_out_attn", [1024])
    w_out = din("w_out", [D, D])
    g_norm_ffn = din("g_norm_ffn", [D])
    peer_w_q = din("peer_w_q", [D, D])
    peer_keys = din("peer_keys", [16 * 128, 128])
    peer_u = din("peer_u", [NEXP, D])
    peer_v = din("peer_v", [NEXP, D])
    g_final = din("g_final", [D])
    y_out = nc.dram_tensor("y_out", [NT, D], F32, kind="ExternalOutput").ap()
    x2_d = nc.dram_tensor("x2_scratch", [NT, D], F32, kind="Internal").ap()
    sc_d = nc.dram_tensor("sc_scratch", [NT, D], F32, kind="Internal").ap()

    A = Arena(nc, top)

    def fin():
        S.final_wait("sp")
        with nc.allow_non_contiguous_dma(reason="tiny per-partition parameter vectors"):
            S.emit()
        S.close()
        return nc, dbg_outs

    def dbg(name, ap, reads):
        if name in debug:
            o = nc.dram_tensor("dbg_" + name, list(ap.shape), ap.dtype, kind="ExternalOutput").ap()
            S.dma(o, ap, reads=reads, semkey="dbg_" + name)
            dbg_outs.append("dbg_" + name)

    ps = [top.enter_context(nc.psum_tensor("ps%d" % i, [128, 512], F32)) for i in range(4)]
    psO = top.enter_context(nc.psum_tensor("psO", [128, 2048], F32))
    psOb = [psO[:, i * 512:(i + 1) * 512] for i in range(4)]

    def psT(i):
        return ps[i][:].bitcast(BF16)

    ident_f = A.alloc("ident_f", [128, 128])
    ident_b = A.alloc("ident_b", [128, 128], BF16)
    ones_f = A.alloc("ones_f", [128, 128])
    ones_b = A.alloc("ones_b", [128, 128], BF16)
    cst = A.alloc("cst", [128, 2])
    cT = A.alloc("cT", [128, 16, 128])
    cs_ = A.alloc("cs_", [128, 16])
    prm = A.alloc("prm", [128, 64])

    S.dma(ident_f, ident_d, writes=["ident_f"])
    S.dma(cst, consts, writes=["cst"])
    S.dma(cs_, c_vec, writes=["cs_"])
    S.dma(prm[:, 0:24].rearrange("p (k j) -> p k j", k=3), conv_w.rearrange("k (j p) -> p k j", p=128),
          writes=["prm_a"])
    S.dma(prm[:, 24:32], g_out_conv.rearrange("(j p) -> p j", p=128), writes=["prm_b"])
    S.dma(prm[:, 32:40], g_out_attn.rearrange("(j p) -> p j", p=128), writes=["prm_c"])
    S.dma(prm[:, 40:44], g_q_lat.rearrange("(j p) -> p j", p=128), writes=["prm_d"])
    S.dma(prm[:, 44:48], g_kv_lat.rearrange("(j p) -> p j", p=128), writes=["prm_e"])
    PRM = ["prm_a", "prm_b", "prm_c", "prm_d", "prm_e"]
    S.op("act", lambda e: e.copy(out=ident_b, in_=ident_f), reads=["ident_f"], writes=["ident_b"])
    S.op("pool", lambda e: e.memset(ones_f, 1.0), writes=["ones_f"])
    S.op("pool", lambda e: e.memset(ones_b, 1.0), writes=["ones_b"])
    S.op("act", lambda e: e.activation(out=cs_, in_=cs_, func=AF.Silu), reads=["cs_"], writes=["cs_"])
    for kc in range(16):
        S.op("dve", lambda e, kc=kc: e.tensor_copy(out=cT[:, kc, :], in_=cs_[:, kc:kc + 1].to_broadcast([128, 128])),
             reads=["cs_"], writes=["cT"])

    def rstd_from(out_ap, in_ap, n, reads, writes):
        S.op("act", lambda e: e.activation(out=out_ap, in_=in_ap, func=AF.Sqrt, scale=1.0 / n, bias=EPS),
             reads=reads, writes=writes)
        S.op("dve", lambda e: e.reciprocal(out=out_ap, in_=out_ap), reads=writes, writes=writes)

    def compute_mod(col0, ncols, dst, dst_key):
        wa = [A.alloc("wa%d" % i, [128, 16, 256]) for i in range(2)]
        bb = [A.alloc("bb%d" % i, [128, 256]) for i in range(2)]
        for g in range(ncols // 256):
            sl = g % 2
            c0 = col0 + g * 256
            S.dma(wa[sl], w_ada[:, c0:c0 + 256].rearrange("(k p) c -> p k c", p=128), writes=["wa%d" % sl])
            S.dma(bb[sl], b_ada[c0:c0 + 256].partition_broadcast(128), writes=["bb%d" % sl])
            pb = ps[g % 2]
            for kc in range(16):
                S.op("pe", lambda e, kc=kc, sl=sl, pb=pb: e.matmul(pb[:, 0:256], cT[:, kc, :], wa[sl][:, kc, :],
                                                                     start=(kc == 0), stop=(kc == 15)),
                     reads=["cT", "wa%d" % sl], writes=["ps%d" % (g % 2)])
            S.op("dve", lambda e, g=g, sl=sl, pb=pb: e.tensor_tensor(out=dst[:, g * 256:(g + 1) * 256], in0=pb[:, 0:256],
                                                                      in1=bb[sl], op=ALU.add),
                 reads=["ps%d" % (g % 2), "bb%d" % sl], writes=[dst_key])
        S.barrier()
        A.release("wa0", "wa1", "bb0", "bb1")

    def norm_tile(xt, xkey, hb, hbkey, gsc, sh, ss, gkeys):
        S.op("act", lambda e: e.activation(out=hb, in_=xt, func=AF.Square, accum_out=ss[:, 0:1]),
             reads=[xkey], writes=[hbkey, "ss"])
        rstd_from(ss[:, 0:1], ss[:, 0:1], D, ["ss"], ["ss"])
        S.op("dve", lambda e: e.scalar_tensor_tensor(out=xt, in0=xt, scalar=ss[:, 0:1], in1=gsc, op0=ALU.mult,
                                                      op1=ALU.mult), reads=[xkey, "ss"] + gkeys, writes=[xkey])
        S.op("pool", lambda e: e.tensor_tensor(out=hb, in0=xt, in1=sh, op=ALU.add), reads=[xkey] + gkeys,
             writes=[hbkey])

    def transpose_tile(hb, hbkey, dstT, dkey, col0):
        for half in range(2):
            bank = 2 + half
            pt = psT(bank)
            for k in range(8):
                kc = half * 8 + k
                S.op("pe", lambda e, kc=kc, k=k, pt=pt: e.transpose(out=pt[:, k * 128:(k + 1) * 128],
                                                                     in_=hb[:, kc * 128:(kc + 1) * 128], identity=ident_b),
                     reads=[hbkey, "ident_b"], writes=["ps%d" % bank])
            S.op("act", lambda e, half=half, pt=pt: e.copy(out=dstT[:, half * 8:(half + 1) * 8, col0:col0 + 128],
                                                            in_=pt[:, 0:1024].rearrange("p (k t) -> p k t", k=8)),
                 reads=["ps%d" % bank], writes=[dkey])

    gsc = A.alloc("gsc", [128, D])
    shm = A.alloc("shm", [128, D])
    compute_mod(0, D, shm, "shm")
    compute_mod(D, D, gsc, "gsc")
    gnm = A.alloc("gnm", [128, D])
    S.dma(gnm, g_norm_mix.partition_broadcast(128), writes=["gnm"])
    S.op("dve", lambda e: e.scalar_tensor_tensor(out=gsc, in0=gsc, scalar=1.0, in1=gnm, op0=ALU.add,
                                                  op1=ALU.mult), reads=["gsc", "gnm"], writes=["gsc"])
    S.barrier()
    A.release("gnm")
    dbg("shm", shm[0:1, :], ["shm"])
    dbg("gsc", gsc[0:1, :], ["gsc"])
    dbg("prm", prm, PRM)
    if stop == "mod":
        return fin()

    hT = A.alloc("hT", [128, 16, NT], BF16)
    hTh = A.alloc("hTh", [128, 16, 2], BF16)
    kvn = A.alloc("kvn", [128, 4, 2 * NT], BF16, hi=True)
    qn = A.alloc("qn", [128, 4, NT], BF16, hi=True)
    krope = A.alloc("krope", [64, 2 * NT], BF16, hi=True)
    rkv_bc = A.alloc("rkv_bc", [128, 2 * NT], hi=True)
    rq_bc = A.alloc("rq_bc", [128, NT], hi=True)
    ropet = A.alloc("ropet", [64, 2, NT])
    xts = [A.alloc("xt%d" % i, [128, D]) for i in range(2)]
    hb = A.alloc("hb", [128, D], BF16)
    ss = A.alloc("ss", [128, 2])
    wraw = A.alloc("wraw", [128, 8, 3, 128])
    wbf = [A.alloc("wbf%d" % i, [128, 16, 3, 128], BF16) for i in range(2)]
    sqb = A.alloc("sqb", [128, NT])

    wctr = [0]

    def load_w(cols_list, width, swap=False):
        sl = wctr[0] % 2
        wctr[0] += 1
        n = len(cols_list)
        for kh in range(2):
            for i, c0 in enumerate(cols_list):
                S.dma(wraw[:, :, i, 0:width],
                      w_in[kh * 1024:(kh + 1) * 1024, c0:c0 + width].rearrange("(k p) c -> p k c", p=128),
                      writes=["wraw%d" % i])
            if not swap:
                S.op("pool", lambda e, kh=kh: e.tensor_copy(out=wbf[sl][:, kh * 8:(kh + 1) * 8, 0:n, 0:width],
                                                            in_=wraw[:, :, 0:n, 0:width]),
                     reads=["wraw%d" % i for i in range(n)], writes=["wbf%d" % sl])
            else:
                S.op("pool", lambda e, kh=kh: e.tensor_copy(out=wbf[sl][:, kh * 8:(kh + 1) * 8, 0, 0:64], in_=wraw[:, :, 0, 0:64]),
                     reads=["wraw0"], writes=["wbf%d" % sl])
                S.op("pool", lambda e, kh=kh: e.tensor_copy(out=wbf[sl][:, kh * 8:(kh + 1) * 8, 1, 0:32], in_=wraw[:, :, 0, 32:64]),
                     reads=["wraw0"], writes=["wbf%d" % sl])
                S.op("pool", lambda e, kh=kh: e.tensor_copy(out=wbf[sl][:, kh * 8:(kh + 1) * 8, 1, 32:64], in_=wraw[:, :, 0, 0:32]),
                     reads=["wraw0"], writes=["wbf%d" % sl])
        return wbf[sl], "wbf%d" % sl

    def lin(pbank, wt, wkey, gi, width, rhs_fn, n):
        for kc in range(16):
            S.op("pe", lambda e, kc=kc: e.matmul(ps[pbank][0:width, 0:n], wt[:, kc, gi, 0:width], rhs_fn(kc),
                                                  start=(kc == 0), stop=(kc == 15)),
                 reads=[wkey, "hT", "hTh"], writes=["ps%d" % pbank])

    def lat_chunk(wt, wkey, dst, dkey, dcol0, gcol, stat_first, stat_last):
        for th in range(2):
            lin(0, wt, wkey, 0, 128, lambda kc, th=th: hT[:, kc, th * 512:(th + 1) * 512], 512)
            S.op("act", lambda e: e.copy(out=sqb[:, 512:1024], in_=ps[0][:, :]), reads=["ps0"], writes=["sqraw"])
            S.op("dve", lambda e, th=th: e.tensor_scalar(out=dst[:, dcol0 + th * 512: dcol0 + (th + 1) * 512],
                                                          in0=sqb[:, 512:1024], scalar1=prm[:, gcol:gcol + 1], scalar2=None,
                                                          op0=ALU.mult), reads=["sqraw"] + PRM, writes=[dkey])
            S.op("pool", lambda e: e.tensor_tensor(out=sqb[:, 0:512], in0=sqb[:, 512:1024], in1=sqb[:, 512:1024], op=ALU.mult),
                 reads=["sqraw"], writes=["sqb"])
            if stop == "kv0c":
                continue
            S.op("pe", lambda e, th=th: e.matmul(ps[2 + th][:, :], ones_f, sqb[:, 0:512], start=stat_first,
                                                  stop=stat_last), reads=["ones_f", "sqb"], writes=["ps%d" % (2 + th)])

    def krope_part(wt, wkey, tok0):
        for th in range(2):
            lin(0, wt, wkey, 0, 64, lambda kc, th=th: hT[:, kc, th * 512:(th + 1) * 512], 512)
            lin(1, wt, wkey, 1, 64, lambda kc, th=th: hT[:, kc, th * 512:(th + 1) * 512], 512)
            S.op("dve", lambda e, th=th: e.tensor_tensor(out=sqb[0:64, 0:512], in0=ps[0][0:64, :],
                                                          in1=ropet[:, 0, th * 512:(th + 1) * 512], op=ALU.mult),
                 reads=["ps0", "ropet"], writes=["sqb"])
            S.op("dve", lambda e, th=th: e.tensor_tensor(out=sqb[0:64, 512:1024], in0=ps[1][0:64, :],
                                                          in1=ropet[:, 1, th * 512:(th + 1) * 512], op=ALU.mult),
                 reads=["ps1", "ropet"], writes=["sqraw"])
            S.op("pool", lambda e, th=th: e.tensor_tensor(out=krope[:, tok0 + th * 512: tok0 + (th + 1) * 512],
                                                           in0=sqb[0:64, 0:512], in1=sqb[0:64, 512:1024], op=ALU.add),
                 reads=["sqb", "sqraw"], writes=["krope"])

    for part in range(2):
        src = x_prev if part == 0 else x_own
        S.dma(ropet, rope[2 * part:2 * part + 2].rearrange("f p t -> p f t"), writes=["ropet"])
        for ti in range(8):
            sl = ti % 2
            S.dma(xts[sl], src[ti * 128:(ti + 1) * 128, :], writes=["xt%d" % sl])
            norm_tile(xts[sl], "xt%d" % sl, hb, "hb", gsc, shm, ss, ["gsc", "shm"])
            transpose_tile(hb, "hb", hT, "hT", ti * 128)
            if stop == "A1a" and ti == 1:
                dbg("hT", hT[:, 3, 0:256], ["hT"])
                return fin()
        if stop == "A1":
            dbg("hT", hT[:, 3, :], ["hT"])
            return fin()
        tok0 = part * NT
        for qc in range(4):
            wt, wkey = load_w([3584 + qc * 128], 128)
            if stop == "kv0a":
                dbg("wbf", wt[:, :, 0, :], [wkey])
                return fin()
            lat_chunk(wt, wkey, kvn[:, qc, :], "kvn", tok0, 44 + qc, qc == 0, qc == 3)
            if stop in ("kv0", "kv0b", "kv0c"):
                dbg("kvn", kvn[:, 0, 0:NT], ["kvn"])
                return fin()
        for th in range(2):
            rstd_from(rkv_bc[:, tok0 + th * 512: tok0 + (th + 1) * 512], ps[2 + th][:, :], 512,
                      ["ps%d" % (2 + th)], ["rkv_bc"])
        wt, wkey = load_w([4096], 64, swap=True)
        krope_part(wt, wkey, tok0)
        if part == 0:
            S.op("pool", lambda e: e.tensor_copy(out=hTh, in_=hT[:, :, NT - 2:NT]), reads=["hT"], writes=["hTh"])
            continue
        for qc in range(4):
            wt, wkey = load_w([3072 + qc * 128], 128)
            lat_chunk(wt, wkey, qn[:, qc, :], "qn", 0, 40 + qc, qc == 0, qc == 3)
        for th in range(2):
            rstd_from(rq_bc[:, th * 512:(th + 1) * 512], ps[2 + th][:, :], 512, ["ps%d" % (2 + th)], ["rq_bc"])
    dbg("kvn", kvn[:, 0, :], ["kvn"])
    dbg("krope", krope, ["krope"])
    dbg("rkv", rkv_bc[0:1, :], ["rkv_bc"])
    dbg("qn", qn[:, 0, :], ["qn"])
    dbg("rq", rq_bc[0:1, :], ["rq_bc"])

    if stop == "A2":
        return fin()
    S.barrier()
    A.release("gsc", "shm", "xt0", "xt1", "hb", "ss")
    merged = A.alloc("merged", [128, 16, NT], BF16, hi=True)
    ctmp = A.alloc("ctmp", [128, 512])
    zbuf = A.alloc("zbuf", [128, NT + 2])
    ybuf = A.alloc("ybuf", [128, NT])
    bbuf = A.alloc("bbuf", [128, NT])
    rsb = A.alloc("rsb", [128, NT])
    xs4 = A.alloc("xs4", [128, 4])
    for j in range(8):
        wt, wkey = load_w([j * 128, 1024 + j * 128, 2048 + j * 128], 128)
        for gi, off in ((1, 0), (2, 2)):
            for kc in range(16):
                S.op("pe", lambda e, kc=kc, gi=gi, off=off, wt=wt: e.matmul(ps[3][:, off:off + 2], wt[:, kc, gi, :],
                                                                             hTh[:, kc, :], start=(kc == 0), stop=(kc == 15)),
                     reads=[wkey, "hTh"], writes=["ps3"])
            if gi == 1:
                S.op("act", lambda e: e.copy(out=xs4[:, 0:2], in_=ps[3][:, 0:2]), reads=["ps3"], writes=["xs4"])
            else:
                S.op("dve", lambda e: e.scalar_tensor_tensor(out=zbuf[:, 0:2], in0=xs4[:, 0:2], scalar=cst[:, 0:1],
                                                              in1=ps[3][:, 2:4], op0=ALU.mult, op1=ALU.mult),
                     reads=["xs4", "cst", "ps3"], writes=["zbuf"])
        for th in range(2):
            rf = lambda kc, th=th: hT[:, kc, th * 512:(th + 1) * 512]
            lin(0, wt, wkey, 1, 128, rf, 512)
            lin(1, wt, wkey, 2, 128, rf, 512)
            lin(2, wt, wkey, 0, 128, rf, 512)
            S.op("act", lambda e: e.copy(out=ctmp, in_=ps[0][:, :]), reads=["ps0"], writes=["ctmp"])
            S.op("dve", lambda e, th=th: e.tensor_tensor(out=zbuf[:, 2 + th * 512: 2 + (th + 1) * 512], in0=ctmp,
                                                          in1=ps[1][:, :], op=ALU.mult),
                 reads=["ctmp", "ps1"], writes=["zbuf"])
            S.op("act", lambda e, th=th: e.copy(out=bbuf[:, th * 512:(th + 1) * 512], in_=ps[2][:, :]),
                 reads=["ps2"], writes=["bbuf"])
        S.op("dve", lambda e, j=j: e.tensor_scalar(out=ybuf, in0=zbuf[:, 0:NT], scalar1=prm[:, j:j + 1],
                                                    scalar2=None, op0=ALU.mult), reads=["zbuf"] + PRM, writes=["ybuf"])
        for k in (1, 2):
            S.op("dve", lambda e, j=j, k=k: e.scalar_tensor_tensor(out=ybuf, in0=zbuf[:, k:NT + k],
                                                                    scalar=prm[:, 8 * k + j: 8 * k + j + 1], in1=ybuf,
                                                                    op0=ALU.mult, op1=ALU.add),
                 reads=["zbuf", "ybuf"] + PRM, writes=["ybuf"])
        S.op("pool", lambda e: e.tensor_tensor(out=ybuf, in0=ybuf, in1=bbuf, op=ALU.mult),
             reads=["ybuf", "bbuf"], writes=["ybuf"])
        S.op("act", lambda e: e.activation(out=sqb, in_=ybuf, func=AF.Square), reads=["ybuf"], writes=["sqb"])
        for th in range(2):
            S.op("pe", lambda e, th=th: e.matmul(ps[3][:, :], ones_f, sqb[:, th * 512:(th + 1) * 512], start=True,
                                                  stop=True), reads=["ones_f", "sqb"], writes=["ps3"])
            rstd_from(rsb[:, th * 512:(th + 1) * 512], ps[3][:, :], 128, ["ps3"], ["rsb"])
        S.op("dve", lambda e, j=j: e.scalar_tensor_tensor(out=merged[:, j, :], in0=ybuf, scalar=prm[:, 24 + j:25 + j],
                                                           in1=rsb, op0=ALU.mult, op1=ALU.mult),
             reads=["ybuf", "rsb"] + PRM, writes=["merged"])
    dbg("mconv", merged[:, 0, :], ["merged"])

    if stop == "conv":
        return fin()
    S.barrier()
    A.release("hT", "hTh", "wraw", "wbf0", "wbf1", "sqb", "ctmp", "zbuf", "ybuf", "bbuf", "rsb", "xs4")
    qTn = A.alloc("qTn", [128, 8, NT], BF16, hi=True)
    qTr = A.alloc("qTr", [64, 8, NT], BF16, hi=True)
    wq = A.alloc("wq", [128, 4, 1536], BF16)
    wqr = A.alloc("wqr", [128, 4, 8, 64], BF16)
    wst = A.alloc("wst", [128, 2048])
    cq = A.alloc("cq", [64, 2, NT])
    rtmp = A.alloc("rtmp", [64, 1024])
    for kc in range(4):
        S.dma(wst[:, 0:1536], w_uq[kc * 128:(kc + 1) * 128, :], writes=["wst"])
        S.op("pool", lambda e, kc=kc: e.tensor_copy(out=wq[:, kc, :], in_=wst[:, 0:1536]), reads=["wst"], writes=["wq"])
    wq4 = wq.rearrange("p k (h c) -> p k h c", h=8)
    S.op("pool", lambda e: e.tensor_copy(out=wqr[:, :, :, 0:32], in_=wq4[:, :, :, 160:192]), reads=["wq"], writes=["wqr"])
    S.op("pool", lambda e: e.tensor_copy(out=wqr[:, :, :, 32:64], in_=wq4[:, :, :, 128:160]), reads=["wq"], writes=["wqr"])
    for f in range(2):
        S.op("dve", lambda e, f=f: e.tensor_tensor(out=cq[:, f, :], in0=ropet[:, f, :], in1=rq_bc[0:64, :], op=ALU.mult),
             reads=["ropet", "rq_bc"], writes=["cq"])
    dbg("rq2", rq_bc[0:1, :], ["rq_bc"])
    dbg("wq", wq[:, 0, 0:192], ["wq"])
    dbg("wq3", wq[:, 3, 0:192], ["wq"])
    dbg("wst", wst[:, 0:192], ["wst"])
    if stop == "A3q0":
        return fin()
    for h in range(8):
        for th in range(2):
            tsl = slice(th * 512, (th + 1) * 512)
            for kc in range(4):
                S.op("pe", lambda e, kc=kc, h=h, tsl=tsl: e.matmul(ps[0][:, :], wq[:, kc, h * 192:h * 192 + 128], qn[:, kc, tsl],
                                                                   start=(kc == 0), stop=(kc == 3)),
                     reads=["wq", "qn"], writes=["ps0"])
            S.op("dve", lambda e, h=h, tsl=tsl: e.tensor_tensor(out=qTn[:, h, tsl], in0=ps[0][:, :], in1=rq_bc[:, tsl], op=ALU.mult),
                 reads=["ps0", "rq_bc"], writes=["qTn"])
            for kc in range(4):
                S.op("pe", lambda e, kc=kc, h=h, tsl=tsl: e.matmul(ps[1][0:64, :], wq[:, kc, h * 192 + 128:h * 192 + 192], qn[:, kc, tsl],
                                                                   start=(kc == 0), stop=(kc == 3)),
                     reads=["wq", "qn"], writes=["ps1"])
            for kc in range(4):
                S.op("pe", lambda e, kc=kc, h=h, tsl=tsl: e.matmul(ps[2][0:64, :], wqr[:, kc, h, :], qn[:, kc, tsl],
                                                                   start=(kc == 0), stop=(kc == 3)),
                     reads=["wqr", "qn"], writes=["ps2"])
            S.op("dve", lambda e, tsl=tsl: e.tensor_tensor(out=rtmp[:, 0:512], in0=ps[1][0:64, :], in1=cq[:, 0, tsl], op=ALU.mult),
                 reads=["ps1", "cq"], writes=["rtmp"])
            S.op("dve", lambda e, tsl=tsl: e.tensor_tensor(out=rtmp[:, 512:1024], in0=ps[2][0:64, :], in1=cq[:, 1, tsl], op=ALU.mult),
                 reads=["ps2", "cq"], writes=["rtmp"])
            S.op("pool", lambda e, h=h, tsl=tsl: e.tensor_tensor(out=qTr[:, h, tsl], in0=rtmp[:, 0:512], in1=rtmp[:, 512:1024], op=ALU.add),
                 reads=["rtmp"], writes=["qTr"])
        if stop == "A3q1":
            dbg("qTn", qTn[:, 0, :], ["qTn"])
            dbg("wqb", wq[:, 0, 0:192], ["wq"])
            dbg("qn2", qn[:, 0, :], ["qn"])
            return fin()
    dbg("qTn", qTn[:, 0, :], ["qTn"])
    dbg("qTr", qTr[:, 0, :], ["qTr"])

    S.barrier()
    A.release("qn", "rq_bc", "ropet", "wq", "wqr", "cq", "rtmp", "wst")
    kTn = A.alloc("kTn", [128, 8, 2 * NT], BF16, hi=True)
    vtm = A.alloc("vtm", [128, 16, 1024], BF16)
    rkv_tm = A.alloc("rkv_tm", [128, 16])
    wkv = A.alloc("wkv", [128, 4, 2048], BF16)
    wst_kv = A.alloc("wst_kv", [128, 2048])
    for kc in range(4):
        S.dma(wst_kv, w_ukv[kc * 128:(kc + 1) * 128, :], writes=["wst_kv"])
        S.op("pool", lambda e, kc=kc: e.tensor_copy(out=wkv[:, kc, :], in_=wst_kv), reads=["wst_kv"], writes=["wkv"])
    for blk in range(16):
        S.op("pe", lambda e, blk=blk: e.transpose(out=ps[3][:, 0:128], in_=rkv_bc[:, blk * 128:(blk + 1) * 128],
                                                   identity=ident_f), reads=["rkv_bc", "ident_f"], writes=["ps3"])
        S.op("act", lambda e, blk=blk: e.copy(out=rkv_tm[:, blk:blk + 1], in_=ps[3][:, 0:1]), reads=["ps3"], writes=["rkv_tm"])
    for h in range(8):
        for tc in range(4):
            tsl = slice(tc * 512, (tc + 1) * 512)
            pb = tc % 2
            for kc in range(4):
                S.op("pe", lambda e, kc=kc, h=h, tsl=tsl, pb=pb: e.matmul(ps[pb][:, :], wkv[:, kc, h * 256:h * 256 + 128], kvn[:, kc, tsl],
                                                                          start=(kc == 0), stop=(kc == 3)),
                     reads=["wkv", "kvn"], writes=["ps%d" % pb])
            S.op("dve", lambda e, h=h, tsl=tsl, pb=pb: e.tensor_tensor(out=kTn[:, h, tsl], in0=ps[pb][:, :], in1=rkv_bc[:, tsl], op=ALU.mult),
                 reads=["ps%d" % pb, "rkv_bc"], writes=["kTn"])
    wkv4 = wkv.rearrange("p k (h c) -> p k h c", h=8)
    for blk in range(16):
        for hg in range(2):
            pb = hg
            for kc in range(4):
                S.op("pe", lambda e, kc=kc, blk=blk, hg=hg, pb=pb: e.matmul(
                    ps[pb][:, :].rearrange("p (h c) -> p h c", h=4), kvn[:, kc, blk * 128:(blk + 1) * 128],
                    wkv4[:, kc, hg * 4:(hg + 1) * 4, 128:256], start=(kc == 0), stop=(kc == 3)),
                    reads=["wkv", "kvn"], writes=["ps%d" % pb])
            S.op("act", lambda e, blk=blk, hg=hg, pb=pb: e.activation(out=vtm[:, blk, hg * 512:(hg + 1) * 512], in_=ps[pb][:, :],
                                                                      func=AF.Copy, scale=rkv_tm[:, blk:blk + 1]),
                 reads=["ps%d" % pb, "rkv_tm"], writes=["vtm"])
    dbg("kTn", kTn[:, 0, :], ["kTn"])
    dbg("vtm", vtm[:, 0, :], ["vtm"])

    if stop == "A3":
        return fin()
    S.barrier()
    A.release("kvn", "rkv_bc", "wkv", "wst_kv", "rkv_tm")
    pTb = [A.alloc("pT%d" % i, [128, NT], BF16) for i in range(2)]
    oT = A.alloc("oT", [128, NT])
    rz = A.alloc("rz", [128, NT])
    sq4 = A.alloc("sq4", [128, NT])
    it = 0
    last_j = {0: 11, 1: 15}
    for h in range(8):
        for j in range(16):
            own = j >= 8
            jj = j - 8
            sl = it % 2
            it += 1
            ranges = []
            for th in range(2):
                lo = max(jj * 128, th * 512) if own else th * 512
                hi = (th + 1) * 512
                if lo < hi:
                    ranges.append((th, lo, hi))
            for (th, lo, hi) in ranges:
                n = hi - lo
                S.op("pe", lambda e, h=h, j=j, lo=lo, hi=hi, n=n, th=th: e.matmul(ps[th][:, 0:n], kTn[:, h, j * 128:(j + 1) * 128],
                                                                                 qTn[:, h, lo:hi], start=True, stop=False),
                     reads=["kTn", "qTn"], writes=["ps%d" % th])
                S.op("pe", lambda e, h=h, j=j, lo=lo, hi=hi, n=n, th=th: e.matmul(ps[th][:, 0:n], krope[:, j * 128:(j + 1) * 128],
                                                                                 qTr[:, h, lo:hi], start=False, stop=True),
                     reads=["krope", "qTr"], writes=["ps%d" % th])
                if own:
                    S.op("act", lambda e, lo=lo, hi=hi, n=n, th=th, sl=sl: e.activation(out=pTb[sl][:, lo:hi], in_=ps[th][:, 0:n],
                                                                                       func=AF.Exp, scale=SCALE),
                         reads=["ps%d" % th], writes=["pT%d" % sl])
                else:
                    S.op("act", lambda e, lo=lo, hi=hi, n=n, th=th, sl=sl: e.activation(out=pTb[sl][:, lo:hi], in_=ps[th][:, 0:n],
                                                                                       func=AF.Exp, scale=SCALE, bias=cst[:, 1:2]),
                         reads=["ps%d" % th, "cst"], writes=["pT%d" % sl])
            if own:
                S.op("pool", lambda e, jj=jj, sl=sl: e.memset(pTb[sl][64:128, jj * 128: jj * 128 + 64], 0.0),
                     reads=[], writes=["pT%d" % sl])
            for (th, lo, hi) in ranges:
                o0 = lo - th * 512
                n = hi - lo
                S.op("pe", lambda e, h=h, j=j, lo=lo, hi=hi, th=th, o0=o0, n=n, sl=sl: e.matmul(
                    psOb[th][:, o0:o0 + n], vtm[:, j, h * 128:(h + 1) * 128], pTb[sl][:, lo:hi],
                    start=(j == 0), stop=(j == last_j[th])), reads=["vtm", "pT%d" % sl], writes=["psO%d" % th])
                S.op("pe", lambda e, j=j, lo=lo, hi=hi, th=th, o0=o0, n=n, sl=sl: e.matmul(
                    psOb[2 + th][:, o0:o0 + n], ones_b, pTb[sl][:, lo:hi],
                    start=(j == 0), stop=(j == last_j[th])), reads=["ones_b", "pT%d" % sl], writes=["psO%d" % (2 + th)])
        for th in range(2):
            tsl = slice(th * 512, (th + 1) * 512)
            S.op("dve", lambda e, th=th, tsl=tsl: e.reciprocal(out=rz[:, tsl], in_=psOb[2 + th]), reads=["psO%d" % (2 + th)],
                 writes=["rz"])
            S.op("dve", lambda e, th=th, tsl=tsl: e.tensor_tensor(out=oT[:, tsl], in0=psOb[th], in1=rz[:, tsl], op=ALU.mult),
                 reads=["psO%d" % th, "rz"], writes=["oT"])
        S.op("act", lambda e: e.activation(out=sq4, in_=oT, func=AF.Square), reads=["oT"], writes=["sq4"])
        for th in range(2):
            tsl = slice(th * 512, (th + 1) * 512)
            S.op("pe", lambda e, tsl=tsl: e.matmul(ps[2][:, :], ones_f, sq4[:, tsl], start=True, stop=True),
                 reads=["ones_f", "sq4"], writes=["ps2"])
            rstd_from(rz[:, tsl], ps[2][:, :], 128, ["ps2"], ["rz"])
        S.op("dve", lambda e, h=h: e.scalar_tensor_tensor(out=merged[:, 8 + h, :], in0=oT, scalar=prm[:, 32 + h:33 + h],
                                                           in1=rz, op0=ALU.mult, op1=ALU.mult),
             reads=["oT", "rz"] + PRM, writes=["merged"])
    dbg("mattn", merged[:, 8, :], ["merged"])

    if stop == "A4":
        return fin()
    S.barrier()
    A.release("qTn", "qTr", "kTn", "vtm", "krope", "pT0", "pT1", "oT", "rz", "sq4")
    gtm = A.alloc("gtm", [128, D])
    compute_mod(2 * D, D, gtm, "gtm")
    wout = A.alloc("wout", [128, 16, D], BF16)
    wst_o = [A.alloc("wsto_%d" % i, [128, D]) for i in range(2)]
    xts_o = [A.alloc("xto%d" % i, [128, D]) for i in range(2)]
    x2t = [A.alloc("x2t%d" % i, [128, D]) for i in range(2)]
    for kc in range(16):
        sl = kc % 2
        S.dma(wst_o[sl], w_out[kc * 128:(kc + 1) * 128, :], writes=["wsto_%d" % sl])
        S.op("pool", lambda e, kc=kc, sl=sl: e.tensor_copy(out=wout[:, kc, :], in_=wst_o[sl]), reads=["wsto_%d" % sl], writes=["wout"])
    for ti in range(8):
        sl = ti % 2
        S.dma(xts_o[sl], x_own[ti * 128:(ti + 1) * 128, :], writes=["xto%d" % sl])
        for nq in range(4):
            pb = nq % 2
            for kc in range(16):
                S.op("pe", lambda e, kc=kc, ti=ti, nq=nq, pb=pb: e.matmul(ps[pb][:, :], merged[:, kc, ti * 128:(ti + 1) * 128],
                                                                          wout[:, kc, nq * 512:(nq + 1) * 512], start=(kc == 0), stop=(kc == 15)),
                     reads=["merged", "wout"], writes=["ps%d" % pb])
            S.op("dve", lambda e, nq=nq, pb=pb, sl=sl: e.tensor_tensor(out=x2t[sl][:, nq * 512:(nq + 1) * 512], in0=ps[pb][:, :],
                                                                       in1=gtm[:, nq * 512:(nq + 1) * 512], op=ALU.mult),
                 reads=["ps%d" % pb, "gtm"], writes=["x2t%d" % sl])
        S.op("pool", lambda e, sl=sl: e.tensor_tensor(out=x2t[sl], in0=x2t[sl], in1=xts_o[sl], op=ALU.add),
             reads=["x2t%d" % sl, "xto%d" % sl], writes=["x2t%d" % sl])
        S.dma(x2_d[ti * 128:(ti + 1) * 128, :], x2t[sl], reads=["x2t%d" % sl], writes=["x2_d%d" % ti], semkey="x2st%d" % sl)
    dbg("x2t", x2t[1], ["x2t1"])
    if stop == "A5":
        return fin()
    S.barrier()
    A.release("gtm", "wout", "wsto_0", "wsto_1", "xto0", "xto1", "x2t0", "x2t1", "merged")
    gscf = A.alloc("gscf", [128, D])
    shf = A.alloc("shf", [128, D])
    gtf = A.alloc("gtf", [128, D], hi=True)
    compute_mod(3 * D, D, shf, "shf")
    compute_mod(4 * D, D, gscf, "gscf")
    compute_mod(5 * D, D, gtf, "gtf")
    gnf = A.alloc("gnf", [128, D])
    S.dma(gnf, g_norm_ffn.partition_broadcast(128), writes=["gnf"])
    S.op("dve", lambda e: e.scalar_tensor_tensor(out=gscf, in0=gscf, scalar=1.0, in1=gnf, op0=ALU.add,
                                                  op1=ALU.mult), reads=["gscf", "gnf"], writes=["gscf"])
    S.barrier()
    A.release("gnf")
    h2T = A.alloc("h2T", [128, 16, NT], BF16, hi=True)
    thr = A.alloc("thr", [128, 8, 8], hi=True)
    nb = A.alloc("nb", [128, 8, 8], hi=True)
    wqp = A.alloc("wqp", [128, 16, D], BF16)
    wst_p = [A.alloc("wstp_%d" % i, [128, D]) for i in range(2)]
    keysT = A.alloc("keysT", [128, 16, 128], BF16)
    kst = A.alloc("kst", [128, 16, 128], BF16)
    xts_p = [A.alloc("xtp%d" % i, [128, D]) for i in range(2)]
    hb_p = A.alloc("hb_p", [128, D], BF16)
    ss_p = A.alloc("ss_p", [128, 2])
    qTt = A.alloc("qTt", [128, 16, 128], BF16)
    sct = [A.alloc("sct0", [128, 8, 2, 128])] * 2
    tv = A.alloc("tv", [128, 8, 2, 16])
    wk = A.alloc("wk", [128, 256])
    cand = A.alloc("cand", [128, 8, 16, 16])
    bv = A.alloc("bv", [128, 8, 16])
    ez = A.alloc("ez", [128, 16])
    zz = A.alloc("zz", [128, 8])
    nmx = A.alloc("nmx", [128, 8])
    for kc in range(16):
        sl = kc % 2
        S.dma(wst_p[sl], peer_w_q[kc * 128:(kc + 1) * 128, :], writes=["wstp_%d" % sl])
        S.op("pool", lambda e, kc=kc, sl=sl: e.tensor_copy(out=wqp[:, kc, :], in_=wst_p[sl]), reads=["wstp_%d" % sl], writes=["wqp"])
    S.barrier()
    S.dma(wst_p[0].rearrange("p (a b) -> p a b", a=16), peer_keys.rearrange("(a n) d -> n a d", n=128), writes=["wstp_0"])
    S.op("pool", lambda e: e.tensor_copy(out=kst, in_=wst_p[0].rearrange("p (a b) -> p a b", a=16)), reads=["wstp_0"], writes=["kst"])
    for half in range(2):
        pt = psT(2 + half)
        for k in range(8):
            S.op("pe", lambda e, k=k, half=half, pt=pt: e.transpose(out=pt[:, k * 128:(k + 1) * 128], in_=kst[:, half * 8 + k, :],
                                                                     identity=ident_b), reads=["kst", "ident_b"], writes=["ps%d" % (2 + half)])
        S.op("act", lambda e, half=half, pt=pt: e.copy(out=keysT[:, half * 8:(half + 1) * 8, :],
                                                        in_=pt[:, 0:1024].rearrange("p (k t) -> p k t", k=8)),
             reads=["ps%d" % (2 + half)], writes=["keysT"])
    for ti in range(8):
        sl = ti % 2
        S.dma(xts_p[sl], x2_d[ti * 128:(ti + 1) * 128, :], reads=["x2_d%d" % ti], writes=["xtp%d" % sl])
        norm_tile(xts_p[sl], "xtp%d" % sl, hb_p, "hb_p", gscf, shf, ss_p, ["gscf", "shf"])
        transpose_tile(hb_p, "hb_p", h2T, "h2T", ti * 128)
        for hp in range(16):
            pb = hp % 2
            for kc in range(16):
                S.op("pe", lambda e, kc=kc, hp=hp, ti=ti, pb=pb: e.matmul(ps[pb][:, 0:128], wqp[:, kc, hp * 128:(hp + 1) * 128],
                                                                          h2T[:, kc, ti * 128:(ti + 1) * 128], start=(kc == 0), stop=(kc == 15)),
                     reads=["wqp", "h2T"], writes=["ps%d" % pb])
            S.op("act", lambda e, hp=hp, pb=pb: e.copy(out=qTt[:, hp, :], in_=ps[pb][:, 0:128]), reads=["ps%d" % pb], writes=["qTt"])
        for hp in range(16):
            S.op("pe", lambda e, hp=hp: e.matmul(psO[:, hp * 128:(hp + 1) * 128], qTt[:, hp, :], keysT[:, hp, :], start=True, stop=True),
                 reads=["qTt", "keysT"], writes=["psO"])
        sc = sct[sl]
        S.op("act", lambda e, sc=sc: e.copy(out=sc.rearrange("p h s n -> p (h s n)"), in_=psO[:, :]), reads=["psO"], writes=["sct0"])
        S.dma(sc_d[ti * 128:(ti + 1) * 128, :], sc.rearrange("p h s n -> p (h s n)"), reads=["sct0"], writes=["sc_d%d" % ti],
              semkey="scst%d" % sl)
        for h in range(8):
            for p_ in range(2):
                S.op("dve", lambda e, h=h, p_=p_, sc=sc: e.max(out=tv[:, h, p_, 0:8], in_=sc[:, h, p_, :]), reads=["sct0"], writes=["tv"])
                S.op("dve", lambda e, h=h, p_=p_, sc=sc: e.match_replace(out=wk[:, 0:128], in_to_replace=tv[:, h, p_, 0:8],
                                                                         in_values=sc[:, h, p_, :], imm_value=-1e30),
                     reads=["sct0", "tv"], writes=["wk"])
                S.op("dve", lambda e, h=h, p_=p_: e.max(out=tv[:, h, p_, 8:16], in_=wk[:, 0:128]), reads=["wk"], writes=["tv"])
        S.op("dve", lambda e: e.tensor_tensor(out=cand, in0=tv[:, :, 0, :].unsqueeze(3).to_broadcast([128, 8, 16, 16]),
                                              in1=tv[:, :, 1, :].unsqueeze(2).to_broadcast([128, 8, 16, 16]), op=ALU.add),
             reads=["tv"], writes=["cand"])
        for h in range(8):
            ch = cand[:, h, :, :].rearrange("p a b -> p (a b)")
            S.op("dve", lambda e, h=h, ch=ch: e.max(out=bv[:, h, 0:8], in_=ch), reads=["cand"], writes=["bv"])
            S.op("dve", lambda e, h=h, ch=ch: e.match_replace(out=wk, in_to_replace=bv[:, h, 0:8], in_values=ch, imm_value=-1e30),
                 reads=["cand", "bv"], writes=["wk"])
            S.op("dve", lambda e, h=h: e.max(out=bv[:, h, 8:16], in_=wk), reads=["wk"], writes=["bv"])
        S.op("dve", lambda e, ti=ti: e.tensor_copy(out=thr[:, ti, :], in_=bv[:, :, 15]), reads=["bv"], writes=["thr"])
        S.op("dve", lambda e: e.tensor_scalar(out=nmx, in0=bv[:, :, 0], scalar1=-1.0, scalar2=None, op0=ALU.mult),
             reads=["bv"], writes=["nmx"])
        for h in range(8):
            S.op("act", lambda e, h=h: e.activation(out=ez, in_=bv[:, h, :], func=AF.Exp, bias=nmx[:, h:h + 1],
                                                    accum_out=zz[:, h:h + 1]), reads=["bv", "nmx"], writes=["ez", "zz"])
        S.op("act", lambda e: e.activation(out=zz, in_=zz, func=AF.Ln), reads=["zz"], writes=["zz"])
        S.op("dve", lambda e, ti=ti: e.tensor_tensor(out=nb[:, ti, :], in0=nmx, in1=zz, op=ALU.subtract), reads=["nmx", "zz"], writes=["nb"])
    dbg("thr", thr, ["thr"])
    dbg("nb", nb, ["nb"])
    dbg("sct", sct[1].rearrange("p h s n -> p (h s n)"), ["sct0"])

    if stop == "B0":
        return fin()
    S.barrier()
    A.release("wqp", "wstp_0", "wstp_1", "keysT", "kst", "xtp0", "xtp1", "hb_p", "ss_p", "qTt", "sct0", "tv", "wk", "cand",
              "bv", "ez", "zz", "nmx", "gscf", "shf", "cT", "cs_")
    NE = GE // 128
    acc = A.alloc("acc", [128, TP, D])
    sc4 = A.alloc("sc4", [128, TP, 8, 2, 128])
    uraw = [A.alloc("uraw%d" % i, [128, D]) for i in range(2)]
    ubf = [A.alloc("ubf0", [128, D], BF16)] * 2
    UT = A.alloc("UT", [128, 16, GE], BF16)
    vraw = [A.alloc("vraw%d" % i, [128, D]) for i in range(2)]
    Vg = A.alloc("Vg", [128, NE, D], BF16)
    NB = 2
    Sg = [A.alloc("Sg%d" % i, [128, NE, 128]) for i in range(NB)]
    Eg = [A.alloc("Eg%d" % i, [128, NE, 128], BF16) for i in range(NB)]
    Tm = [A.alloc("Tm%d" % i, [128, 8, GE], BF16) for i in range(2)]
    ga = A.alloc("ga", [128, GE], BF16)
    actT = A.alloc("actT", [128, NE, 128], BF16)
    ss_f = A.alloc("ss_f", [128, 2])
    gfin = vraw[0]
    def prep_group(g):
        for e_ in range(NE):
            sl = (g * NE + e_) % 2
            r0 = (g * NE + e_) * 128
            S.dma(uraw[sl], peer_u[r0:r0 + 128, :], writes=["uraw%d" % sl])
            S.dma(vraw[sl], peer_v[r0:r0 + 128, :], writes=["vraw%d" % sl])
            S.op("act", lambda e, sl=sl: e.copy(out=ubf[sl], in_=uraw[sl]), reads=["uraw%d" % sl], writes=["ubf0"])
            veng = "act" if e_ % 2 == 0 else "pool"
            if veng == "act":
                S.op("act", lambda e, sl=sl, e_=e_: e.copy(out=Vg[:, e_, :], in_=vraw[sl]), reads=["vraw%d" % sl], writes=["Vg"])
            else:
                S.op("pool", lambda e, sl=sl, e_=e_: e.tensor_copy(out=Vg[:, e_, :], in_=vraw[sl]), reads=["vraw%d" % sl], writes=["Vg"])
            for half in range(2):
                pt = psT(2 + half)
                for k in range(8):
                    S.op("pe", lambda e, k=k, half=half, pt=pt, sl=sl: e.transpose(out=pt[:, k * 128:(k + 1) * 128],
                                                                                  in_=ubf[sl][:, (half * 8 + k) * 128:(half * 8 + k + 1) * 128],
                                                                                  identity=ident_b),
                         reads=["ubf0", "ident_b"], writes=["ps%d" % (2 + half)])
                S.op("act", lambda e, half=half, pt=pt, e_=e_: e.copy(out=UT[:, half * 8:(half + 1) * 8, e_ * 128:(e_ + 1) * 128],
                                                                      in_=pt[:, 0:1024].rearrange("p (k t) -> p k t", k=8)),
                     reads=["ps%d" % (2 + half)], writes=["UT"])

    def stage1a(k, p, g, tt):
        ti = p * TP + tt
        b = k % 2
        c0 = g * NE
        for e_ in range(NE):
            for kc in range(16):
                S.op("pe", lambda e, kc=kc, e_=e_: e.matmul(ps[b][:, e_ * 128:(e_ + 1) * 128], UT[:, kc, e_ * 128:(e_ + 1) * 128],
                                                             h2T[:, kc, ti * 128:(ti + 1) * 128], start=(kc == 0), stop=(kc == 15)),
                     reads=["h2T", "UT"], writes=["ps%d" % b])

        def sg_op(h):
            q = h % NB
            S.op("dve", lambda e, h=h, q=q: e.tensor_tensor(
                out=Sg[q], in0=sc4[:, tt, h, 0, c0:c0 + NE].unsqueeze(2).to_broadcast([128, NE, 128]),
                in1=sc4[:, tt, h, 1, :].unsqueeze(1).to_broadcast([128, NE, 128]), op=ALU.add),
                reads=["sc4"], writes=["Sg%d" % q])
        sg_op(0)
        for h in range(8):
            q = h % NB
            if h + 1 < 8:
                sg_op(h + 1)
            S.op("act", lambda e, h=h, q=q: e.activation(out=Eg[q], in_=Sg[q], func=AF.Exp, bias=nb[:, ti, h:h + 1]),
                 reads=["Sg%d" % q, "nb"], writes=["Eg%d" % q])
            S.op("dve", lambda e, h=h, q=q: e.scalar_tensor_tensor(
                out=Tm[b][:, h, :].rearrange("p (a n) -> p a n", a=NE), in0=Sg[q], scalar=thr[:, ti, h:h + 1], in1=Eg[q],
                op0=ALU.is_ge, op1=ALU.mult), reads=["Sg%d" % q, "Eg%d" % q, "thr"], writes=["Tm%d" % b])

    def stage1b(k):
        b = k % 2
        for e_ in range(NE):
            for h in range(8):
                S.op("pe", lambda e, e_=e_, h=h: e.matmul(ps[2 + b][:, e_ * 128:(e_ + 1) * 128], Tm[b][:, h, e_ * 128:(e_ + 1) * 128],
                                                           ident_b, start=(h == 0), stop=(h == 7)),
                     reads=["Tm%d" % b, "ident_b"], writes=["ps%d" % (2 + b)])
        S.op("act", lambda e: e.activation(out=ga, in_=ps[b][:, 0:GE], func=AF.Gelu), reads=["ps%d" % b], writes=["ga"])
        S.op("dve", lambda e: e.tensor_tensor(out=actT.rearrange("p a n -> p (a n)"), in0=ga, in1=ps[2 + b][:, 0:GE], op=ALU.mult),
             reads=["ga", "ps%d" % (2 + b)], writes=["actT"])

    def stage2b(k, g, tt):
        for nq in range(4):
            for e_ in range(NE):
                S.op("pe", lambda e, e_=e_, nq=nq: e.matmul(psOb[nq], actT[:, e_, :], Vg[:, e_, nq * 512:(nq + 1) * 512],
                                                            start=(e_ == 0), stop=(e_ == NE - 1)),
                     reads=["actT", "Vg"], writes=["psO"])
        if g == 0:
            S.op("dve", lambda e: e.tensor_copy(out=acc[:, tt, :], in_=psO[:, :]), reads=["psO"], writes=["acc%d" % tt])
        else:
            S.op("dve", lambda e: e.tensor_tensor(out=acc[:, tt, :], in0=acc[:, tt, :], in1=psO[:, :], op=ALU.add),
                 reads=["psO", "acc%d" % tt], writes=["acc%d" % tt])

    NG = NEXP // GE
    kk = 0
    for p in range(NPASS):
        for tt in range(TP):
            ti = p * TP + tt
            S.dma(sc4[:, tt].rearrange("p h s n -> p (h s n)"), sc_d[ti * 128:(ti + 1) * 128, :], reads=["sc_d%d" % ti], writes=["sc4"],
                  semkey="sc4_%d" % tt)
        iters = [(g, tt) for g in range(NG) for tt in range(TP)]
        prep_group(0)
        stage1a(kk, p, 0, 0)
        stage1b(kk)
        for n_, (g, tt) in enumerate(iters):
            nxt = iters[n_ + 1] if n_ + 1 < len(iters) else None
            if nxt is not None and nxt[0] == g:
                stage1a(kk + 1, p, nxt[0], nxt[1])
                stage2b(kk, g, tt)
                stage1b(kk + 1)
            else:
                stage2b(kk, g, tt)
                if nxt is not None:
                    prep_group(nxt[0])
                    stage1a(kk + 1, p, nxt[0], nxt[1])
                    stage1b(kk + 1)
            kk += 1
        S.barrier()
        S.dma(gfin, g_final.partition_broadcast(128), writes=["vraw0"])
        for tt in range(TP):
            ti = p * TP + tt
            a_t = acc[:, tt, :]
            xb_ = uraw[tt % 2]
            S.dma(xb_, x2_d[ti * 128:(ti + 1) * 128, :], reads=["x2_d%d" % ti], writes=["uraw%d" % (tt % 2)])
            S.op("dve", lambda e, a_t=a_t: e.tensor_tensor(out=a_t, in0=a_t, in1=gtf, op=ALU.mult),
                 reads=["acc%d" % tt, "gtf"], writes=["acc%d" % tt])
            S.op("pool", lambda e, a_t=a_t, xb_=xb_: e.tensor_tensor(out=a_t, in0=a_t, in1=xb_, op=ALU.add),
                 reads=["acc%d" % tt, "uraw%d" % (tt % 2)], writes=["acc%d" % tt])
            S.op("act", lambda e, a_t=a_t: e.activation(out=ubf[0], in_=a_t, func=AF.Square, accum_out=ss_f[:, 0:1]),
                 reads=["acc%d" % tt], writes=["ubf0", "ss"])
            rstd_from(ss_f[:, 0:1], ss_f[:, 0:1], D, ["ss"], ["ss"])
            S.op("dve", lambda e, a_t=a_t: e.scalar_tensor_tensor(out=a_t, in0=a_t, scalar=ss_f[:, 0:1], in1=gfin, op0=ALU.mult,
                                                                   op1=ALU.mult), reads=["acc%d" % tt, "ss", "vraw0"], writes=["acc%d" % tt])
            S.dma(y_out[ti * 128:(ti + 1) * 128, :], a_t, reads=["acc%d" % tt], semkey="yout%d" % tt)
        S.barrier()
    return fin()


_CACHE = {}


def _host_inputs(x, c, w):
    ident = np.eye(128, dtype=np.float32)
    inv = 1.0 / (10000.0 ** (np.arange(0, 64, 2, dtype=np.float32) / 64.0))
    maps = []
    for core in range(8):
        b, s = core // 2, core % 2
        pos_own = np.arange(s * NT, (s + 1) * NT, dtype=np.float32)
        pos_prev = np.arange(0, NT, dtype=np.float32)
        tabs = []
        for pos in (pos_prev, pos_own):
            ang = inv[:, None] * pos[None, :]
            cs, sn = np.cos(ang).astype(np.float32), np.sin(ang).astype(np.float32)
            tabs.append(np.concatenate([cs, cs], 0))
            tabs.append(np.concatenate([-sn, sn], 0))
        consts = np.zeros((128, 2), np.float32)
        consts[:, 0] = float(s)
        consts[:, 1] = 0.0 if s == 1 else -30000.0
        m = dict(w)
        m["x_own"] = np.ascontiguousarray(x[b, s * NT:(s + 1) * NT])
        m["x_prev"] = np.ascontiguousarray(x[b, 0:NT])
        m["c_vec"] = np.ascontiguousarray(c[b].reshape(16, 128).T)
        m["consts"] = consts
        m["rope"] = np.stack(tabs).astype(np.float32)
        m["ident"] = ident
        maps.append(m)
    return maps


def kernel(x, c, w_ada, b_ada, g_norm_mix, w_in, conv_w, g_q_lat, w_uq, g_kv_lat, w_ukv, g_out_conv, g_out_attn,
           w_out, g_norm_ffn, peer_w_q, peer_sub_keys, peer_u, peer_v, g_final, _debug=(), _stop=None):
    f = lambda a: np.ascontiguousarray(np.asarray(a, dtype=np.float32))
    x, c = f(x), f(c)
    w = {
        "w_ada": f(w_ada)[0], "b_ada": f(b_ada)[0], "g_norm_mix": f(g_norm_mix)[0], "w_in": f(w_in)[0],
        "conv_w": f(conv_w)[0], "g_q_lat": f(g_q_lat)[0], "w_uq": f(w_uq)[0], "g_kv_lat": f(g_kv_lat)[0],
        "w_ukv": f(w_ukv)[0], "g_out_conv": f(g_out_conv)[0], "g_out_attn": f(g_out_attn)[0], "w_out": f(w_out)[0],
        "g_norm_ffn": f(g_norm_ffn)[0], "peer_w_q": f(peer_w_q)[0],
        "peer_keys": f(peer_sub_keys)[0].reshape(16 * 128, 128), "peer_u": f(peer_u)[0], "peer_v": f(peer_v)[0],
        "g_final": f(g_final),
    }
    key = (tuple(_debug), _stop)
    if key not in _CACHE:
        _CACHE[key] = build(debug=_debug, stop=_stop)
    nc, dbg_outs = _CACHE[key]
    maps = _host_inputs(x, c, w)
    res = run_bass_kernel_spmd(nc, maps, core_ids=list(range(8)))
    out = np.empty((4, SEQ, D), np.float32)
    for core in range(8):
        b, s = core // 2, core % 2
        out[b, s * NT:(s + 1) * NT] = res.results[core]["y_out"]
    if _debug:
        return out, [{k: r[k] for k in dbg_outs} for r in res.results]
    return out
```
